# Optimizing a Trainium2 kernel written in Bass

```python
import jax, jax.numpy as jnp
from jax import lax
import numpy as np

D_MODEL = 1024
BATCH = 2
SEQ = 16384
DEPTH = 1
DEC_BATCH = 8
DEC_SEQ = 16
PAST_LEN = 4096

CHUNK = 64
Q_BLOCK = 128
H_A = 4
K_A = 128
V_A = 128
D_A = H_A * K_A
D_AV = H_A * V_A
H_B = 8
D_HB = 64
D_B = H_B * D_HB
D_FF = 2816
N_IN = 2 * D_A + 2 * D_AV + 3 * D_B + 2 * D_MODEL
_SPLITS = (D_A, 2 * D_A, 2 * D_A + D_AV, 2 * D_A + 2 * D_AV,
           2 * D_A + 2 * D_AV + D_B, 2 * D_A + 2 * D_AV + 2 * D_B,
           2 * D_A + 2 * D_AV + 3 * D_B, 2 * D_A + 2 * D_AV + 3 * D_B + D_MODEL)
ALPHA = (2 * DEPTH) ** 0.25
BETA_INIT = (8 * DEPTH) ** -0.25
SB_SCALE = D_HB ** -0.5
LN_EPS = 1e-5
RMS_EPS = 1e-6

kernel_name = 'hgrn2_stickbreaking_macaron_deepnorm_step'


def _layer_norm(x, g, b):
    xf = x.astype(jnp.float32)
    mu = jnp.mean(xf, axis=-1, keepdims=True)
    var = jnp.mean(jnp.square(xf - mu), axis=-1, keepdims=True)
    y = (xf - mu) * lax.rsqrt(var + LN_EPS) * g.astype(jnp.float32) + b.astype(jnp.float32)
    return y.astype(x.dtype)


def _swiglu(x, wg, wu, wd):
    return (jax.nn.silu(x @ wg) * (x @ wu)) @ wd


def _hgrn2_scan(q, k, v, log_f, s0):
    bsz, t_len = q.shape[:2]
    c_len = CHUNK if t_len >= CHUNK else t_len
    n_chunks = t_len // c_len

    def to_chunks(a):
        a = a.astype(jnp.float32).reshape(bsz, n_chunks, c_len, a.shape[2], a.shape[3])
        return jnp.moveaxis(a, 1, 0)

    qc, kc, vc, fc = to_chunks(q), to_chunks(k), to_chunks(v), to_chunks(log_f)
    bc = lax.cumsum(fc, axis=2)
    incl = jnp.tril(jnp.ones((c_len, c_len), dtype=bool))[None, :, :, None, None]

    def step(state, inp):
        qt, kt, vt, bt = inp
        o_inter = jnp.einsum('bthk,bhkv->bthv', qt * jnp.exp(bt), state)
        diff = bt[:, :, None] - bt[:, None, :]
        decay = jnp.where(incl, jnp.exp(jnp.where(incl, diff, 0.0)), 0.0)
        scores = jnp.einsum('bthk,btshk,bshk->bhts', qt, decay, kt)
        o_intra = jnp.einsum('bhts,bshv->bthv', scores, vt)
        b_last = bt[:, -1]
        k_dec = kt * jnp.exp(b_last[:, None] - bt)
        new_state = jnp.exp(b_last)[..., None] * state + jnp.einsum('bshk,bshv->bhkv', k_dec, vt)
        return new_state, o_inter + o_intra

    s_fin, o = lax.scan(step, s0.astype(jnp.float32), (qc, kc, vc, bc))
    o = jnp.moveaxis(o, 0, 1).reshape(bsz, t_len, H_A, V_A)
    return o, s_fin


def _sb_block(q, k, v, q_pos, k_pos):
    z = jnp.einsum('bthd,bshd->bhts', q, k).astype(jnp.float32) * SB_SCALE
    causal = k_pos[None, :] < q_pos[:, None]
    log_surv = jnp.where(causal, jax.nn.log_sigmoid(-z), 0.0)
    after = lax.cumsum(log_surv, axis=3, reverse=True) - log_surv
    w = jnp.where(causal, jnp.exp(jax.nn.log_sigmoid(z) + after), 0.0)
    return jnp.einsum('bhts,bshd->bthd', w, v.astype(jnp.float32))


def _stick_breaking(q, k, v, q_start):
    bsz, t_len = q.shape[:2]
    k_pos = jnp.arange(k.shape[1], dtype=jnp.int32)
    if t_len <= Q_BLOCK:
        return _sb_block(q, k, v, q_start + jnp.arange(t_len, dtype=jnp.int32), k_pos)
    n_blk = t_len // Q_BLOCK
    q_blocks = jnp.moveaxis(q.reshape(bsz, n_blk, Q_BLOCK, H_B, D_HB), 1, 0)

    def one_block(args):
        blk, q_blk = args
        q_pos = q_start + blk * Q_BLOCK + jnp.arange(Q_BLOCK, dtype=jnp.int32)
        return _sb_block(q_blk, k, v, q_pos, k_pos)

    o = lax.map(one_block, (jnp.arange(n_blk, dtype=jnp.int32), q_blocks))
    return jnp.moveaxis(o, 0, 1).reshape(bsz, t_len, H_B, D_HB)


def _token_mixer(x, s0, past_k, past_v, lb, w_in, hgrn_norm_g, w_branch_a, w_branch_b, w_out):
    bsz, t_len, _ = x.shape
    q_a, f_a, i_a, g_a, q_b, k_b, v_b, gate_a, gate_b = jnp.split(x @ w_in, _SPLITS, axis=-1)

    lb_h = lb.reshape(H_A, K_A)
    z_f = f_a.astype(jnp.float32).reshape(bsz, t_len, H_A, K_A)
    log_f = jnp.logaddexp(jnp.log(lb_h), jnp.log1p(-lb_h) + jax.nn.log_sigmoid(z_f))
    hk = -jnp.expm1(log_f)
    hq = jax.nn.silu(q_a).reshape(bsz, t_len, H_A, K_A)
    hv = i_a.reshape(bsz, t_len, H_A, V_A)
    o_a, s_new = _hgrn2_scan(hq, hk, hv, log_f, s0)
    o_a = o_a * lax.rsqrt(jnp.mean(jnp.square(o_a), axis=-1, keepdims=True) + RMS_EPS)
    o_a = o_a * hgrn_norm_g.astype(jnp.float32).reshape(H_A, V_A)
    h_a = o_a.reshape(bsz, t_len, D_AV).astype(x.dtype) * jax.nn.silu(g_a)

    qb = q_b.reshape(bsz, t_len, H_B, D_HB)
    kb = k_b.reshape(bsz, t_len, H_B, D_HB)
    vb = v_b.reshape(bsz, t_len, H_B, D_HB)
    if past_k is None:
        k_all, v_all, q_start = kb, vb, 0
    else:
        k_all = jnp.concatenate([past_k.astype(kb.dtype), kb], axis=1)
        v_all = jnp.concatenate([past_v.astype(vb.dtype), vb], axis=1)
        q_start = past_k.shape[1]
    h_b = _stick_breaking(qb, k_all, v_all, q_start).reshape(bsz, t_len, D_B).astype(x.dtype)

    merged = jax.nn.sigmoid(gate_a) * (h_a @ w_branch_a) + jax.nn.sigmoid(gate_b) * (h_b @ w_branch_b)
    return merged @ w_out, kb, vb, s_new


def _trunk(x, s0, past_k, past_v, weights):
    (ffn1_wg, ffn1_wu, ffn1_wd, ln1_g, ln1_b, w_in, lb_logits, hgrn_norm_g,
     w_branch_a, w_branch_b, w_out, ln2_g, ln2_b,
     ffn2_wg, ffn2_wu, ffn2_wd, ln3_g, ln3_b) = weights
    lower_bounds = lax.cumsum(jax.nn.softmax(lb_logits.astype(jnp.float32), axis=0), axis=0)
    ks, vs, ss = [], [], []
    for l in range(DEPTH):
        x = _layer_norm(ALPHA * x + 0.5 * _swiglu(x, ffn1_wg[l], ffn1_wu[l], ffn1_wd[l]), ln1_g[l], ln1_b[l])
        mix, k_new, v_new, s_new = _token_mixer(
            x, s0[l],
            None if past_k is None else past_k[l],
            None if past_v is None else past_v[l],
            lower_bounds[l], w_in[l], hgrn_norm_g[l], w_branch_a[l], w_branch_b[l], w_out[l])
        x = _layer_norm(ALPHA * x + mix, ln2_g[l], ln2_b[l])
        x = _layer_norm(ALPHA * x + 0.5 * _swiglu(x, ffn2_wg[l], ffn2_wu[l], ffn2_wd[l]), ln3_g[l], ln3_b[l])
        ks.append(k_new)
        vs.append(v_new)
        ss.append(s_new)
    return x, jnp.stack(ks), jnp.stack(vs), jnp.stack(ss)


def setup_inputs(seed: int = 0) -> dict:
    key = jax.random.key(seed)
    ks = jax.random.split(key, 24)

    def nrm(k, shape, scale):
        return scale * jax.random.normal(k, shape, jnp.float32)

    d_in = D_MODEL ** -0.5
    d_ff = D_FF ** -0.5
    return {
        'x_prompt': nrm(ks[0], (BATCH, SEQ, D_MODEL), 1.0),
        'x_sample': nrm(ks[1], (DEC_BATCH, DEC_SEQ, D_MODEL), 1.0),
        'cache_sb_k': nrm(ks[2], (DEPTH, DEC_BATCH, PAST_LEN, H_B, D_HB), 1.0),
        'cache_sb_v': nrm(ks[3], (DEPTH, DEC_BATCH, PAST_LEN, H_B, D_HB), 1.0),
        'state_hgrn': nrm(ks[4], (DEPTH, DEC_BATCH, H_A, K_A, V_A), 0.5),
        'ffn1_wg': nrm(ks[5], (DEPTH, D_MODEL, D_FF), d_in),
        'ffn1_wu': nrm(ks[6], (DEPTH, D_MODEL, D_FF), d_in * BETA_INIT),
        'ffn1_wd': nrm(ks[7], (DEPTH, D_FF, D_MODEL), d_ff * BETA_INIT),
        'ln1_g': 1.0 + nrm(ks[8], (DEPTH, D_MODEL), 0.02),
        'ln1_b': nrm(ks[9], (DEPTH, D_MODEL), 0.02),
        'w_in': nrm(ks[10], (DEPTH, D_MODEL, N_IN), d_in),
        'lb_logits': nrm(ks[11], (DEPTH + 1, D_A), 0.5),
        'hgrn_norm_g': 1.0 + nrm(ks[12], (DEPTH, D_AV), 0.02),
        'w_branch_a': nrm(ks[13], (DEPTH, D_AV, D_MODEL), (D_AV ** -0.5) * BETA_INIT),
        'w_branch_b': nrm(ks[14], (DEPTH, D_B, D_MODEL), (D_B ** -0.5) * BETA_INIT),
        'w_out': nrm(ks[15], (DEPTH, D_MODEL, D_MODEL), d_in * BETA_INIT),
        'ln2_g': 1.0 + nrm(ks[16], (DEPTH, D_MODEL), 0.02),
        'ln2_b': nrm(ks[17], (DEPTH, D_MODEL), 0.02),
        'ffn2_wg': nrm(ks[18], (DEPTH, D_MODEL, D_FF), d_in),
        'ffn2_wu': nrm(ks[19], (DEPTH, D_MODEL, D_FF), d_in * BETA_INIT),
        'ffn2_wd': nrm(ks[20], (DEPTH, D_FF, D_MODEL), d_ff * BETA_INIT),
        'ln3_g': 1.0 + nrm(ks[21], (DEPTH, D_MODEL), 0.02),
        'ln3_b': nrm(ks[22], (DEPTH, D_MODEL), 0.02),
    }


def reference(x_prompt, x_sample, cache_sb_k, cache_sb_v, state_hgrn,
              ffn1_wg, ffn1_wu, ffn1_wd, ln1_g, ln1_b, w_in, lb_logits, hgrn_norm_g,
              w_branch_a, w_branch_b, w_out, ln2_g, ln2_b,
              ffn2_wg, ffn2_wu, ffn2_wd, ln3_g, ln3_b):
    weights = (ffn1_wg, ffn1_wu, ffn1_wd, ln1_g, ln1_b, w_in, lb_logits, hgrn_norm_g,
               w_branch_a, w_branch_b, w_out, ln2_g, ln2_b,
               ffn2_wg, ffn2_wu, ffn2_wd, ln3_g, ln3_b)
    s0_prompt = jnp.zeros((DEPTH, x_prompt.shape[0], H_A, K_A, V_A), jnp.float32)
    y_prompt, k_prompt, v_prompt, s_prompt = _trunk(x_prompt, s0_prompt, None, None, weights)
    y_sample, k_sample, v_sample, s_sample = _trunk(x_sample, state_hgrn, cache_sb_k, cache_sb_v, weights)
    return (y_prompt, y_sample, k_prompt, v_prompt, s_prompt, k_sample, v_sample, s_sample)
```

```python
import numpy as np
import os
from contextlib import ExitStack
KNOB = os.environ.get('KNOB', '')
import concourse.bass as bass
import concourse.mybir as mybir
from concourse.bass_utils import run_bass_kernel_spmd

F32 = mybir.dt.float32
BF16 = mybir.dt.bfloat16
AF = mybir.ActivationFunctionType
ALU = mybir.AluOpType

D = 1024
KC = 8
FF = 2816
FC = 22
HA = 4
HB = 8
DH = 64
NIN = 5632
ALPHA = 2.0 ** 0.25
LN_EPS = 1e-5
RMS_EPS = 1e-6
TB = 512
DEC = 16


class Buf:
    __slots__ = ("name", "w", "r", "sem", "accum")

    def __init__(self, name, accum=False):
        self.name = name
        self.w = {}
        self.r = {}
        self.sem = None
        self.accum = accum


class T:
    __slots__ = ("buf", "ap")

    def __init__(self, buf, ap):
        self.buf = buf
        self.ap = ap

    def __getitem__(self, idx):
        return T(self.buf, self.ap[idx])

    def sub(self, idx, name):
        return T(Buf(name), self.ap[idx])

    def re(self, s, **kw):
        return T(self.buf, self.ap.rearrange(s, **kw))


class DSem:
    __slots__ = ("sem", "total", "id")

    def __init__(self, sem, i):
        self.sem = sem
        self.total = 0
        self.id = i


class Prog:
    ENGS = ("pe", "act", "dve", "pool", "sp")

    def __init__(self, nc, es):
        self.nc = nc
        self.es = es
        self.eng = {"pe": nc.tensor, "act": nc.scalar, "dve": nc.vector, "pool": nc.gpsimd, "sp": nc.sync}
        self.sem = {e: es.enter_context(nc.semaphore("sem_" + e)) for e in self.ENGS}
        self.cnt = {e: 0 for e in self.ENGS}
        self.seen = {e: {} for e in self.ENGS}
        self.dsems = []
        self.nwait = 0
        self.nops = 0

    def sb(self, es, name, shape, dtype):
        self.nalloc = getattr(self, "nalloc", 0) + 1
        name = "%s_%d" % (name, self.nalloc)
        nb = int(np.prod(shape[1:])) * (2 if dtype == BF16 else 4)
        self.sb_cur = getattr(self, "sb_cur", 0) + nb
        self.sb_peak = max(getattr(self, "sb_peak", 0), self.sb_cur)
        es.callback(self._free, nb)
        t = es.enter_context(self.nc.sbuf_tensor(name, list(shape), dtype))
        return T(Buf(name), t[:] if hasattr(t, "__getitem__") else t)

    def _free(self, nb):
        self.sb_cur -= nb

    def ps(self, es, name, shape, dtype=F32):
        t = es.enter_context(self.nc.psum_tensor(name, list(shape), dtype))
        return T(Buf(name), t[:])

    def dram(self, name, shape, dtype, kind="Internal", accum=False):
        t = self.nc.dram_tensor(name, list(shape), dtype, kind=kind)
        return T(Buf(name, accum=accum), t.ap())

    def dsem(self, buf, e):
        kind = "sw" if e == "pool" else "hw"
        if buf.sem is None:
            buf.sem = {}
        if kind not in buf.sem:
            s = self.es.enter_context(self.nc.semaphore("dsem%d" % len(self.dsems)))
            buf.sem[kind] = DSem(s, len(self.dsems))
            self.dsems.append(buf.sem[kind])
        return buf.sem[kind]

    def _wait(self, e, key, val):
        if key[0] == "c":
            sem = self.sem[key[1]]
            sid = key[1]
        else:
            sem = key[1].sem
            sid = key[1].id
            val = key[1].total
        if val <= 0:
            return
        if self.seen[e].get(sid, 0) >= val:
            return
        self.seen[e][sid] = val
        self.eng[e].wait_ge(sem, val)
        self.nwait += 1

    def _deps(self, e, reads, writes, is_dma):
        for b in reads:
            for key, val in list(b.w.items()):
                self._wait(e, key, val)
        for b in writes:
            if b.accum:
                continue
            for key, val in list(b.w.items()):
                if key == ("c", "pe") and e == "pe" and not is_dma:
                    continue
                self._wait(e, key, val)
            for key, val in list(b.r.items()):
                self._wait(e, key, val)

    def op(self, e, fn, reads, writes, sig=True):
        rb = [x.buf if isinstance(x, T) else x for x in reads]
        wb = [x.buf if isinstance(x, T) else x for x in writes]
        self._deps(e, rb, wb, False)
        ins = fn()
        if sig:
            self.cnt[e] += 1
            ins.then_inc(self.sem[e], 1)
            val = self.cnt[e]
        else:
            val = self.cnt[e] + 1
        key = ("c", e)
        for b in rb:
            b.r[key] = val
        for b in wb:
            if not b.accum:
                b.w = {key: val}
                b.r = {}
            else:
                b.w[key] = val
        self.nops += 1
        return ins

    def dma(self, e, out, in_, owner=None, **kw):
        ob, ib = out.buf, in_.buf
        if owner is None:
            owner = ib if ob.accum else ob
        ds = self.dsem(owner, e)
        self._deps(e, [ib], [ob], True)
        ins = self.eng[e].dma_start(out=out.ap, in_=in_.ap, **kw)
        ds.total += 16
        ins.then_inc(ds.sem, 16)
        key = ("d", ds)
        ib.r[key] = ds.total
        if ob.accum:
            ob.w[key] = ds.total
        else:
            ob.w = {key: ds.total}
            ob.r = {}
        self.nops += 1
        return ins

    def barrier(self):
        for e in self.ENGS:
            for o in self.ENGS:
                if o != e:
                    self._wait(e, ("c", o), self.cnt[o])
            for ds in self.dsems:
                self._wait(e, ("d", ds), ds.total)

    def mm(self, out, lhsT, rhs, start=True, stop=True, sig=True):
        return self.op("pe", lambda: self.nc.tensor.matmul(out.ap, lhsT.ap, rhs.ap, start=start, stop=stop),
                       [lhsT, rhs], [out], sig=sig)

    def tr(self, out, in_, ident, sig=True):
        return self.op("pe", lambda: self.nc.tensor.transpose(out.ap, in_.ap, ident.ap), [in_, ident], [out], sig=sig)

    def act(self, out, in_, func, bias=None, scale=None, extra_reads=()):
        kw = {}
        rd = [in_] + list(extra_reads)
        if bias is not None:
            if isinstance(bias, T):
                kw["bias"] = bias.ap
                rd.append(bias)
            else:
                kw["bias"] = bias
        if scale is not None:
            if isinstance(scale, T):
                kw["scale"] = scale.ap
                rd.append(scale)
            else:
                kw["scale"] = scale
        return self.op("act", lambda: self.nc.scalar.activation(out.ap, in_.ap, func, **kw), rd, [out])

    def _e(self, e):
        return self.eng[e]

    def tt(self, e, out, in0, in1, op):
        return self.op(e, lambda: self._e(e).tensor_tensor(out.ap, in0.ap, in1.ap, op), [in0, in1], [out])

    def ts(self, e, out, in0, s1, op0, s2=None, op1=None):
        rd = [in0]
        a1 = s1
        a2 = s2
        if isinstance(s1, T):
            rd.append(s1)
            a1 = s1.ap
        if isinstance(s2, T):
            rd.append(s2)
            a2 = s2.ap
        if op1 is None:
            return self.op(e, lambda: self._e(e).tensor_scalar(out.ap, in0.ap, a1, a2, op0), rd, [out])
        return self.op(e, lambda: self._e(e).tensor_scalar(out.ap, in0.ap, a1, a2, op0, op1), rd, [out])

    def stt(self, out, in0, scalar, in1, op0, op1):
        rd = [in0, in1]
        a = scalar
        if isinstance(scalar, T):
            rd.append(scalar)
            a = scalar.ap
        return self.op("dve", lambda: self.nc.vector.scalar_tensor_tensor(out.ap, in0.ap, a, in1.ap, op0, op1), rd, [out])

    def cp(self, e, out, in_):
        if e == "act":
            return self.op(e, lambda: self.nc.scalar.copy(out.ap, in_.ap), [in_], [out])
        return self.op(e, lambda: self._e(e).tensor_copy(out.ap, in_.ap), [in_], [out])

    def memset(self, e, out, val):
        return self.op(e, lambda: self._e(e).memset(out.ap, val), [], [out])


W_SHAPES = {
    "ffn1_wg": (D, FF), "ffn1_wu": (D, FF), "ffn1_wd": (FF, D), "ln1_g": (1, D), "ln1_b": (1, D),
    "w_in": (D, NIN), "lb_logits": (2, 512), "hgrn_norm_g": (1, 512),
    "w_branch_a": (512, D), "w_branch_b": (512, D), "w_out": (D, D), "ln2_g": (1, D), "ln2_b": (1, D),
    "ffn2_wg": (D, FF), "ffn2_wu": (D, FF), "ffn2_wd": (FF, D), "ln3_g": (1, D), "ln3_b": (1, D),
}
ST_BLOCKS = {"qa": (0, 4), "fa": (512, 4), "ga": (1536, 4), "qb": (2048, 4), "gta": (3584, 8), "gtb": (4608, 8)}
MV_BLOCKS = {"ia": 1024, "kb": 2560, "vb": 3072}


def build(SEQ, PAST, stop_after=None):
    NG = SEQ // TB
    NSG = SEQ // 2048
    NOWN = SEQ // 4
    NKB = SEQ // 128
    NPB = PAST // 128
    SKS = PAST + 128
    nc = bass.Bass("TRN2", target_bir_lowering=False)
    es = ExitStack()
    P = Prog(nc, es)
    IN = lambda n, s: P.dram(n, s, F32, "ExternalInput")
    OUT = lambda n, s: P.dram(n, s, F32, "ExternalOutput", accum=True)
    xb = IN("xb", [SEQ, D])
    xs = IN("xs", [DEC, D])
    ck = IN("ck", [PAST, 512])
    cv = IN("cv", [PAST, 512])
    st0 = IN("st0", [HA, 128, 128])
    Wd = {n: IN(n, list(s)) for n, s in W_SHAPES.items()}
    c_ident = IN("c_ident", [128, 128])
    c_mask64 = IN("c_mask64", [128, 512])
    c_mask2 = IN("c_mask2", [128, 512])
    c_uincl = IN("c_uincl", [128, 128])
    c_sbmask = IN("c_sbmask", [16, 128, 512])
    c_smask = IN("c_smask", [128, DEC])
    c_flag = IN("c_flag", [128, 4])
    y_own = OUT("y_own", [NOWN, D])
    k_all = OUT("k_all", [SEQ, 512])
    v_all = OUT("v_all", [SEQ, 512])
    s_fin = OUT("s_fin", [HA, 128, 128])
    ys = OUT("ys", [DEC, D])
    ks = OUT("ks", [DEC, 512])
    vs = OUT("vs", [DEC, 512])
    ss = OUT("ss", [HA, 128, 128])
    dbg = OUT("dbg", [SEQ, D]) if stop_after else None

    SCR = lambda n, s, dt=BF16: P.dram(n, s, dt, "Internal", accum=True)
    ws = {}
    for l in ("ffn1", "ffn2"):
        ws[l + "_wg"] = SCR(l + "_wg_s", [FC, 128, KC * 128])
        ws[l + "_wu"] = SCR(l + "_wu_s", [FC, 128, KC * 128])
        ws[l + "_wd"] = SCR(l + "_wd_s", [FC, 128, D])
    for n, (c0, noc) in ST_BLOCKS.items():
        ws[n] = SCR("win_" + n, [noc, 128, KC * 128])
    for n, c0 in MV_BLOCKS.items():
        ws[n] = SCR("win_" + n, [128, KC, 512])
    ws["wa"] = SCR("wa_s", [8, 128, 4 * 128])
    ws["wb"] = SCR("wb_s", [8, 64, 8 * 128])
    ws["wo"] = SCR("wo_s", [2, 128, KC, 512])
    KT_scr = SCR("KT_scr", [HB, DH, SEQ])
    V_scr = SCR("V_scr", [HB, 128, NKB, DH])
    QT_scr = SCR("QT_scr", [HB, DH, NOWN])
    GA_scr = SCR("GA_scr", [4, 128, NOWN])
    GT_scr = SCR("GT_scr", [16, 128, NOWN])
    OT_scr = SCR("OT_scr", [HA, 128, NOWN], F32)
    X1_scr = SCR("X1_scr", [NOWN, D], F32)
    HB_scr = SCR("HB_scr", [HB, DH, NOWN])
    KTs_scr = SCR("KTs_scr", [HB, DH, SKS])
    Vs_scr = SCR("Vs_scr", [HB, 128, NPB + 1, DH])
    QTs_scr = SCR("QTs_scr", [HB, DH, 128])
    GAs_scr = SCR("GAs_scr", [4, 128, 128])
    GTs_scr = SCR("GTs_scr", [16, 128, 128])
    OTs_scr = SCR("OTs_scr", [HA, 128, 128], F32)
    X1s_scr = SCR("X1s_scr", [128, D], F32)
    HBs_scr = SCR("HBs_scr", [HB, DH, 128])

    def prep_weights():
        q = "pool"
        kpc = lambda t: t.re("p (k c) -> p k c", c=128)
        def prep_ffn(l):
            for f in range(FC):
                for nm in ("_wg", "_wu"):
                    P.dma(q, kpc(ws[l + nm][f]), Wd[l + nm][:, f * 128:(f + 1) * 128].re("(k p) c -> p k c", p=128))
                P.dma(q, ws[l + "_wd"][f], Wd[l + "_wd"][f * 128:(f + 1) * 128, :])
        prep_ffn("ffn1")
        for n, (c0, noc) in ST_BLOCKS.items():
            for oc in range(noc):
                P.dma(q, kpc(ws[n][oc]), Wd["w_in"][:, c0 + oc * 128:c0 + (oc + 1) * 128].re("(k p) c -> p k c", p=128))
        for n, c0 in MV_BLOCKS.items():
            P.dma(q, ws[n], Wd["w_in"][:, c0:c0 + 512].re("(k p) c -> p k c", p=128))
        for oc in range(8):
            P.dma(q, kpc(ws["wa"][oc]), Wd["w_branch_a"][:, oc * 128:(oc + 1) * 128].re("(k p) c -> p k c", p=128))
            P.dma(q, kpc(ws["wb"][oc]), Wd["w_branch_b"][:, oc * 128:(oc + 1) * 128].re("(h p) c -> p h c", p=64))
        for hf in range(2):
            P.dma(q, ws["wo"][hf], Wd["w_out"][:, hf * 512:(hf + 1) * 512].re("(k p) c -> p k c", p=128))
        for h in range(HB):
            for b0 in range(0, NPB, 8):
                b1 = min(NPB, b0 + 8)
                P.dma(q, Vs_scr[h, :, b0:b1, :],
                      cv[b0 * 128:b1 * 128, h * DH:(h + 1) * DH].re("(b p) d -> p b d", p=128))
        prep_ffn("ffn2")

    prep_weights()

    psum = es.enter_context(nc.psum_tensor("psum", [128, 8 * 512], F32))
    PS = [T(Buf("ps%d" % i), psum[:, i * 512:(i + 1) * 512]) for i in range(8)]
    ident = P.sb(es, "ident", [128, 128], F32)
    P.dma("sp", ident, c_ident)
    flag = P.sb(es, "flag", [128, 4], F32)
    P.dma("sp", flag, c_flag)
    NRING = 6
    ring = [P.sb(es, "ring%d" % i, [128, 1024], BF16) for i in range(NRING)]
    ring_i = [0]

    def ring_load(src, k=None, npart=128):
        slot = ring[ring_i[0] % NRING]
        ring_i[0] += 1
        n = src.ap.shape[-1]
        P.dma("sp", slot[0:npart, 0:n], src)
        v = slot[0:npart, 0:n]
        return v if k is None else v.re("p (k c) -> p k c", k=k)

    bank_i = [0]

    def bank(lo=0, hi=4):
        b = PS[lo + bank_i[0] % (hi - lo)]
        bank_i[0] += 1
        return b

    ev_i = [0]

    def evac(out, in_):
        e = "act" if ev_i[0] % 2 == 0 else "dve"
        ev_i[0] += 1
        P.cp(e, out, in_)

    def transpose_to_fm(src_tm, dstT, ntc, nk=KC):
        for k in range(nk):
            b = bank()
            for c in range(ntc):
                P.tr(b[:, c * 128:(c + 1) * 128], src_tm[c][:, k * 128:(k + 1) * 128], ident, sig=(c == ntc - 1))
            evac(dstT[:, k, 0:ntc * 128], b[:, 0:ntc * 128])

    def layer_norm(o, lng, lnb, small, eps):
        st, mv, ve, rstd, mhalf = small
        P.op("dve", lambda: nc.vector.bn_stats(st.ap[:, 0:6], o.ap[:, 0:512]), [o], [st])
        P.op("dve", lambda: nc.vector.bn_stats(st.ap[:, 6:12], o.ap[:, 512:1024]), [o], [st])
        P.op("dve", lambda: nc.vector.bn_aggr(mv.ap, st.ap), [st], [mv])
        P.ts("pool", ve, mv[:, 1:2], eps, ALU.add)
        P.tt("pool", rstd, ve, mhalf, ALU.pow)
        P.ts("dve", o, o, mv[:, 0:1], ALU.subtract, rstd, ALU.mult)
        P.tt("dve", o, o, lng, ALU.mult)
        P.tt("dve", o, o, lnb, ALU.add)

    def ffn_ln(pref, x_tm, xT, hT, sg, out_tm, lng, lnb, small, ntc):
        NT = ntc * 128
        wg, wu, wd = ws[pref + "_wg"], ws[pref + "_wu"], ws[pref + "_wd"]
        for f in range(FC):
            tg = ring_load(wg[f], KC)
            tu = ring_load(wu[f], KC)
            bg = bank()
            bu = bank()
            for k in range(KC):
                P.mm(bg[:, 0:NT], tg[:, k, :], xT[:, k, 0:NT], start=(k == 0), stop=(k == KC - 1), sig=(k == KC - 1))
            for k in range(KC):
                P.mm(bu[:, 0:NT], tu[:, k, :], xT[:, k, 0:NT], start=(k == 0), stop=(k == KC - 1), sig=(k == KC - 1))
            s = sg[f % 2]
            P.act(s[:, 0:NT], bg[:, 0:NT], AF.Silu)
            P.tt("dve", hT[f][:, 0:NT], s[:, 0:NT], bu[:, 0:NT], ALU.mult)
        for p0 in range(0, ntc, 2):
            cs = list(range(p0, min(p0 + 2, ntc)))
            ab = 4 if p0 == 0 else 0
            for f in range(FC):
                td = ring_load(wd[f])
                for c in cs:
                    for hf in range(2):
                        P.mm(PS[ab + (c % 2) * 2 + hf], hT[f][:, c * 128:(c + 1) * 128], td[:, hf * 512:(hf + 1) * 512],
                             start=(f == 0), stop=(f == FC - 1))
            for c in cs:
                o = out_tm[c]
                for hf in range(2):
                    P.stt(o[:, hf * 512:(hf + 1) * 512], PS[ab + (c % 2) * 2 + hf], 0.5 / ALPHA,
                          x_tm[c][:, hf * 512:(hf + 1) * 512], ALU.mult, ALU.add)
            for c in cs:
                layer_norm(out_tm[c], lng, lnb, small, LN_EPS / (ALPHA * ALPHA))

    def bcast_load(es_, name, src, n):
        t = P.sb(es_, name, [128, n], F32)
        P.dma("sp", t, T(src.buf, src.ap[0:1, :].partition_broadcast(128)))
        return t

    def mk_small(es_):
        st = P.sb(es_, "bnst", [128, 12], F32)
        mv = P.sb(es_, "bnmv", [128, 2], F32)
        ve = P.sb(es_, "bnve", [128, 1], F32)
        rstd = P.sb(es_, "bnrstd", [128, 1], F32)
        mhalf = P.sb(es_, "mhalf", [128, 1], F32)
        P.memset("pool", mhalf, -0.5)
        return (st, mv, ve, rstd, mhalf)

    with ExitStack() as ea:
        x_tm = [P.sb(ea, "x_tm%d" % c, [128, D], F32) for c in range(4)]
        x1_tm = [P.sb(ea, "x1_tm%d" % c, [128, D], F32) for c in range(4)]
        xT_all = P.sb(ea, "xT", [128, KC, TB], BF16)
        x1T_all = P.sb(ea, "x1T", [128, KC, TB], BF16)
        hT_all = P.sb(ea, "hT", [128, FC, TB], BF16)
        hT = [hT_all.sub((slice(None), f, slice(None)), "hT%d" % f) for f in range(FC)]
        sg = [P.sb(ea, "sg%d" % i, [128, TB], F32) for i in range(2)]
        ln1g = bcast_load(ea, "ln1g", Wd["ln1_g"], D)
        ln1b = bcast_load(ea, "ln1b", Wd["ln1_b"], D)
        small = mk_small(ea)
        wia = P.sb(ea, "wia", [128, KC, 512], BF16)
        wkb = P.sb(ea, "wkb", [128, KC, 512], BF16)
        wvb = P.sb(ea, "wvb", [128, KC, 512], BF16)
        P.dma("sp", wia, ws["ia"])
        P.dma("sp", wkb, ws["kb"])
        P.dma("sp", wvb, ws["vb"])
        mask64 = P.sb(ea, "mask64", [128, 512], F32)
        mask2 = P.sb(ea, "mask2", [128, 512], F32)
        P.dma("sp", mask64, c_mask64)
        P.dma("sp", mask2, c_mask2)
        lbl = P.sb(ea, "lbl", [128, 2, 4], F32)
        P.dma("sp", lbl, Wd["lb_logits"].re("r (h p) -> p r h", p=128), allow_slow_non_contiguous=True)
        oml = P.sb(ea, "oml", [128, 4], F32)
        noml = P.sb(ea, "noml", [128, 4], F32)
        lbd = P.sb(ea, "lbd", [128, 4], F32)
        P.tt("dve", lbd, lbl[:, 0, :], lbl[:, 1, :], ALU.subtract)
        P.act(lbd, lbd, AF.Exp)
        P.ts("dve", lbd, lbd, 1.0, ALU.add)
        P.op("dve", lambda: nc.vector.reciprocal(oml.ap, lbd.ap), [lbd], [oml])
        P.ts("dve", noml, oml, -1.0, ALU.mult)
        vtm = P.sb(ea, "vtm", [128, 4, 512], BF16)
        ktm = [P.sb(ea, "ktm%d" % i, [128, 512], F32) for i in range(4)]
        vbtm = [P.sb(ea, "vbtm%d" % i, [128, 512], F32) for i in range(2)]
        kTs = [P.sb(ea, "kTs%d" % i, [128, 512], F32) for i in range(2)]
        hqT = [P.sb(ea, "hqT%d" % h, [128, TB], F32) for h in range(HA)]
        he = P.sb(ea, "he", [128, TB], F32)
        hr = P.sb(ea, "hr", [128, TB], F32)
        hlf = P.sb(ea, "hlf", [128, TB], F32)
        hb = P.sb(ea, "hb", [128, TB], F32)
        heb = P.sb(ea, "heb", [128, TB], F32)
        hebn = P.sb(ea, "hebn", [128, TB], F32)
        qt = P.sb(ea, "qt", [128, TB], BF16)
        kt = P.sb(ea, "kt", [128, TB], BF16)
        kdT = P.sb(ea, "kdT", [128, TB], F32)
        kd_tm = P.sb(ea, "kd_tm", [128, 4, 128], BF16)
        AT = P.sb(ea, "AT", [128, TB], BF16)
        Sst = [[P.sb(ea, "S%d_%d" % (h, i), [128, 128], F32) for i in range(2)] for h in range(HA)]
        Ssm = [[P.sb(ea, "Ss%d_%d" % (h, i), [128, 128], F32) for i in range(2)] for h in range(HA)]
        Sb_all = P.sb(ea, "Sb8", [128, 8, 128], BF16)
        Sb = [Sb_all.sub((slice(None), i, slice(None)), "Sb8_%d" % i) for i in range(8)]
        sb_i = [0]
        oTown = P.sb(ea, "oTown", [128, HA, 128], F32)
        x1own = P.sb(ea, "x1own", [128, D], F32)
        x1Town = P.sb(ea, "x1Town", [128, KC, TB], BF16)
        pj = [P.sb(ea, "pj%d" % i, [128, TB], BF16) for i in range(2)]
        pj_i = [0]
        for h in range(HA):
            P.memset("pool", Sst[h][0], 0.0)

        def kT_path(src_tm_list, dst_scr, col0, nblk):
            for oc in range(4):
                b = bank(4, 8)
                for i, src in enumerate(src_tm_list):
                    P.tr(b[:, i * 128:(i + 1) * 128], src[:, oc * 128:(oc + 1) * 128], ident, sig=(i == nblk - 1))
                kk = kTs[oc % 2]
                evac(kk[:, 0:nblk * 128], b[:, 0:nblk * 128])
                P.dma("pool", dst_scr[2 * oc:2 * oc + 2, :, col0:col0 + nblk * 128].re("h d s -> (h d) s"),
                      kk[:, 0:nblk * 128])

        def group_tail(gi, ntc, sample):
            NT = ntc * 128
            S = Ssm if sample else Sst
            transpose_to_fm(x1_tm, x1T_all, ntc)
            ktl = []
            for c in range(ntc):
                csl = slice(c * 128, (c + 1) * 128)
                b = bank()
                for k in range(KC):
                    P.mm(b, x1T_all[:, k, csl], wia[:, k, :], start=(k == 0), stop=(k == KC - 1), sig=(k == KC - 1))
                evac(vtm[:, c, :], b)
                b = bank()
                for k in range(KC):
                    P.mm(b, x1T_all[:, k, csl], wkb[:, k, :], start=(k == 0), stop=(k == KC - 1), sig=(k == KC - 1))
                kk = ktm[c % 4]
                ktl.append(kk)
                evac(kk, b)
                b = bank()
                for k in range(KC):
                    P.mm(b, x1T_all[:, k, csl], wvb[:, k, :], start=(k == 0), stop=(k == KC - 1), sig=(k == KC - 1))
                vv = vbtm[c % 2]
                evac(vv, b)
                if sample:
                    P.dma("pool", ks, kk[0:DEC, :])
                    P.dma("pool", vs, vv[0:DEC, :])
                    P.dma("pool", Vs_scr[:, :, NPB, :].re("h p d -> p h d"), vv.re("p (h d) -> p h d", d=DH))
                    kT_path([kk], KTs_scr, PAST, 1)
                else:
                    r0 = gi * TB + c * 128
                    P.dma("pool", k_all[r0:r0 + 128, :], kk)
                    P.dma("pool", v_all[r0:r0 + 128, :], vv)
                    P.dma("pool", V_scr[:, :, gi * 4 + c, :].re("h p d -> p h d"), vv.re("p (h d) -> p h d", d=DH))
            if not sample:
                kT_path(ktl, KT_scr, gi * TB, ntc)
            for h in range(HA):
                tq = ring_load(ws["qa"][h], KC)
                b = bank()
                for k in range(KC):
                    P.mm(b[:, 0:NT], tq[:, k, :], x1T_all[:, k, 0:NT], start=(k == 0), stop=(k == KC - 1), sig=(k == KC - 1))
                P.act(hqT[h][:, 0:NT], b[:, 0:NT], AF.Silu)
            chunks = [(0, DEC)] if sample else [(c * 64, 64) for c in range(NT // 64)]
            for h in range(HA):
                tf = ring_load(ws["fa"][h], KC)
                zf = bank()
                for k in range(KC):
                    P.mm(zf[:, 0:NT], tf[:, k, :], x1T_all[:, k, 0:NT], start=(k == 0), stop=(k == KC - 1), sig=(k == KC - 1))
                N_ = slice(0, NT)
                P.act(he[:, N_], zf[:, N_], AF.Exp)
                P.act(he[:, N_], he[:, N_], AF.Ln, bias=1.0)
                P.act(hr[:, N_], he[:, N_], AF.Exp, scale=-1.0)
                P.act(hlf[:, N_], hr[:, N_], AF.Ln, bias=1.0, scale=noml[:, h:h + 1])
                P.op("dve", lambda: nc.vector.tensor_tensor_scan(hb.ap[:, N_], mask64.ap[:, N_], hlf.ap[:, N_], 0.0,
                                                                  ALU.mult, ALU.add), [mask64, hlf], [hb])
                P.act(heb[:, N_], hb[:, N_], AF.Exp)
                P.act(hebn[:, N_], hb[:, N_], AF.Exp, scale=-1.0)
                P.tt("dve", qt[:, N_], hqT[h][:, N_], heb[:, N_], ALU.mult)
                P.stt(kt[:, N_], hr[:, N_], oml[:, h:h + 1], hebn[:, N_], ALU.mult, ALU.mult)
                if sample:
                    P.memset("pool", kdT[:, DEC:128], 0.0)
                    P.ts("dve", kdT[:, 0:DEC], kt[:, 0:DEC], heb[:, DEC - 1:DEC], ALU.mult)
                else:
                    nch = NT // 64
                    lastb = T(heb.buf, heb.ap[:, N_].rearrange("p (c t) -> p c t", t=64)[:, :, 63:64].to_broadcast([128, nch, 64]))
                    P.tt("dve", kdT[:, N_].re("p (c t) -> p c t", t=64), kt[:, N_].re("p (c t) -> p c t", t=64), lastb, ALU.mult)
                b = bank()
                for c in range(ntc):
                    P.tr(b[:, c * 128:(c + 1) * 128], kdT[:, c * 128:(c + 1) * 128], ident, sig=(c == ntc - 1))
                evac(kd_tm[:, 0:ntc, :], b[:, 0:NT].re("p (c k) -> p c k", k=128))
                b = bank()
                for c in range(ntc):
                    csl = slice(c * 128, (c + 1) * 128)
                    P.mm(b[:, csl], kt[:, csl], qt[:, csl], start=True, stop=True, sig=(c == ntc - 1))
                P.tt("dve", AT[:, N_], b[:, N_], mask2[:, N_], ALU.mult)
                pb = [PS[6], PS[7]]
                for ci, (c0, cl) in enumerate(chunks):
                    prt = slice(c0 % 128, c0 % 128 + cl)
                    P.mm(pb[ci % 2][:, (ci // 2) * 128:(ci // 2 + 1) * 128], kd_tm[prt, c0 // 128, :],
                         vtm[prt, c0 // 128, h * 128:(h + 1) * 128], start=True, stop=True)
                ob = PS[4 + h % 2]
                for c in range(ntc):
                    csl = slice(c * 128, (c + 1) * 128)
                    P.mm(ob[:, csl], vtm[:, c, h * 128:(h + 1) * 128], AT[:, csl], start=(c == 0), stop=False)
                cur = 0 if not sample else 0
                for ci, (c0, cl) in enumerate(chunks):
                    s_in = S[h][cur_state[(sample, h)]]
                    s_out = S[h][1 - cur_state[(sample, h)]]
                    sbf = Sb[sb_i[0] % 8]
                    sb_i[0] += 1
                    P.cp("dve", sbf, s_in)
                    cw = 64 if not sample else 64
                    P.mm(ob[:, c0:c0 + cw], sbf, qt[:, c0:c0 + cw], start=False, stop=(ci == len(chunks) - 1))
                    P.stt(s_out, s_in, heb[:, c0 + cl - 1:c0 + cl], pb[ci % 2][:, (ci // 2) * 128:(ci // 2 + 1) * 128],
                          ALU.mult, ALU.add)
                    cur_state[(sample, h)] = 1 - cur_state[(sample, h)]
                if sample:
                    P.cp("dve", oTown[:, h, :], ob[:, 0:128])
                else:
                    P.ts("dve", oTown[:, h, :], ob[:, 0:128], flag[:, 0:1], ALU.mult)
                    for r in range(1, 4):
                        P.stt(oTown[:, h, :], ob[:, r * 128:(r + 1) * 128], flag[:, r:r + 1], oTown[:, h, :], ALU.mult, ALU.add)
            if sample:
                P.dma("pool", OTs_scr.re("h v t -> v h t"), oTown)
                P.dma("pool", X1s_scr, x1_tm[0])
                P.cp("pool", x1Town[:, :, 0:128], x1T_all[:, :, 0:128])
                own_proj(0, 128, True)
            elif 'nosel' in KNOB:
                pass
            else:
                P.dma("pool", OT_scr[:, :, gi * 128:(gi + 1) * 128].re("h v t -> v h t"), oTown)
                P.ts("dve", x1own, x1_tm[0], flag[:, 0:1], ALU.mult)
                for r in range(1, 4):
                    P.stt(x1own, x1_tm[r], flag[:, r:r + 1], x1own, ALU.mult, ALU.add)
                P.dma("pool", X1_scr[gi * 128:(gi + 1) * 128, :], x1own)
                dst = x1Town[:, :, (gi % 4) * 128:(gi % 4 + 1) * 128]
                P.ts("dve", dst, x1T_all[:, :, 0:128], flag[:, 0:1], ALU.mult)
                for r in range(1, 4):
                    P.stt(dst, x1T_all[:, :, r * 128:(r + 1) * 128], flag[:, r:r + 1], dst, ALU.mult, ALU.add)
                if gi % 4 == 3:
                    own_proj((gi // 4) * 512, 512, False)

        cur_state = {(s_, h): 0 for s_ in (False, True) for h in range(HA)}

        def own_proj(t0, N, sample):
            QT, GA, GT = (QTs_scr, GAs_scr, GTs_scr) if sample else (QT_scr, GA_scr, GT_scr)
            for (nm, noc) in (("qb", 4), ("ga", 4), ("gta", 8), ("gtb", 8)):
                for oc in range(noc):
                    tw = ring_load(ws[nm][oc], KC)
                    b = bank()
                    for k in range(KC):
                        P.mm(b[:, 0:N], tw[:, k, :], x1Town[:, k, 0:N], start=(k == 0), stop=(k == KC - 1), sig=(k == KC - 1))
                    o = pj[pj_i[0] % 2]
                    pj_i[0] += 1
                    if nm == "qb":
                        P.act(o[:, 0:N], b[:, 0:N], AF.Copy, scale=0.125)
                        P.dma("pool", QT[2 * oc:2 * oc + 2, :, t0:t0 + N].re("h d s -> (h d) s"), o[:, 0:N])
                    elif nm == "ga":
                        P.act(o[:, 0:N], b[:, 0:N], AF.Silu)
                        P.dma("pool", GA[oc, :, t0:t0 + N], o[:, 0:N])
                    else:
                        P.act(o[:, 0:N], b[:, 0:N], AF.Tanh, scale=0.5)
                        P.dma("pool", GT[oc + (8 if nm == "gtb" else 0), :, t0:t0 + N], o[:, 0:N])

        P.memset("pool", x_tm[0], 0.0)
        P.dma("sp", x_tm[0][0:DEC, :], xs)
        for h in range(HA):
            P.dma("sp", Ssm[h][0], st0[h])
        transpose_to_fm(x_tm, xT_all, 1)
        ffn_ln("ffn1", x_tm, xT_all, hT, sg, x1_tm, ln1g, ln1b, small, 1)
        def early_exit():
            P.barrier()
            ea.close()
            P.barrier()
            es.close()
            return nc, P
        if stop_after == "s_ffn":
            P.dma("pool", dbg[0:128, :], x1_tm[0])
            return early_exit()
        group_tail(0, 1, True)
        for h in range(HA):
            P.dma("pool", ss[h], Ssm[h][cur_state[(True, h)]])
        if stop_after == "s_tail":
            return early_exit()
        for b0 in range(0, NPB, 4):
            tiles = []
            nb = min(4, NPB - b0)
            for i in range(nb):
                P.dma("sp", x1_tm[i][:, 0:512], ck[(b0 + i) * 128:(b0 + i + 1) * 128, :])
                tiles.append(x1_tm[i][:, 0:512])
            kT_path(tiles, KTs_scr, b0 * 128, nb)

        if stop_after == "past":
            return early_exit()
        for gi in range(NG):
            if stop_after == "A1" and gi == 1:
                return early_exit()
            for c in range(4):
                P.dma("sp", x_tm[c], xb[gi * TB + c * 128: gi * TB + (c + 1) * 128, :])
            transpose_to_fm(x_tm, xT_all, 4)
            ffn_ln("ffn1", x_tm, xT_all, hT, sg, x1_tm, ln1g, ln1b, small, 4)
            if stop_after == "ffn1":
                for c in range(4):
                    P.dma("pool", dbg[gi * TB + c * 128: gi * TB + (c + 1) * 128, :], x1_tm[c])
                continue
            group_tail(gi, 4, False)
        for h in range(HA):
            P.dma("pool", s_fin[h], Sst[h][cur_state[(False, h)]])
        P.barrier()
    if stop_after in ("ffn1", "A"):
        P.barrier()
        es.close()
        return nc, P

    with ExitStack() as eb_:
        uincl = P.sb(eb_, "uincl", [128, 128], BF16)
        P.dma("pool", uincl, c_uincl)
        onesb = P.sb(eb_, "onesb", [128, 128], BF16)
        P.memset("pool", onesb, 1.0)
        sbm = P.sb(eb_, "sbm", [128, 16, 512], BF16)
        for m0 in range(0, 16, 4):
            P.dma("pool", sbm[:, m0:m0 + 4, :], c_sbmask[m0:m0 + 4].re("m p t -> p m t"))
        smk = P.sb(eb_, "smk", [128, DEC], BF16)
        P.dma("pool", smk, c_smask)
        KTh = [P.sb(eb_, "KTh%d" % i, [DH, max(SEQ, SKS)], BF16) for i in range(2)]
        Vh = [P.sb(eb_, "Vh%d" % i, [128, max(NKB, NPB + 1), DH], BF16) for i in range(2)]
        QTh = [P.sb(eb_, "QTh%d" % i, [DH, max(NOWN, 128)], BF16) for i in range(2)]
        E = [P.sb(eb_, "sbE%d" % i, [128, 128], F32) for i in range(4)]
        SPt = [P.sb(eb_, "sbSP%d" % i, [128, 128], BF16) for i in range(4)]
        SPm = [P.sb(eb_, "sbSPm%d" % i, [128, 128], BF16) for i in range(4)]
        Em = [P.sb(eb_, "sbEm%d" % i, [128, 128], F32) for i in range(4)]
        E2 = [P.sb(eb_, "sbE2%d" % i, [128, 128], F32) for i in range(2)]
        Wt = [P.sb(eb_, "sbW%d" % i, [128, 128], BF16) for i in range(2)]
        R32 = [P.sb(eb_, "sbR32_%d" % i, [128, 128], F32) for i in range(2)]
        Rb = [P.sb(eb_, "sbRb%d" % i, [128, 128], BF16) for i in range(4)]
        hbo = [P.sb(eb_, "hbo%d" % i, [DH, 512], BF16) for i in range(2)]
        for i in range(2):
            P.memset("pool", hbo[i], 0.0)
        ti = [0]
        oi = [0]

        def sb_attend(KT, V, QT, nq, kblocks, dst):
            Q_ = slice(0, nq)
            n = len(kblocks)
            outb = PS[6 + oi[0] % 2]
            ho = hbo[oi[0] % 2]
            oi[0] += 1
            base = ti[0]
            ti[0] += n
            spm_of, em_of = {}, {}

            def stA(i):
                kb = kblocks[i][0]
                P.mm(PS[(base + i) % 3][:, Q_], KT[:, kb * 128:(kb + 1) * 128], QT[:, Q_])

            def stB(i):
                mk = kblocks[i][1]
                j = base + i
                e = E[j % 4]
                P.act(e[:, Q_], PS[j % 3][:, Q_], AF.Exp)
                sp = SPt[j % 4]
                P.act(sp[:, Q_], e[:, Q_], AF.Ln, bias=1.0)
                if mk is not None:
                    spm = SPm[j % 4]
                    P.tt("dve", spm[:, Q_], sp[:, Q_], mk, ALU.mult)
                    em = Em[j % 4]
                    P.tt("dve", em[:, Q_], e[:, Q_], mk, ALU.mult)
                else:
                    spm, em = sp, e
                spm_of[i], em_of[i] = spm, em

            def stC(i):
                c = PS[3 + (base + i) % 3]
                ops = [(uincl, spm_of[i])]
                if i >= 1:
                    ops.append((onesb, spm_of[i - 1]))
                if i >= 2:
                    ops.append((onesb, Rb[(base + i - 2) % 4]))
                for oi_, (l, r_) in enumerate(ops):
                    P.mm(c[:, Q_], l, r_[:, Q_], start=(oi_ == 0), stop=(oi_ == len(ops) - 1), sig=(oi_ == len(ops) - 1))

            def stR(i):
                if i + 2 > n - 1:
                    return
                if i == 0:
                    P.cp("pool", R32[(base + i) % 2][:, Q_], spm_of[0][:, Q_])
                else:
                    P.tt("pool", R32[(base + i) % 2][:, Q_], R32[(base + i - 1) % 2][:, Q_], spm_of[i][:, Q_], ALU.add)

            def stRc(i):
                if i + 2 > n - 1:
                    return
                P.cp("dve", Rb[(base + i) % 4][:, Q_], R32[(base + i) % 2][:, Q_])

            def stD(i):
                j = base + i
                P.act(E2[j % 2][:, Q_], PS[3 + j % 3][:, Q_], AF.Exp, scale=-1.0)

            def stE(i):
                j = base + i
                P.tt("dve", Wt[j % 2][:, Q_], em_of[i][:, Q_], E2[j % 2][:, Q_], ALU.mult)

            def stF(i):
                kb = kblocks[i][0]
                P.mm(outb[0:DH, Q_], V[:, kb, :], Wt[(base + i) % 2][:, Q_], start=(i == 0), stop=(i == n - 1))

            for t in range(-3, n + 2):
                for fn, off in ((stA, 3), (stB, 2), (stRc, -1), (stC, 1), (stR, 0), (stD, 0), (stE, -1), (stF, -1)):
                    i = t + off
                    if 0 <= i < n:
                        fn(i)
            P.cp("act", ho[:, Q_], outb[0:DH, Q_])
            P.dma("pool", dst, ho[:, 0:max(nq, 128)])

        ZP = [T(Buf("zp%d" % m), psum[:, 2 * m * 512:(2 * m + 2) * 512]) for m in range(3)]
        E2p = [P.sb(eb_, "pE%d" % i, [128, 1024], F32) for i in range(2)]
        SP2 = [P.sb(eb_, "pSP%d" % i, [128, 1024], BF16) for i in range(3)]
        S2 = [P.sb(eb_, "pS2%d" % i, [128, 512], BF16) for i in range(4)]
        W2 = [P.sb(eb_, "pW%d" % i, [128, 1024], BF16) for i in range(3)]
        Rb2 = [P.sb(eb_, "pRb%d" % i, [128, 512], BF16) for i in range(3)]
        R2 = [P.sb(eb_, "pR%d" % i, [128, 512], F32) for i in range(2)]
        qn = [P.sb(eb_, "qn%d" % i, [DH, 512], BF16) for i in range(2)]
        pi_ = [0]

        def sb_pipeline(jobs):
            H = (slice(0, 512), slice(512, 1024))
            look = []
            for jn, jb in enumerate(jobs):
                jb["np"] = len(jb["kblocks"]) // 2
                jb["outb"] = PS[6 + jn % 2]
                jb["ho"] = hbo[jn % 2]
                jb["qneg"] = qn[jn % 2]
                for i in range(jb["np"]):
                    look.append((jb, i))
            NP = len(look)

            def kbs(jb, i):
                return jb["kblocks"][2 * i], jb["kblocks"][2 * i + 1]

            def stA(G):
                jb, i = look[G]
                if i == 0:
                    if jb.get("pre"):
                        jb["pre"]()
                    P.ts("dve", jb["qneg"], jb["QT"], -1.0, ALU.mult)
                z = ZP[G % 3]
                for hh, (kb, mk) in enumerate(kbs(jb, i)):
                    P.mm(z[:, H[hh]], jb["KT"][:, kb * 128:(kb + 1) * 128], jb["QT"])

            def stB(G):
                jb, i = look[G]
                z, e, sp = ZP[G % 3], E2p[G % 2], SP2[G % 3]
                P.act(e, z, AF.Exp)
                P.act(sp, e, AF.Ln, bias=1.0)
                for hh, (kb, mk) in enumerate(kbs(jb, i)):
                    if mk is not None:
                        P.tt("dve", sp[:, H[hh]], sp[:, H[hh]], mk, ALU.mult)
                if i + 1 <= jb["np"] - 1:
                    P.tt("dve", S2[G % 4], sp[:, H[0]], sp[:, H[1]], ALU.add)

            def stRc(G):
                jb, i = look[G]
                if i + 2 > jb["np"] - 1:
                    return
                P.cp("dve", Rb2[G % 3], R2[G % 2])

            def stC(G):
                jb, i = look[G]
                z, sp = ZP[G % 3], SP2[G % 3]
                for hh, (kb, mk) in enumerate(kbs(jb, i)):
                    ops = [(jb["KT"][:, kb * 128:(kb + 1) * 128], jb["qneg"]), (uincl, sp[:, H[hh]])]
                    if hh == 1:
                        ops.append((onesb, sp[:, H[0]]))
                    if i >= 1:
                        ops.append((onesb, S2[(G - 1) % 4]))
                    if i >= 2:
                        ops.append((onesb, Rb2[(G - 2) % 3]))
                    for oi_, (l, r_) in enumerate(ops):
                        P.mm(z[:, H[hh]], l, r_, start=(oi_ == 0), stop=(oi_ == len(ops) - 1), sig=(oi_ == len(ops) - 1))

            def stR(G):
                jb, i = look[G]
                if i + 2 > jb["np"] - 1:
                    return
                if i == 0:
                    P.cp("pool", R2[G % 2], S2[G % 4])
                else:
                    P.tt("pool", R2[G % 2], R2[(G - 1) % 2], S2[G % 4], ALU.add)

            def stD(G):
                jb, i = look[G]
                w = W2[G % 3]
                P.act(w, ZP[G % 3], AF.Exp, scale=-1.0)
                for hh, (kb, mk) in enumerate(kbs(jb, i)):
                    if mk is not None:
                        P.tt("dve", w[:, H[hh]], w[:, H[hh]], mk, ALU.mult)

            def stF(G):
                jb, i = look[G]
                w = W2[G % 3]
                for hh, (kb, mk) in enumerate(kbs(jb, i)):
                    P.mm(jb["outb"][0:DH, :], jb["V"][:, kb, :], w[:, H[hh]], start=(i == 0 and hh == 0),
                         stop=(i == jb["np"] - 1 and hh == 1))
                if i == jb["np"] - 1:
                    P.cp("act", jb["ho"], jb["outb"][0:DH, :])
                    P.dma("pool", jb["dst"], jb["ho"])

            for t in range(-2, NP + 1):
                for fn, off in ((stA, 2), (stB, 1), (stRc, -1), (stC, 1), (stR, 0), (stD, 0), (stF, -1)):
                    G = t + off
                    if 0 <= G < NP:
                        fn(G)

        hi = [0]
        for h in range(HB):
            kt_, v_, q_ = KTh[hi[0] % 2], Vh[hi[0] % 2], QTh[hi[0] % 2]
            hi[0] += 1
            P.dma("sp", kt_[:, 0:SKS], KTs_scr[h])
            P.dma("sp", v_[:, 0:NPB + 1, :], Vs_scr[h])
            P.dma("sp", q_[:, 0:128], QTs_scr[h])
            kbl = [(NPB, smk)] + [(kb, None) for kb in range(NPB - 1, -1, -1)]
            sb_attend(kt_, v_, q_, DEC, kbl, HBs_scr[h, :, 0:128])
        P.barrier()
        if stop_after != "Bs":
            jobs = []
            for h in range(HB):
                kt_, v_, q_ = KTh[hi[0] % 2], Vh[hi[0] % 2], QTh[hi[0] % 2]
                hi[0] += 1

                def pre(h=h, kt_=kt_, v_=v_, q_=q_):
                    P.dma("sp", kt_[:, 0:SEQ], KT_scr[h])
                    P.dma("sp", v_[:, 0:NKB, :], V_scr[h])
                    P.dma("sp", q_[:, 0:NOWN], QT_scr[h])
                for M in range(NSG):
                    kbl = [(kb, (sbm[:, kb - 16 * M, :] if kb >= 16 * M else None)) for kb in range(16 * M + 15, -1, -1)]
                    jobs.append(dict(pre=(pre if M == 0 else None), KT=kt_, V=v_, QT=q_[:, M * 512:(M + 1) * 512],
                                     kblocks=kbl, dst=HB_scr[h, :, M * 512:(M + 1) * 512]))
            sb_pipeline(jobs)
        P.barrier()
    if stop_after in ("B", "Bs"):
        P.barrier()
        es.close()
        return nc, P

    with ExitStack() as ec:
        x1o = [P.sb(ec, "x1o%d" % c, [128, D], F32) for c in range(4)]
        x2_tm = [P.sb(ec, "x2_tm%d" % c, [128, D], F32) for c in range(4)]
        y_tm = [P.sb(ec, "y_tm%d" % c, [128, D], F32) for c in range(4)]
        xT_all = P.sb(ec, "xT2", [128, KC, TB], BF16)
        hT_all = P.sb(ec, "hT2", [128, FC, TB], BF16)
        hT = [hT_all.sub((slice(None), f, slice(None)), "hT2_%d" % f) for f in range(FC)]
        sg = [P.sb(ec, "sg2_%d" % i, [128, TB], F32) for i in range(2)]
        ln2g = bcast_load(ec, "ln2g", Wd["ln2_g"], D)
        ln2b = bcast_load(ec, "ln2b", Wd["ln2_b"], D)
        ln3g = bcast_load(ec, "ln3g", Wd["ln3_g"], D)
        ln3b = bcast_load(ec, "ln3b", Wd["ln3_b"], D)
        small = mk_small(ec)
        mhalf = small[4]
        wo = [P.sb(ec, "wo%d" % i, [128, KC, 512], BF16) for i in range(2)]
        for i in range(2):
            P.dma("sp", wo[i], ws["wo"][i])
        gn = P.sb(ec, "gn", [128, HA], F32)
        P.dma("sp", gn, Wd["hgrn_norm_g"].re("o (h p) -> p (o h)", p=128), allow_slow_non_contiguous=True)
        onesf = P.sb(ec, "onesf", [128, 128], F32)
        P.memset("pool", onesf, 1.0 / 128.0)
        oT4 = P.sb(ec, "oT4", [128, HA, TB], F32)
        ga4 = P.sb(ec, "ga4", [128, HA, TB], BF16)
        gt16 = P.sb(ec, "gt16", [128, 16, TB], BF16)
        hb8 = P.sb(ec, "hb8", [DH, HB, TB], BF16)
        sq = P.sb(ec, "sq", [128, TB], F32)
        t1 = P.sb(ec, "t1", [128, TB], F32)
        rs = P.sb(ec, "rs", [128, TB], F32)
        haT = P.sb(ec, "haT", [128, HA, TB], BF16)
        m1 = P.sb(ec, "m1", [128, TB], F32)
        m2 = P.sb(ec, "m2", [128, TB], F32)
        mergedT = P.sb(ec, "mergedT", [128, KC, TB], BF16)

        def own_tail(t0, ntc, sample):
            N = ntc * 128
            N_ = slice(0, N)
            OTs, GAs, GTs, HBs, X1s = (OTs_scr, GAs_scr, GTs_scr, HBs_scr, X1s_scr) if sample else \
                (OT_scr, GA_scr, GT_scr, HB_scr, X1_scr)
            P.dma("sp", oT4[:, :, N_], OTs[:, :, t0:t0 + N].re("h v t -> v h t"))
            P.dma("sp", ga4[:, :, N_], GAs[:, :, t0:t0 + N].re("h p t -> p h t"))
            P.dma("sp", gt16[:, :, N_], GTs[:, :, t0:t0 + N].re("h p t -> p h t"))
            P.dma("sp", hb8[:, :, N_], HBs[:, :, t0:t0 + N].re("h d t -> d h t"))
            for c in range(ntc):
                P.dma("sp", x1o[c], X1s[t0 + c * 128:t0 + (c + 1) * 128, :])
            for h in range(HA):
                P.act(sq[:, N_], oT4[:, h, N_], AF.Square)
                b = bank()
                P.mm(b[:, N_], onesf, sq[:, N_])
                P.ts("dve", t1[:, N_], b[:, N_], RMS_EPS, ALU.add)
                P.act(rs[:, N_], t1[:, N_], AF.Ln)
                P.act(rs[:, N_], rs[:, N_], AF.Exp, scale=-0.5)
                P.tt("dve", t1[:, N_], oT4[:, h, N_], rs[:, N_], ALU.mult)
                P.stt(haT[:, h, N_], t1[:, N_], gn[:, h:h + 1], ga4[:, h, N_], ALU.mult, ALU.mult)
            for oc in range(KC):
                wa = ring_load(ws["wa"][oc], 4)
                pa = bank()
                for kc in range(4):
                    P.mm(pa[:, N_], wa[:, kc, :], haT[:, kc, N_], start=(kc == 0), stop=(kc == 3), sig=(kc == 3))
                wb = ring_load(ws["wb"][oc], 8, npart=DH)
                pb_ = bank()
                for h in range(HB):
                    P.mm(pb_[:, N_], wb[:, h, :], hb8[:, h, N_], start=(h == 0), stop=(h == HB - 1), sig=(h == HB - 1))
                P.stt(m1[:, N_], gt16[:, oc, N_], 1.0, pa[:, N_], ALU.add, ALU.mult)
                P.stt(m2[:, N_], gt16[:, 8 + oc, N_], 1.0, pb_[:, N_], ALU.add, ALU.mult)
                P.tt("dve", mergedT[:, oc, N_], m1[:, N_], m2[:, N_], ALU.add)
            for c in range(ntc):
                o = x2_tm[c]
                for hf in range(2):
                    acc = PS[4 + (c % 2) * 2 + hf]
                    for k in range(KC):
                        P.mm(acc, mergedT[:, k, c * 128:(c + 1) * 128], wo[hf][:, k, :], start=(k == 0), stop=(k == KC - 1),
                             sig=(k == KC - 1))
                    P.stt(o[:, hf * 512:(hf + 1) * 512], acc, 0.5 / ALPHA, x1o[c][:, hf * 512:(hf + 1) * 512], ALU.mult, ALU.add)
                layer_norm(o, ln2g, ln2b, small, LN_EPS / (ALPHA * ALPHA))
            transpose_to_fm(x2_tm, xT_all, ntc)
            ffn_ln("ffn2", x2_tm, xT_all, hT, sg, y_tm, ln3g, ln3b, small, ntc)
            for c in range(ntc):
                if sample:
                    P.dma("pool", ys, y_tm[0][0:DEC, :])
                else:
                    P.dma("pool", y_own[t0 + c * 128:t0 + (c + 1) * 128, :], y_tm[c])

        own_tail(0, 1, True)
        if stop_after != "Cs":
            for M in range(NSG):
                own_tail(M * 512, 4, False)
        P.barrier()
    P.barrier()
    es.close()
    return nc, P


def make_consts(g):
    c = {}
    c["c_ident"] = np.eye(128, dtype=np.float32)
    t = np.arange(512)
    c["c_mask64"] = np.broadcast_to((t % 64 != 0).astype(np.float32), (128, 512)).copy()
    s = np.arange(128)[:, None]
    tt = np.arange(128)[None, :]
    m2 = ((s // 64 == tt // 64) & (s <= tt)).astype(np.float32)
    c["c_mask2"] = np.tile(m2, (1, 4))
    c["c_uincl"] = (np.arange(128)[:, None] >= np.arange(128)[None, :]).astype(np.float32)
    sb = np.zeros((16, 128, 512), np.float32)
    for mrel in range(16):
        for j in range(4):
            kpos = 128 * mrel + np.arange(128)[:, None]
            qpos = 128 * (4 * j + g) + np.arange(128)[None, :]
            sb[mrel, :, j * 128:(j + 1) * 128] = (kpos < qpos)
    c["c_sbmask"] = sb
    c["c_smask"] = ((np.arange(128)[:, None] < np.arange(DEC)[None, :])).astype(np.float32)
    fl = np.zeros((128, 4), np.float32)
    fl[:, g] = 1.0
    c["c_flag"] = fl
    return c


def make_in_maps(inp, SEQ, PAST):
    maps = []
    f = lambda a: np.ascontiguousarray(np.asarray(a, dtype=np.float32))
    wts = {k: f(inp[k]).reshape(W_SHAPES[k]) for k in W_SHAPES}
    for c in range(8):
        b, g = c // 4, c % 4
        m = dict(wts)
        m["xb"] = f(inp["x_prompt"][b])
        m["xs"] = f(inp["x_sample"][c])
        m["ck"] = f(inp["cache_sb_k"][0, c]).reshape(PAST, 512)
        m["cv"] = f(inp["cache_sb_v"][0, c]).reshape(PAST, 512)
        m["st0"] = f(inp["state_hgrn"][0, c])
        m.update(make_consts(g))
        maps.append(m)
    return maps


_CACHE = {}


def kernel(**inputs):
    SEQ = inputs["x_prompt"].shape[1]
    PAST = inputs["cache_sb_k"].shape[2]
    key = (SEQ, PAST)
    if key not in _CACHE:
        _CACHE[key] = build(SEQ, PAST)[0]
    nc = _CACHE[key]
    maps = make_in_maps(inputs, SEQ, PAST)
    res = run_bass_kernel_spmd(nc, maps, core_ids=list(range(8)))
    return assemble(res.results, SEQ)


def assemble(r, SEQ):
    NB128 = SEQ // 128
    y_prompt = np.zeros((2, SEQ, D), np.float32)
    for c in range(8):
        b, g = c // 4, c % 4
        yo = np.asarray(r[c]["y_own"]).reshape(NB128 // 4, 128, D)
        y_prompt[b].reshape(NB128 // 4, 4, 128, D)[:, g] = yo
    y_sample = np.stack([np.asarray(r[c]["ys"]) for c in range(8)])
    k_p = np.stack([np.asarray(r[4 * b]["k_all"]).reshape(SEQ, HB, DH) for b in range(2)])[None]
    v_p = np.stack([np.asarray(r[4 * b]["v_all"]).reshape(SEQ, HB, DH) for b in range(2)])[None]
    s_p = np.stack([np.asarray(r[4 * b]["s_fin"]) for b in range(2)])[None]
    k_s = np.stack([np.asarray(r[c]["ks"]).reshape(DEC, HB, DH) for c in range(8)])[None]
    v_s = np.stack([np.asarray(r[c]["vs"]).reshape(DEC, HB, DH) for c in range(8)])[None]
    s_s = np.stack([np.asarray(r[c]["ss"]) for c in range(8)])[None]
    return (y_prompt, y_sample, k_p.astype(np.float32), v_p.astype(np.float32), s_p.astype(np.float32),
            k_s.astype(np.float32), v_s.astype(np.float32), s_s.astype(np.float32))
```

```python
import numpy as np
import os
from contextlib import ExitStack
KNOB = os.environ.get('KNOB', '')
import concourse.bass as bass
import concourse.mybir as mybir
from concourse.bass_utils import run_bass_kernel_spmd

F32 = mybir.dt.float32
BF16 = mybir.dt.bfloat16
AF = mybir.ActivationFunctionType
ALU = mybir.AluOpType

D = 1024
KC = 8
FF = 2816
FC = 22
HA = 4
HB = 8
DH = 64
NIN = 5632
ALPHA = 2.0 ** 0.25
LN_EPS = 1e-5
RMS_EPS = 1e-6
TB = 512
DEC = 16


class Buf:
    __slots__ = ("name", "w", "r", "sem", "accum")

    def __init__(self, name, accum=False):
        self.name = name
        self.w = {}
        self.r = {}
        self.sem = None
        self.accum = accum


class T:
    __slots__ = ("buf", "ap")

    def __init__(self, buf, ap):
        self.buf = buf
        self.ap = ap

    def __getitem__(self, idx):
        return T(self.buf, self.ap[idx])

    def sub(self, idx, name):
        return T(Buf(name), self.ap[idx])

    def re(self, s, **kw):
        return T(self.buf, self.ap.rearrange(s, **kw))


class DSem:
    __slots__ = ("sem", "total", "id")

    def __init__(self, sem, i):
        self.sem = sem
        self.total = 0
        self.id = i


class Prog:
    ENGS = ("pe", "act", "dve", "pool", "sp")

    def __init__(self, nc, es):
        self.nc = nc
        self.es = es
        self.eng = {"pe": nc.tensor, "act": nc.scalar, "dve": nc.vector, "pool": nc.gpsimd, "sp": nc.sync}
        self.sem = {e: es.enter_context(nc.semaphore("sem_" + e)) for e in self.ENGS}
        self.cnt = {e: 0 for e in self.ENGS}
        self.seen = {e: {} for e in self.ENGS}
        self.dsems = []
        self.nwait = 0
        self.nops = 0

    def sb(self, es, name, shape, dtype):
        self.nalloc = getattr(self, "nalloc", 0) + 1
        name = "%s_%d" % (name, self.nalloc)
        nb = int(np.prod(shape[1:])) * (2 if dtype == BF16 else 4)
        self.sb_cur = getattr(self, "sb_cur", 0) + nb
        self.sb_peak = max(getattr(self, "sb_peak", 0), self.sb_cur)
        es.callback(self._free, nb)
        t = es.enter_context(self.nc.sbuf_tensor(name, list(shape), dtype))
        return T(Buf(name), t[:] if hasattr(t, "__getitem__") else t)

    def _free(self, nb):
        self.sb_cur -= nb

    def ps(self, es, name, shape, dtype=F32):
        t = es.enter_context(self.nc.psum_tensor(name, list(shape), dtype))
        return T(Buf(name), t[:])

    def dram(self, name, shape, dtype, kind="Internal", accum=False):
        t = self.nc.dram_tensor(name, list(shape), dtype, kind=kind)
        return T(Buf(name, accum=accum), t.ap())

    def dsem(self, buf, e):
        kind = "sw" if e == "pool" else "hw"
        if buf.sem is None:
            buf.sem = {}
        if kind not in buf.sem:
            s = self.es.enter_context(self.nc.semaphore("dsem%d" % len(self.dsems)))
            buf.sem[kind] = DSem(s, len(self.dsems))
            self.dsems.append(buf.sem[kind])
        return buf.sem[kind]

    def _wait(self, e, key, val):
        if key[0] == "c":
            sem = self.sem[key[1]]
            sid = key[1]
        else:
            sem = key[1].sem
            sid = key[1].id
            val = key[1].total
        if val <= 0:
            return
        if self.seen[e].get(sid, 0) >= val:
            return
        self.seen[e][sid] = val
        self.eng[e].wait_ge(sem, val)
        self.nwait += 1

    def _deps(self, e, reads, writes, is_dma):
        for b in reads:
            for key, val in list(b.w.items()):
                self._wait(e, key, val)
        for b in writes:
            if b.accum:
                continue
            for key, val in list(b.w.items()):
                if key == ("c", "pe") and e == "pe" and not is_dma:
                    continue
                self._wait(e, key, val)
            for key, val in list(b.r.items()):
                self._wait(e, key, val)

    def op(self, e, fn, reads, writes, sig=True):
        rb = [x.buf if isinstance(x, T) else x for x in reads]
        wb = [x.buf if isinstance(x, T) else x for x in writes]
        self._deps(e, rb, wb, False)
        ins = fn()
        if sig:
            self.cnt[e] += 1
            ins.then_inc(self.sem[e], 1)
            val = self.cnt[e]
        else:
            val = self.cnt[e] + 1
        key = ("c", e)
        for b in rb:
            b.r[key] = val
        for b in wb:
            if not b.accum:
                b.w = {key: val}
                b.r = {}
            else:
                b.w[key] = val
        self.nops += 1
        return ins

    def dma(self, e, out, in_, owner=None, **kw):
        ob, ib = out.buf, in_.buf
        if owner is None:
            owner = ib if ob.accum else ob
        ds = self.dsem(owner, e)
        self._deps(e, [ib], [ob], True)
        ins = self.eng[e].dma_start(out=out.ap, in_=in_.ap, **kw)
        ds.total += 16
        ins.then_inc(ds.sem, 16)
        key = ("d", ds)
        ib.r[key] = ds.total
        if ob.accum:
            ob.w[key] = ds.total
        else:
            ob.w = {key: ds.total}
            ob.r = {}
        self.nops += 1
        return ins

    def barrier(self):
        for e in self.ENGS:
            for o in self.ENGS:
                if o != e:
                    self._wait(e, ("c", o), self.cnt[o])
            for ds in self.dsems:
                self._wait(e, ("d", ds), ds.total)

    def mm(self, out, lhsT, rhs, start=True, stop=True, sig=True):
        return self.op("pe", lambda: self.nc.tensor.matmul(out.ap, lhsT.ap, rhs.ap, start=start, stop=stop),
                       [lhsT, rhs], [out], sig=sig)

    def tr(self, out, in_, ident, sig=True):
        return self.op("pe", lambda: self.nc.tensor.transpose(out.ap, in_.ap, ident.ap), [in_, ident], [out], sig=sig)

    def act(self, out, in_, func, bias=None, scale=None, extra_reads=()):
        kw = {}
        rd = [in_] + list(extra_reads)
        if bias is not None:
            if isinstance(bias, T):
                kw["bias"] = bias.ap
                rd.append(bias)
            else:
                kw["bias"] = bias
        if scale is not None:
            if isinstance(scale, T):
                kw["scale"] = scale.ap
                rd.append(scale)
            else:
                kw["scale"] = scale
        return self.op("act", lambda: self.nc.scalar.activation(out.ap, in_.ap, func, **kw), rd, [out])

    def _e(self, e):
        return self.eng[e]

    def tt(self, e, out, in0, in1, op):
        return self.op(e, lambda: self._e(e).tensor_tensor(out.ap, in0.ap, in1.ap, op), [in0, in1], [out])

    def ts(self, e, out, in0, s1, op0, s2=None, op1=None):
        rd = [in0]
        a1 = s1
        a2 = s2
        if isinstance(s1, T):
            rd.append(s1)
            a1 = s1.ap
        if isinstance(s2, T):
            rd.append(s2)
            a2 = s2.ap
        if op1 is None:
            return self.op(e, lambda: self._e(e).tensor_scalar(out.ap, in0.ap, a1, a2, op0), rd, [out])
        return self.op(e, lambda: self._e(e).tensor_scalar(out.ap, in0.ap, a1, a2, op0, op1), rd, [out])

    def stt(self, out, in0, scalar, in1, op0, op1):
        rd = [in0, in1]
        a = scalar
        if isinstance(scalar, T):
            rd.append(scalar)
            a = scalar.ap
        return self.op("dve", lambda: self.nc.vector.scalar_tensor_tensor(out.ap, in0.ap, a, in1.ap, op0, op1), rd, [out])

    def cp(self, e, out, in_):
        if e == "act":
            return self.op(e, lambda: self.nc.scalar.copy(out.ap, in_.ap), [in_], [out])
        return self.op(e, lambda: self._e(e).tensor_copy(out.ap, in_.ap), [in_], [out])

    def memset(self, e, out, val):
        return self.op(e, lambda: self._e(e).memset(out.ap, val), [], [out])


W_SHAPES = {
    "ffn1_wg": (D, FF), "ffn1_wu": (D, FF), "ffn1_wd": (FF, D), "ln1_g": (1, D), "ln1_b": (1, D),
    "w_in": (D, NIN), "lb_logits": (2, 512), "hgrn_norm_g": (1, 512),
    "w_branch_a": (512, D), "w_branch_b": (512, D), "w_out": (D, D), "ln2_g": (1, D), "ln2_b": (1, D),
    "ffn2_wg": (D, FF), "ffn2_wu": (D, FF), "ffn2_wd": (FF, D), "ln3_g": (1, D), "ln3_b": (1, D),
}
ST_BLOCKS = {"qa": (0, 4), "fa": (512, 4), "ga": (1536, 4), "qb": (2048, 4), "gta": (3584, 8), "gtb": (4608, 8)}
MV_BLOCKS = {"ia": 1024, "kb": 2560, "vb": 3072}


def build(SEQ, PAST, stop_after=None):
    NG = SEQ // TB
    NSG = SEQ // 2048
    NOWN = SEQ // 4
    NKB = SEQ // 128
    NPB = PAST // 128
    SKS = PAST + 128
    nc = bass.Bass("TRN2", target_bir_lowering=False)
    es = ExitStack()
    P = Prog(nc, es)
    IN = lambda n, s: P.dram(n, s, F32, "ExternalInput")
    OUT = lambda n, s: P.dram(n, s, F32, "ExternalOutput", accum=True)
    xb = IN("xb", [SEQ, D])
    xs = IN("xs", [DEC, D])
    ck = IN("ck", [PAST, 512])
    cv = IN("cv", [PAST, 512])
    st0 = IN("st0", [HA, 128, 128])
    Wd = {n: IN(n, list(s)) for n, s in W_SHAPES.items()}
    c_ident = IN("c_ident", [128, 128])
    c_mask64 = IN("c_mask64", [128, 512])
    c_mask2 = IN("c_mask2", [128, 512])
    c_uincl = IN("c_uincl", [128, 128])
    c_sbmask = IN("c_sbmask", [16, 128, 512])
    c_smask = IN("c_smask", [128, DEC])
    c_flag = IN("c_flag", [128, 4])
    y_own = OUT("y_own", [NOWN, D])
    k_all = OUT("k_all", [SEQ, 512])
    v_all = OUT("v_all", [SEQ, 512])
    s_fin = OUT("s_fin", [HA, 128, 128])
    ys = OUT("ys", [DEC, D])
    ks = OUT("ks", [DEC, 512])
    vs = OUT("vs", [DEC, 512])
    ss = OUT("ss", [HA, 128, 128])
    dbg = OUT("dbg", [SEQ, D]) if stop_after else None

    SCR = lambda n, s, dt=BF16: P.dram(n, s, dt, "Internal", accum=True)
    ws = {}
    for l in ("ffn1", "ffn2"):
        ws[l + "_wg"] = SCR(l + "_wg_s", [FC, 128, KC * 128])
        ws[l + "_wu"] = SCR(l + "_wu_s", [FC, 128, KC * 128])
        ws[l + "_wd"] = SCR(l + "_wd_s", [FC, 128, D])
    for n, (c0, noc) in ST_BLOCKS.items():
        ws[n] = SCR("win_" + n, [noc, 128, KC * 128])
    for n, c0 in MV_BLOCKS.items():
        ws[n] = SCR("win_" + n, [128, KC, 512])
    ws["wa"] = SCR("wa_s", [8, 128, 4 * 128])
    ws["wb"] = SCR("wb_s", [8, 64, 8 * 128])
    ws["wo"] = SCR("wo_s", [2, 128, KC, 512])
    KT_scr = SCR("KT_scr", [HB, DH, SEQ])
    V_scr = SCR("V_scr", [HB, 128, NKB, DH])
    QT_scr = SCR("QT_scr", [HB, DH, NOWN])
    GA_scr = SCR("GA_scr", [4, 128, NOWN])
    GT_scr = SCR("GT_scr", [16, 128, NOWN])
    OT_scr = SCR("OT_scr", [HA, 128, NOWN], F32)
    X1_scr = SCR("X1_scr", [NOWN, D], F32)
    HB_scr = SCR("HB_scr", [HB, DH, NOWN])
    KTs_scr = SCR("KTs_scr", [HB, DH, SKS])
    Vs_scr = SCR("Vs_scr", [HB, 128, NPB + 1, DH])
    QTs_scr = SCR("QTs_scr", [HB, DH, 128])
    GAs_scr = SCR("GAs_scr", [4, 128, 128])
    GTs_scr = SCR("GTs_scr", [16, 128, 128])
    OTs_scr = SCR("OTs_scr", [HA, 128, 128], F32)
    X1s_scr = SCR("X1s_scr", [128, D], F32)
    HBs_scr = SCR("HBs_scr", [HB, DH, 128])

    def prep_weights():
        q = "pool"
        kpc = lambda t: t.re("p (k c) -> p k c", c=128)
        for l in ("ffn1", "ffn2"):
            for f in range(FC):
                for nm in ("_wg", "_wu"):
                    P.dma(q, kpc(ws[l + nm][f]), Wd[l + nm][:, f * 128:(f + 1) * 128].re("(k p) c -> p k c", p=128))
                P.dma(q, ws[l + "_wd"][f], Wd[l + "_wd"][f * 128:(f + 1) * 128, :])
        for n, (c0, noc) in ST_BLOCKS.items():
            for oc in range(noc):
                P.dma(q, kpc(ws[n][oc]), Wd["w_in"][:, c0 + oc * 128:c0 + (oc + 1) * 128].re("(k p) c -> p k c", p=128))
        for n, c0 in MV_BLOCKS.items():
            P.dma(q, ws[n], Wd["w_in"][:, c0:c0 + 512].re("(k p) c -> p k c", p=128))
        for oc in range(8):
            P.dma(q, kpc(ws["wa"][oc]), Wd["w_branch_a"][:, oc * 128:(oc + 1) * 128].re("(k p) c -> p k c", p=128))
            P.dma(q, kpc(ws["wb"][oc]), Wd["w_branch_b"][:, oc * 128:(oc + 1) * 128].re("(h p) c -> p h c", p=64))
        for hf in range(2):
            P.dma(q, ws["wo"][hf], Wd["w_out"][:, hf * 512:(hf + 1) * 512].re("(k p) c -> p k c", p=128))
        for h in range(HB):
            for b0 in range(0, NPB, 8):
                b1 = min(NPB, b0 + 8)
                P.dma(q, Vs_scr[h, :, b0:b1, :],
                      cv[b0 * 128:b1 * 128, h * DH:(h + 1) * DH].re("(b p) d -> p b d", p=128))

    prep_weights()

    psum = es.enter_context(nc.psum_tensor("psum", [128, 8 * 512], F32))
    PS = [T(Buf("ps%d" % i), psum[:, i * 512:(i + 1) * 512]) for i in range(8)]
    ident = P.sb(es, "ident", [128, 128], F32)
    P.dma("sp", ident, c_ident)
    flag = P.sb(es, "flag", [128, 4], F32)
    P.dma("sp", flag, c_flag)
    NRING = 6
    ring = [P.sb(es, "ring%d" % i, [128, 1024], BF16) for i in range(NRING)]
    ring_i = [0]

    def ring_load(src, k=None, npart=128):
        slot = ring[ring_i[0] % NRING]
        ring_i[0] += 1
        n = src.ap.shape[-1]
        P.dma("sp", slot[0:npart, 0:n], src)
        v = slot[0:npart, 0:n]
        return v if k is None else v.re("p (k c) -> p k c", k=k)

    bank_i = [0]

    def bank(lo=0, hi=4):
        b = PS[lo + bank_i[0] % (hi - lo)]
        bank_i[0] += 1
        return b

    ev_i = [0]

    def evac(out, in_):
        e = "act" if ev_i[0] % 2 == 0 else "dve"
        ev_i[0] += 1
        P.cp(e, out, in_)

    def transpose_to_fm(src_tm, dstT, ntc, nk=KC):
        for k in range(nk):
            b = bank()
            for c in range(ntc):
                P.tr(b[:, c * 128:(c + 1) * 128], src_tm[c][:, k * 128:(k + 1) * 128], ident, sig=(c == ntc - 1))
            evac(dstT[:, k, 0:ntc * 128], b[:, 0:ntc * 128])

    def layer_norm(o, lng, lnb, small, eps):
        st, mv, ve, rstd, mhalf = small
        P.op("dve", lambda: nc.vector.bn_stats(st.ap[:, 0:6], o.ap[:, 0:512]), [o], [st])
        P.op("dve", lambda: nc.vector.bn_stats(st.ap[:, 6:12], o.ap[:, 512:1024]), [o], [st])
        P.op("dve", lambda: nc.vector.bn_aggr(mv.ap, st.ap), [st], [mv])
        P.ts("pool", ve, mv[:, 1:2], eps, ALU.add)
        P.tt("pool", rstd, ve, mhalf, ALU.pow)
        P.ts("dve", o, o, mv[:, 0:1], ALU.subtract, rstd, ALU.mult)
        P.tt("dve", o, o, lng, ALU.mult)
        P.tt("dve", o, o, lnb, ALU.add)

    def ffn_ln(pref, x_tm, xT, hT, sg, out_tm, lng, lnb, small, ntc):
        NT = ntc * 128
        wg, wu, wd = ws[pref + "_wg"], ws[pref + "_wu"], ws[pref + "_wd"]
        for f in range(FC):
            tg = ring_load(wg[f], KC)
            tu = ring_load(wu[f], KC)
            bg = bank()
            bu = bank()
            for k in range(KC):
                P.mm(bg[:, 0:NT], tg[:, k, :], xT[:, k, 0:NT], start=(k == 0), stop=(k == KC - 1), sig=(k == KC - 1))
            for k in range(KC):
                P.mm(bu[:, 0:NT], tu[:, k, :], xT[:, k, 0:NT], start=(k == 0), stop=(k == KC - 1), sig=(k == KC - 1))
            s = sg[f % 2]
            P.act(s[:, 0:NT], bg[:, 0:NT], AF.Silu)
            P.tt("dve", hT[f][:, 0:NT], s[:, 0:NT], bu[:, 0:NT], ALU.mult)
        for p0 in range(0, ntc, 2):
            cs = list(range(p0, min(p0 + 2, ntc)))
            ab = 4 if p0 == 0 else 0
            for f in range(FC):
                td = ring_load(wd[f])
                for c in cs:
                    for hf in range(2):
                        P.mm(PS[ab + (c % 2) * 2 + hf], hT[f][:, c * 128:(c + 1) * 128], td[:, hf * 512:(hf + 1) * 512],
                             start=(f == 0), stop=(f == FC - 1))
            for c in cs:
                o = out_tm[c]
                for hf in range(2):
                    P.stt(o[:, hf * 512:(hf + 1) * 512], PS[ab + (c % 2) * 2 + hf], 0.5 / ALPHA,
                          x_tm[c][:, hf * 512:(hf + 1) * 512], ALU.mult, ALU.add)
            for c in cs:
                layer_norm(out_tm[c], lng, lnb, small, LN_EPS / (ALPHA * ALPHA))

    def bcast_load(es_, name, src, n):
        t = P.sb(es_, name, [128, n], F32)
        P.dma("sp", t, T(src.buf, src.ap[0:1, :].partition_broadcast(128)))
        return t

    def mk_small(es_):
        st = P.sb(es_, "bnst", [128, 12], F32)
        mv = P.sb(es_, "bnmv", [128, 2], F32)
        ve = P.sb(es_, "bnve", [128, 1], F32)
        rstd = P.sb(es_, "bnrstd", [128, 1], F32)
        mhalf = P.sb(es_, "mhalf", [128, 1], F32)
        P.memset("pool", mhalf, -0.5)
        return (st, mv, ve, rstd, mhalf)

    with ExitStack() as ea:
        x_tm = [P.sb(ea, "x_tm%d" % c, [128, D], F32) for c in range(4)]
        x1_tm = [P.sb(ea, "x1_tm%d" % c, [128, D], F32) for c in range(4)]
        xT_all = P.sb(ea, "xT", [128, KC, TB], BF16)
        x1T_all = P.sb(ea, "x1T", [128, KC, TB], BF16)
        hT_all = P.sb(ea, "hT", [128, FC, TB], BF16)
        hT = [hT_all.sub((slice(None), f, slice(None)), "hT%d" % f) for f in range(FC)]
        sg = [P.sb(ea, "sg%d" % i, [128, TB], F32) for i in range(2)]
        ln1g = bcast_load(ea, "ln1g", Wd["ln1_g"], D)
        ln1b = bcast_load(ea, "ln1b", Wd["ln1_b"], D)
        small = mk_small(ea)
        wia = P.sb(ea, "wia", [128, KC, 512], BF16)
        wkb = P.sb(ea, "wkb", [128, KC, 512], BF16)
        wvb = P.sb(ea, "wvb", [128, KC, 512], BF16)
        P.dma("sp", wia, ws["ia"])
        P.dma("sp", wkb, ws["kb"])
        P.dma("sp", wvb, ws["vb"])
        mask64 = P.sb(ea, "mask64", [128, 512], F32)
        mask2 = P.sb(ea, "mask2", [128, 512], F32)
        P.dma("sp", mask64, c_mask64)
        P.dma("sp", mask2, c_mask2)
        lbl = P.sb(ea, "lbl", [128, 2, 4], F32)
        P.dma("sp", lbl, Wd["lb_logits"].re("r (h p) -> p r h", p=128), allow_slow_non_contiguous=True)
        oml = P.sb(ea, "oml", [128, 4], F32)
        noml = P.sb(ea, "noml", [128, 4], F32)
        lbd = P.sb(ea, "lbd", [128, 4], F32)
        P.tt("dve", lbd, lbl[:, 0, :], lbl[:, 1, :], ALU.subtract)
        P.act(lbd, lbd, AF.Exp)
        P.ts("dve", lbd, lbd, 1.0, ALU.add)
        P.op("dve", lambda: nc.vector.reciprocal(oml.ap, lbd.ap), [lbd], [oml])
        P.ts("dve", noml, oml, -1.0, ALU.mult)
        vtm = P.sb(ea, "vtm", [128, 4, 512], BF16)
        ktm = [P.sb(ea, "ktm%d" % i, [128, 512], F32) for i in range(4)]
        vbtm = [P.sb(ea, "vbtm%d" % i, [128, 512], F32) for i in range(2)]
        kTs = [P.sb(ea, "kTs%d" % i, [128, 512], F32) for i in range(2)]
        hqT = [P.sb(ea, "hqT%d" % h, [128, TB], F32) for h in range(HA)]
        he = P.sb(ea, "he", [128, TB], F32)
        hr = P.sb(ea, "hr", [128, TB], F32)
        hlf = P.sb(ea, "hlf", [128, TB], F32)
        hb = P.sb(ea, "hb", [128, TB], F32)
        hebn = P.sb(ea, "hebn", [128, TB], F32)
        qt2 = [P.sb(ea, "qt%d" % i, [128, TB], BF16) for i in range(2)]
        kt2 = [P.sb(ea, "kt%d" % i, [128, TB], BF16) for i in range(2)]
        kdT2 = [P.sb(ea, "kdT%d" % i, [128, TB], F32) for i in range(2)]
        heb2 = [P.sb(ea, "heb%d" % i, [128, TB], F32) for i in range(2)]
        kd_tm = P.sb(ea, "kd_tm", [128, 4, 128], BF16)
        AT = P.sb(ea, "AT", [128, TB], BF16)
        Sst = [[P.sb(ea, "S%d_%d" % (h, i), [128, 128], F32) for i in range(2)] for h in range(HA)]
        Ssm = [[P.sb(ea, "Ss%d_%d" % (h, i), [128, 128], F32) for i in range(2)] for h in range(HA)]
        Sb = [P.sb(ea, "Sb%d" % i, [128, 128], BF16) for i in range(2)]
        sb_i = [0]
        oTown = P.sb(ea, "oTown", [128, HA, 128], F32)
        x1own = P.sb(ea, "x1own", [128, D], F32)
        x1Town = P.sb(ea, "x1Town", [128, KC, TB], BF16)
        pj = [P.sb(ea, "pj%d" % i, [128, TB], BF16) for i in range(2)]
        pj_i = [0]
        for h in range(HA):
            P.memset("pool", Sst[h][0], 0.0)

        def kT_path(src_tm_list, dst_scr, col0, nblk):
            for oc in range(4):
                b = bank(4, 8)
                for i, src in enumerate(src_tm_list):
                    P.tr(b[:, i * 128:(i + 1) * 128], src[:, oc * 128:(oc + 1) * 128], ident, sig=(i == nblk - 1))
                kk = kTs[oc % 2]
                evac(kk[:, 0:nblk * 128], b[:, 0:nblk * 128])
                P.dma("pool", dst_scr[2 * oc:2 * oc + 2, :, col0:col0 + nblk * 128].re("h d s -> (h d) s"),
                      kk[:, 0:nblk * 128])

        def group_tail(gi, ntc, sample):
            NT = ntc * 128
            S = Ssm if sample else Sst
            ktl = []
            halves = [list(range(ntc))] if ntc < 4 else [[0, 1], [2, 3]]
            for c in [cc for hv in halves for cc in ["T%d" % halves.index(hv)] + hv]:
                if isinstance(c, str):
                    hv = halves[int(c[1:])]
                    for k in range(KC):
                        bT = bank()
                        for ii, cc in enumerate(hv):
                            P.tr(bT[:, ii * 128:(ii + 1) * 128], x1_tm[cc][:, k * 128:(k + 1) * 128], ident, sig=(ii == len(hv) - 1))
                        evac(x1T_all[:, k, hv[0] * 128:(hv[-1] + 1) * 128], bT[:, 0:len(hv) * 128])
                    continue
                csl = slice(c * 128, (c + 1) * 128)
                b = bank()
                for k in range(KC):
                    P.mm(b, x1T_all[:, k, csl], wia[:, k, :], start=(k == 0), stop=(k == KC - 1), sig=(k == KC - 1))
                evac(vtm[:, c, :], b)
                b = bank()
                for k in range(KC):
                    P.mm(b, x1T_all[:, k, csl], wkb[:, k, :], start=(k == 0), stop=(k == KC - 1), sig=(k == KC - 1))
                kk = ktm[c % 4]
                ktl.append(kk)
                evac(kk, b)
                b = bank()
                for k in range(KC):
                    P.mm(b, x1T_all[:, k, csl], wvb[:, k, :], start=(k == 0), stop=(k == KC - 1), sig=(k == KC - 1))
                vv = vbtm[c % 2]
                evac(vv, b)
                if sample:
                    P.dma("pool", ks, kk[0:DEC, :])
                    P.dma("pool", vs, vv[0:DEC, :])
                    P.dma("pool", Vs_scr[:, :, NPB, :].re("h p d -> p h d"), vv.re("p (h d) -> p h d", d=DH))
                    kT_path([kk], KTs_scr, PAST, 1)
                else:
                    r0 = gi * TB + c * 128
                    P.dma("pool", k_all[r0:r0 + 128, :], kk)
                    P.dma("pool", v_all[r0:r0 + 128, :], vv)
                    P.dma("pool", V_scr[:, :, gi * 4 + c, :].re("h p d -> p h d"), vv.re("p (h d) -> p h d", d=DH))
            if not sample:
                kT_path(ktl, KT_scr, gi * TB, ntc)
            for h in range(HA):
                tq = ring_load(ws["qa"][h], KC)
                b = bank()
                for k in range(KC):
                    P.mm(b[:, 0:NT], tq[:, k, :], x1T_all[:, k, 0:NT], start=(k == 0), stop=(k == KC - 1), sig=(k == KC - 1))
                P.act(hqT[h][:, 0:NT], b[:, 0:NT], AF.Silu)
            chunks = [(0, DEC)] if sample else [(c * 64, 64) for c in range(NT // 64)]
            N_ = slice(0, NT)

            def hg_front(h):
                hs = h % 2
                tf = ring_load(ws["fa"][h], KC)
                zf = bank()
                for k in range(KC):
                    P.mm(zf[:, 0:NT], tf[:, k, :], x1T_all[:, k, 0:NT], start=(k == 0), stop=(k == KC - 1), sig=(k == KC - 1))
                N_ = slice(0, NT)
                P.act(he[:, N_], zf[:, N_], AF.Exp)
                yield
                P.act(he[:, N_], he[:, N_], AF.Ln, bias=1.0)
                P.act(hr[:, N_], he[:, N_], AF.Exp, scale=-1.0)
                P.act(hlf[:, N_], hr[:, N_], AF.Ln, bias=1.0, scale=noml[:, h:h + 1])
                yield
                P.op("dve", lambda: nc.vector.tensor_tensor_scan(hb.ap[:, N_], mask64.ap[:, N_], hlf.ap[:, N_], 0.0,
                                                                  ALU.mult, ALU.add), [mask64, hlf], [hb])
                yield
                P.act(heb2[hs][:, N_], hb[:, N_], AF.Exp)
                P.act(hebn[:, N_], hb[:, N_], AF.Exp, scale=-1.0)
                yield
                P.tt("dve", qt2[hs][:, N_], hqT[h][:, N_], heb2[hs][:, N_], ALU.mult)
                P.stt(kt2[hs][:, N_], hr[:, N_], oml[:, h:h + 1], hebn[:, N_], ALU.mult, ALU.mult)
                if sample:
                    P.memset("pool", kdT2[hs][:, DEC:128], 0.0)
                    P.ts("dve", kdT2[hs][:, 0:DEC], kt2[hs][:, 0:DEC], heb2[hs][:, DEC - 1:DEC], ALU.mult)
                else:
                    nch = NT // 64
                    lastb = T(heb2[hs].buf, heb2[hs].ap[:, N_].rearrange("p (c t) -> p c t", t=64)[:, :, 63:64].to_broadcast([128, nch, 64]))
                    P.tt("dve", kdT2[hs][:, N_].re("p (c t) -> p c t", t=64), kt2[hs][:, N_].re("p (c t) -> p c t", t=64), lastb, ALU.mult)
                yield

            def hg_back(h):
                hs = h % 2
                b = bank()
                for c in range(ntc):
                    P.tr(b[:, c * 128:(c + 1) * 128], kdT2[hs][:, c * 128:(c + 1) * 128], ident, sig=(c == ntc - 1))
                evac(kd_tm[:, 0:ntc, :], b[:, 0:NT].re("p (c k) -> p c k", k=128))
                yield
                b = bank()
                for c in range(ntc):
                    csl = slice(c * 128, (c + 1) * 128)
                    P.mm(b[:, csl], kt2[hs][:, csl], qt2[hs][:, csl], start=True, stop=True, sig=(c == ntc - 1))
                P.tt("dve", AT[:, N_], b[:, N_], mask2[:, N_], ALU.mult)
                yield
                pb = [PS[6], PS[7]]
                for ci, (c0, cl) in enumerate(chunks):
                    prt = slice(c0 % 128, c0 % 128 + cl)
                    P.mm(pb[ci % 2][:, (ci // 2) * 128:(ci // 2 + 1) * 128], kd_tm[prt, c0 // 128, :],
                         vtm[prt, c0 // 128, h * 128:(h + 1) * 128], start=True, stop=True)
                ob = PS[4 + h % 2]
                for c in range(ntc):
                    csl = slice(c * 128, (c + 1) * 128)
                    P.mm(ob[:, csl], vtm[:, c, h * 128:(h + 1) * 128], AT[:, csl], start=(c == 0), stop=False)
                cur = 0 if not sample else 0
                for ci, (c0, cl) in enumerate(chunks):
                    s_in = S[h][cur_state[(sample, h)]]
                    s_out = S[h][1 - cur_state[(sample, h)]]
                    sbf = Sb[sb_i[0] % 2]
                    sb_i[0] += 1
                    P.cp("act", sbf, s_in)
                    cw = 64 if not sample else 64
                    P.mm(ob[:, c0:c0 + cw], sbf, qt2[hs][:, c0:c0 + cw], start=False, stop=(ci == len(chunks) - 1))
                    P.stt(s_out, s_in, heb2[hs][:, c0 + cl - 1:c0 + cl], pb[ci % 2][:, (ci // 2) * 128:(ci // 2 + 1) * 128],
                          ALU.mult, ALU.add)
                    cur_state[(sample, h)] = 1 - cur_state[(sample, h)]
                    yield
                if sample:
                    P.cp("dve", oTown[:, h, :], ob[:, 0:128])
                else:
                    P.ts("dve", oTown[:, h, :], ob[:, 0:128], flag[:, 0:1], ALU.mult)
                    for r in range(1, 4):
                        P.stt(oTown[:, h, :], ob[:, r * 128:(r + 1) * 128], flag[:, r:r + 1], oTown[:, h, :], ALU.mult, ALU.add)
                yield

            def run2(ga, gb):
                la, lb = ga is not None, gb is not None
                while la or lb:
                    if lb:
                        try:
                            next(gb)
                        except StopIteration:
                            lb = False
                    if la:
                        try:
                            next(ga)
                        except StopIteration:
                            la = False

            run2(hg_front(0), None)
            for h in range(HA):
                run2(hg_front(h + 1) if h + 1 < HA else None, hg_back(h))
            if sample:
                P.dma("pool", OTs_scr.re("h v t -> v h t"), oTown)
                P.dma("pool", X1s_scr, x1_tm[0])
                P.cp("pool", x1Town[:, :, 0:128], x1T_all[:, :, 0:128])
                own_proj(0, 128, True)
            elif 'nosel' in KNOB:
                pass
            else:
                P.dma("pool", OT_scr[:, :, gi * 128:(gi + 1) * 128].re("h v t -> v h t"), oTown)
                P.ts("dve", x1own, x1_tm[0], flag[:, 0:1], ALU.mult)
                for r in range(1, 4):
                    P.stt(x1own, x1_tm[r], flag[:, r:r + 1], x1own, ALU.mult, ALU.add)
                P.dma("pool", X1_scr[gi * 128:(gi + 1) * 128, :], x1own)
                dst = x1Town[:, :, (gi % 4) * 128:(gi % 4 + 1) * 128]
                P.ts("dve", dst, x1T_all[:, :, 0:128], flag[:, 0:1], ALU.mult)
                for r in range(1, 4):
                    P.stt(dst, x1T_all[:, :, r * 128:(r + 1) * 128], flag[:, r:r + 1], dst, ALU.mult, ALU.add)
                if gi % 4 == 3:
                    own_proj((gi // 4) * 512, 512, False)

        cur_state = {(s_, h): 0 for s_ in (False, True) for h in range(HA)}

        def own_proj(t0, N, sample):
            QT, GA, GT = (QTs_scr, GAs_scr, GTs_scr) if sample else (QT_scr, GA_scr, GT_scr)
            for (nm, noc) in (("qb", 4), ("ga", 4), ("gta", 8), ("gtb", 8)):
                for oc in range(noc):
                    tw = ring_load(ws[nm][oc], KC)
                    b = bank()
                    for k in range(KC):
                        P.mm(b[:, 0:N], tw[:, k, :], x1Town[:, k, 0:N], start=(k == 0), stop=(k == KC - 1), sig=(k == KC - 1))
                    o = pj[pj_i[0] % 2]
                    pj_i[0] += 1
                    if nm == "qb":
                        P.act(o[:, 0:N], b[:, 0:N], AF.Copy, scale=0.125)
                        P.dma("pool", QT[2 * oc:2 * oc + 2, :, t0:t0 + N].re("h d s -> (h d) s"), o[:, 0:N])
                    elif nm == "ga":
                        P.act(o[:, 0:N], b[:, 0:N], AF.Silu)
                        P.dma("pool", GA[oc, :, t0:t0 + N], o[:, 0:N])
                    else:
                        P.act(o[:, 0:N], b[:, 0:N], AF.Tanh, scale=0.5)
                        P.dma("pool", GT[oc + (8 if nm == "gtb" else 0), :, t0:t0 + N], o[:, 0:N])

        P.memset("pool", x_tm[0], 0.0)
        P.dma("sp", x_tm[0][0:DEC, :], xs)
        for h in range(HA):
            P.dma("sp", Ssm[h][0], st0[h])
        transpose_to_fm(x_tm, xT_all, 1)
        ffn_ln("ffn1", x_tm, xT_all, hT, sg, x1_tm, ln1g, ln1b, small, 1)
        def early_exit():
            P.barrier()
            ea.close()
            P.barrier()
            es.close()
            return nc, P
        if stop_after == "s_ffn":
            P.dma("pool", dbg[0:128, :], x1_tm[0])
            return early_exit()
        group_tail(0, 1, True)
        for h in range(HA):
            P.dma("pool", ss[h], Ssm[h][cur_state[(True, h)]])
        if stop_after == "s_tail":
            return early_exit()
        for b0 in range(0, NPB, 4):
            tiles = []
            nb = min(4, NPB - b0)
            for i in range(nb):
                P.dma("sp", x1_tm[i][:, 0:512], ck[(b0 + i) * 128:(b0 + i + 1) * 128, :])
                tiles.append(x1_tm[i][:, 0:512])
            kT_path(tiles, KTs_scr, b0 * 128, nb)

        if stop_after == "past":
            return early_exit()
        for gi in range(NG):
            if stop_after == "A1" and gi == 1:
                return early_exit()
            for c in range(4):
                P.dma("sp", x_tm[c], xb[gi * TB + c * 128: gi * TB + (c + 1) * 128, :])
            transpose_to_fm(x_tm, xT_all, 4)
            ffn_ln("ffn1", x_tm, xT_all, hT, sg, x1_tm, ln1g, ln1b, small, 4)
            if stop_after == "ffn1":
                for c in range(4):
                    P.dma("pool", dbg[gi * TB + c * 128: gi * TB + (c + 1) * 128, :], x1_tm[c])
                continue
            group_tail(gi, 4, False)
        for h in range(HA):
            P.dma("pool", s_fin[h], Sst[h][cur_state[(False, h)]])
        P.barrier()
    if stop_after in ("ffn1", "A"):
        P.barrier()
        es.close()
        return nc, P

    with ExitStack() as eb_:
        uincl = P.sb(eb_, "uincl", [128, 128], BF16)
        P.dma("pool", uincl, c_uincl)
        onesb = P.sb(eb_, "onesb", [128, 128], BF16)
        P.memset("pool", onesb, 1.0)
        sbm = P.sb(eb_, "sbm", [128, 16, 512], BF16)
        for m0 in range(0, 16, 4):
            P.dma("pool", sbm[:, m0:m0 + 4, :], c_sbmask[m0:m0 + 4].re("m p t -> p m t"))
        smk = P.sb(eb_, "smk", [128, DEC], BF16)
        P.dma("pool", smk, c_smask)
        KTh = [P.sb(eb_, "KTh%d" % i, [DH, max(SEQ, SKS)], BF16) for i in range(2)]
        Vh = [P.sb(eb_, "Vh%d" % i, [128, max(NKB, NPB + 1), DH], BF16) for i in range(2)]
        QTh = [P.sb(eb_, "QTh%d" % i, [DH, max(NOWN, 128)], BF16) for i in range(2)]
        E = [P.sb(eb_, "sbE%d" % i, [128, 128], F32) for i in range(4)]
        SPt = [P.sb(eb_, "sbSP%d" % i, [128, 128], BF16) for i in range(4)]
        SPm = [P.sb(eb_, "sbSPm%d" % i, [128, 128], BF16) for i in range(4)]
        Em = [P.sb(eb_, "sbEm%d" % i, [128, 128], F32) for i in range(4)]
        E2 = [P.sb(eb_, "sbE2%d" % i, [128, 128], F32) for i in range(2)]
        Wt = [P.sb(eb_, "sbW%d" % i, [128, 128], BF16) for i in range(2)]
        R32 = [P.sb(eb_, "sbR32_%d" % i, [128, 128], F32) for i in range(2)]
        Rb = [P.sb(eb_, "sbRb%d" % i, [128, 128], BF16) for i in range(4)]
        hbo = [P.sb(eb_, "hbo%d" % i, [DH, 512], BF16) for i in range(2)]
        for i in range(2):
            P.memset("pool", hbo[i], 0.0)
        ti = [0]
        oi = [0]

        def sb_attend(KT, V, QT, nq, kblocks, dst):
            Q_ = slice(0, nq)
            n = len(kblocks)
            outb = PS[6 + oi[0] % 2]
            ho = hbo[oi[0] % 2]
            oi[0] += 1
            base = ti[0]
            ti[0] += n
            spm_of, em_of = {}, {}

            def stA(i):
                kb = kblocks[i][0]
                P.mm(PS[(base + i) % 3][:, Q_], KT[:, kb * 128:(kb + 1) * 128], QT[:, Q_])

            def stB(i):
                mk = kblocks[i][1]
                j = base + i
                e = E[j % 4]
                P.act(e[:, Q_], PS[j % 3][:, Q_], AF.Exp)
                sp = SPt[j % 4]
                P.act(sp[:, Q_], e[:, Q_], AF.Ln, bias=1.0)
                if mk is not None:
                    spm = SPm[j % 4]
                    P.tt("dve", spm[:, Q_], sp[:, Q_], mk, ALU.mult)
                    em = Em[j % 4]
                    P.tt("dve", em[:, Q_], e[:, Q_], mk, ALU.mult)
                else:
                    spm, em = sp, e
                spm_of[i], em_of[i] = spm, em

            def stC(i):
                c = PS[3 + (base + i) % 3]
                ops = [(uincl, spm_of[i])]
                if i >= 1:
                    ops.append((onesb, spm_of[i - 1]))
                if i >= 2:
                    ops.append((onesb, Rb[(base + i - 2) % 4]))
                for oi_, (l, r_) in enumerate(ops):
                    P.mm(c[:, Q_], l, r_[:, Q_], start=(oi_ == 0), stop=(oi_ == len(ops) - 1), sig=(oi_ == len(ops) - 1))

            def stR(i):
                if i + 2 > n - 1:
                    return
                if i == 0:
                    P.cp("pool", R32[(base + i) % 2][:, Q_], spm_of[0][:, Q_])
                else:
                    P.tt("pool", R32[(base + i) % 2][:, Q_], R32[(base + i - 1) % 2][:, Q_], spm_of[i][:, Q_], ALU.add)

            def stRc(i):
                if i + 2 > n - 1:
                    return
                P.cp("dve", Rb[(base + i) % 4][:, Q_], R32[(base + i) % 2][:, Q_])

            def stD(i):
                j = base + i
                P.act(E2[j % 2][:, Q_], PS[3 + j % 3][:, Q_], AF.Exp, scale=-1.0)

            def stE(i):
                j = base + i
                P.tt("dve", Wt[j % 2][:, Q_], em_of[i][:, Q_], E2[j % 2][:, Q_], ALU.mult)

            def stF(i):
                kb = kblocks[i][0]
                P.mm(outb[0:DH, Q_], V[:, kb, :], Wt[(base + i) % 2][:, Q_], start=(i == 0), stop=(i == n - 1))

            for t in range(-3, n + 2):
                for fn, off in ((stA, 3), (stB, 2), (stRc, -1), (stC, 1), (stR, 0), (stD, 0), (stE, -1), (stF, -1)):
                    i = t + off
                    if 0 <= i < n:
                        fn(i)
            P.cp("act", ho[:, Q_], outb[0:DH, Q_])
            P.dma("pool", dst, ho[:, 0:max(nq, 128)])

        ZP = [T(Buf("zp%d" % m), psum[:, 2 * m * 512:(2 * m + 2) * 512]) for m in range(3)]
        E2p = [P.sb(eb_, "pE%d" % i, [128, 1024], F32) for i in range(2)]
        SP2 = [P.sb(eb_, "pSP%d" % i, [128, 1024], BF16) for i in range(3)]
        S2 = [P.sb(eb_, "pS2%d" % i, [128, 512], BF16) for i in range(4)]
        W2 = [P.sb(eb_, "pW%d" % i, [128, 1024], BF16) for i in range(3)]
        Rb2 = [P.sb(eb_, "pRb%d" % i, [128, 512], BF16) for i in range(3)]
        R2 = [P.sb(eb_, "pR%d" % i, [128, 512], F32) for i in range(2)]
        qn = [P.sb(eb_, "qn%d" % i, [DH, 512], BF16) for i in range(2)]
        pi_ = [0]

        def sb_pipeline(jobs):
            H = (slice(0, 512), slice(512, 1024))
            look = []
            for jn, jb in enumerate(jobs):
                jb["np"] = len(jb["kblocks"]) // 2
                jb["outb"] = PS[6 + jn % 2]
                jb["ho"] = hbo[jn % 2]
                jb["qneg"] = qn[jn % 2]
                for i in range(jb["np"]):
                    look.append((jb, i))
            NP = len(look)

            def kbs(jb, i):
                return jb["kblocks"][2 * i], jb["kblocks"][2 * i + 1]

            def stA(G):
                jb, i = look[G]
                if i == 0:
                    if jb.get("pre"):
                        jb["pre"]()
                    P.ts("dve", jb["qneg"], jb["QT"], -1.0, ALU.mult)
                z = ZP[G % 3]
                for hh, (kb, mk) in enumerate(kbs(jb, i)):
                    P.mm(z[:, H[hh]], jb["KT"][:, kb * 128:(kb + 1) * 128], jb["QT"])

            def stB(G):
                jb, i = look[G]
                z, e, sp = ZP[G % 3], E2p[G % 2], SP2[G % 3]
                P.act(e, z, AF.Exp)
                P.act(sp, e, AF.Ln, bias=1.0)
                for hh, (kb, mk) in enumerate(kbs(jb, i)):
                    if mk is not None:
                        P.tt("dve", sp[:, H[hh]], sp[:, H[hh]], mk, ALU.mult)
                if i + 1 <= jb["np"] - 1:
                    P.tt("dve", S2[G % 4], sp[:, H[0]], sp[:, H[1]], ALU.add)

            def stRc(G):
                jb, i = look[G]
                if i + 2 > jb["np"] - 1:
                    return
                P.cp("dve", Rb2[G % 3], R2[G % 2])

            def stC(G):
                jb, i = look[G]
                z, sp = ZP[G % 3], SP2[G % 3]
                for hh, (kb, mk) in enumerate(kbs(jb, i)):
                    ops = [(jb["KT"][:, kb * 128:(kb + 1) * 128], jb["qneg"]), (uincl, sp[:, H[hh]])]
                    if hh == 1:
                        ops.append((onesb, sp[:, H[0]]))
                    if i >= 1:
                        ops.append((onesb, S2[(G - 1) % 4]))
                    if i >= 2:
                        ops.append((onesb, Rb2[(G - 2) % 3]))
                    for oi_, (l, r_) in enumerate(ops):
                        P.mm(z[:, H[hh]], l, r_, start=(oi_ == 0), stop=(oi_ == len(ops) - 1), sig=(oi_ == len(ops) - 1))

            def stR(G):
                jb, i = look[G]
                if i + 2 > jb["np"] - 1:
                    return
                if i == 0:
                    P.cp("pool", R2[G % 2], S2[G % 4])
                else:
                    P.tt("pool", R2[G % 2], R2[(G - 1) % 2], S2[G % 4], ALU.add)

            def stD(G):
                jb, i = look[G]
                w = W2[G % 3]
                P.act(w, ZP[G % 3], AF.Exp, scale=-1.0)
                for hh, (kb, mk) in enumerate(kbs(jb, i)):
                    if mk is not None:
                        P.tt("dve", w[:, H[hh]], w[:, H[hh]], mk, ALU.mult)

            def stF(G):
                jb, i = look[G]
                w = W2[G % 3]
                for hh, (kb, mk) in enumerate(kbs(jb, i)):
                    P.mm(jb["outb"][0:DH, :], jb["V"][:, kb, :], w[:, H[hh]], start=(i == 0 and hh == 0),
                         stop=(i == jb["np"] - 1 and hh == 1))
                if i == jb["np"] - 1:
                    P.cp("act", jb["ho"], jb["outb"][0:DH, :])
                    P.dma("pool", jb["dst"], jb["ho"])

            for t in range(-2, NP + 1):
                for fn, off in ((stA, 2), (stB, 1), (stRc, -1), (stC, 1), (stR, 0), (stD, 0), (stF, -1)):
                    G = t + off
                    if 0 <= G < NP:
                        fn(G)

        hi = [0]
        for h in range(HB):
            kt_, v_, q_ = KTh[hi[0] % 2], Vh[hi[0] % 2], QTh[hi[0] % 2]
            hi[0] += 1
            P.dma("sp", kt_[:, 0:SKS], KTs_scr[h])
            P.dma("sp", v_[:, 0:NPB + 1, :], Vs_scr[h])
            P.dma("sp", q_[:, 0:128], QTs_scr[h])
            kbl = [(NPB, smk)] + [(kb, None) for kb in range(NPB - 1, -1, -1)]
            sb_attend(kt_, v_, q_, DEC, kbl, HBs_scr[h, :, 0:128])
        P.barrier()
        if stop_after != "Bs":
            jobs = []
            for h in range(HB):
                kt_, v_, q_ = KTh[hi[0] % 2], Vh[hi[0] % 2], QTh[hi[0] % 2]
                hi[0] += 1

                def pre(h=h, kt_=kt_, v_=v_, q_=q_):
                    P.dma("sp", kt_[:, 0:SEQ], KT_scr[h])
                    P.dma("sp", v_[:, 0:NKB, :], V_scr[h])
                    P.dma("sp", q_[:, 0:NOWN], QT_scr[h])
                for M in range(NSG):
                    kbl = [(kb, (sbm[:, kb - 16 * M, :] if kb >= 16 * M else None)) for kb in range(16 * M + 15, -1, -1)]
                    jobs.append(dict(pre=(pre if M == 0 else None), KT=kt_, V=v_, QT=q_[:, M * 512:(M + 1) * 512],
                                     kblocks=kbl, dst=HB_scr[h, :, M * 512:(M + 1) * 512]))
            sb_pipeline(jobs)
        P.barrier()
    if stop_after in ("B", "Bs"):
        P.barrier()
        es.close()
        return nc, P

    with ExitStack() as ec:
        x1o = [P.sb(ec, "x1o%d" % c, [128, D], F32) for c in range(4)]
        x2_tm = [P.sb(ec, "x2_tm%d" % c, [128, D], F32) for c in range(4)]
        y_tm = [P.sb(ec, "y_tm%d" % c, [128, D], F32) for c in range(4)]
        xT_all = P.sb(ec, "xT2", [128, KC, TB], BF16)
        hT_all = P.sb(ec, "hT2", [128, FC, TB], BF16)
        hT = [hT_all.sub((slice(None), f, slice(None)), "hT2_%d" % f) for f in range(FC)]
        sg = [P.sb(ec, "sg2_%d" % i, [128, TB], F32) for i in range(2)]
        ln2g = bcast_load(ec, "ln2g", Wd["ln2_g"], D)
        ln2b = bcast_load(ec, "ln2b", Wd["ln2_b"], D)
        ln3g = bcast_load(ec, "ln3g", Wd["ln3_g"], D)
        ln3b = bcast_load(ec, "ln3b", Wd["ln3_b"], D)
        small = mk_small(ec)
        mhalf = small[4]
        wo = [P.sb(ec, "wo%d" % i, [128, KC, 512], BF16) for i in range(2)]
        for i in range(2):
            P.dma("sp", wo[i], ws["wo"][i])
        gn = P.sb(ec, "gn", [128, HA], F32)
        P.dma("sp", gn, Wd["hgrn_norm_g"].re("o (h p) -> p (o h)", p=128), allow_slow_non_contiguous=True)
        onesf = P.sb(ec, "onesf", [128, 128], F32)
        P.memset("pool", onesf, 1.0 / 128.0)
        oT4 = P.sb(ec, "oT4", [128, HA, TB], F32)
        ga4 = P.sb(ec, "ga4", [128, HA, TB], BF16)
        gt16 = P.sb(ec, "gt16", [128, 16, TB], BF16)
        hb8 = P.sb(ec, "hb8", [DH, HB, TB], BF16)
        sq = P.sb(ec, "sq", [128, TB], F32)
        t1 = P.sb(ec, "t1", [128, TB], F32)
        rs = P.sb(ec, "rs", [128, TB], F32)
        haT = P.sb(ec, "haT", [128, HA, TB], BF16)
        m1 = P.sb(ec, "m1", [128, TB], F32)
        m2 = P.sb(ec, "m2", [128, TB], F32)
        mergedT = P.sb(ec, "mergedT", [128, KC, TB], BF16)

        def own_tail(t0, ntc, sample):
            N = ntc * 128
            N_ = slice(0, N)
            OTs, GAs, GTs, HBs, X1s = (OTs_scr, GAs_scr, GTs_scr, HBs_scr, X1s_scr) if sample else \
                (OT_scr, GA_scr, GT_scr, HB_scr, X1_scr)
            P.dma("sp", oT4[:, :, N_], OTs[:, :, t0:t0 + N].re("h v t -> v h t"))
            P.dma("sp", ga4[:, :, N_], GAs[:, :, t0:t0 + N].re("h p t -> p h t"))
            P.dma("sp", gt16[:, :, N_], GTs[:, :, t0:t0 + N].re("h p t -> p h t"))
            P.dma("sp", hb8[:, :, N_], HBs[:, :, t0:t0 + N].re("h d t -> d h t"))
            for c in range(ntc):
                P.dma("sp", x1o[c], X1s[t0 + c * 128:t0 + (c + 1) * 128, :])
            for h in range(HA):
                P.act(sq[:, N_], oT4[:, h, N_], AF.Square)
                b = bank()
                P.mm(b[:, N_], onesf, sq[:, N_])
                P.ts("dve", t1[:, N_], b[:, N_], RMS_EPS, ALU.add)
                P.act(rs[:, N_], t1[:, N_], AF.Ln)
                P.act(rs[:, N_], rs[:, N_], AF.Exp, scale=-0.5)
                P.tt("dve", t1[:, N_], oT4[:, h, N_], rs[:, N_], ALU.mult)
                P.stt(haT[:, h, N_], t1[:, N_], gn[:, h:h + 1], ga4[:, h, N_], ALU.mult, ALU.mult)
            for oc in range(KC):
                wa = ring_load(ws["wa"][oc], 4)
                pa = bank()
                for kc in range(4):
                    P.mm(pa[:, N_], wa[:, kc, :], haT[:, kc, N_], start=(kc == 0), stop=(kc == 3), sig=(kc == 3))
                wb = ring_load(ws["wb"][oc], 8, npart=DH)
                pb_ = bank()
                for h in range(HB):
                    P.mm(pb_[:, N_], wb[:, h, :], hb8[:, h, N_], start=(h == 0), stop=(h == HB - 1), sig=(h == HB - 1))
                P.stt(m1[:, N_], gt16[:, oc, N_], 1.0, pa[:, N_], ALU.add, ALU.mult)
                P.stt(m2[:, N_], gt16[:, 8 + oc, N_], 1.0, pb_[:, N_], ALU.add, ALU.mult)
                P.tt("dve", mergedT[:, oc, N_], m1[:, N_], m2[:, N_], ALU.add)
            for c in range(ntc):
                o = x2_tm[c]
                for hf in range(2):
                    acc = PS[4 + (c % 2) * 2 + hf]
                    for k in range(KC):
                        P.mm(acc, mergedT[:, k, c * 128:(c + 1) * 128], wo[hf][:, k, :], start=(k == 0), stop=(k == KC - 1),
                             sig=(k == KC - 1))
                    P.stt(o[:, hf * 512:(hf + 1) * 512], acc, 0.5 / ALPHA, x1o[c][:, hf * 512:(hf + 1) * 512], ALU.mult, ALU.add)
                layer_norm(o, ln2g, ln2b, small, LN_EPS / (ALPHA * ALPHA))
            transpose_to_fm(x2_tm, xT_all, ntc)
            ffn_ln("ffn2", x2_tm, xT_all, hT, sg, y_tm, ln3g, ln3b, small, ntc)
            for c in range(ntc):
                if sample:
                    P.dma("pool", ys, y_tm[0][0:DEC, :])
                else:
                    P.dma("pool", y_own[t0 + c * 128:t0 + (c + 1) * 128, :], y_tm[c])

        own_tail(0, 1, True)
        if stop_after != "Cs":
            for M in range(NSG):
                own_tail(M * 512, 4, False)
        P.barrier()
    P.barrier()
    es.close()
    return nc, P


def make_consts(g):
    c = {}
    c["c_ident"] = np.eye(128, dtype=np.float32)
    t = np.arange(512)
    c["c_mask64"] = np.broadcast_to((t % 64 != 0).astype(np.float32), (128, 512)).copy()
    s = np.arange(128)[:, None]
    tt = np.arange(128)[None, :]
    m2 = ((s // 64 == tt // 64) & (s <= tt)).astype(np.float32)
    c["c_mask2"] = np.tile(m2, (1, 4))
    c["c_uincl"] = (np.arange(128)[:, None] >= np.arange(128)[None, :]).astype(np.float32)
    sb = np.zeros((16, 128, 512), np.float32)
    for mrel in range(16):
        for j in range(4):
            kpos = 128 * mrel + np.arange(128)[:, None]
            qpos = 128 * (4 * j + g) + np.arange(128)[None, :]
            sb[mrel, :, j * 128:(j + 1) * 128] = (kpos < qpos)
    c["c_sbmask"] = sb
    c["c_smask"] = ((np.arange(128)[:, None] < np.arange(DEC)[None, :])).astype(np.float32)
    fl = np.zeros((128, 4), np.float32)
    fl[:, g] = 1.0
    c["c_flag"] = fl
    return c


def make_in_maps(inp, SEQ, PAST):
    maps = []
    f = lambda a: np.ascontiguousarray(np.asarray(a, dtype=np.float32))
    wts = {k: f(inp[k]).reshape(W_SHAPES[k]) for k in W_SHAPES}
    for c in range(8):
        b, g = c // 4, c % 4
        m = dict(wts)
        m["xb"] = f(inp["x_prompt"][b])
        m["xs"] = f(inp["x_sample"][c])
        m["ck"] = f(inp["cache_sb_k"][0, c]).reshape(PAST, 512)
        m["cv"] = f(inp["cache_sb_v"][0, c]).reshape(PAST, 512)
        m["st0"] = f(inp["state_hgrn"][0, c])
        m.update(make_consts(g))
        maps.append(m)
    return maps


_CACHE = {}


def kernel(**inputs):
    SEQ = inputs["x_prompt"].shape[1]
    PAST = inputs["cache_sb_k"].shape[2]
    key = (SEQ, PAST)
    if key not in _CACHE:
        _CACHE[key] = build(SEQ, PAST)[0]
    nc = _CACHE[key]
    maps = make_in_maps(inputs, SEQ, PAST)
    res = run_bass_kernel_spmd(nc, maps, core_ids=list(range(8)))
    return assemble(res.results, SEQ)


def assemble(r, SEQ):
    NB128 = SEQ // 128
    y_prompt = np.zeros((2, SEQ, D), np.float32)
    for c in range(8):
        b, g = c // 4, c % 4
        yo = np.asarray(r[c]["y_own"]).reshape(NB128 // 4, 128, D)
        y_prompt[b].reshape(NB128 // 4, 4, 128, D)[:, g] = yo
    y_sample = np.stack([np.asarray(r[c]["ys"]) for c in range(8)])
    k_p = np.stack([np.asarray(r[4 * b]["k_all"]).reshape(SEQ, HB, DH) for b in range(2)])[None]
    v_p = np.stack([np.asarray(r[4 * b]["v_all"]).reshape(SEQ, HB, DH) for b in range(2)])[None]
    s_p = np.stack([np.asarray(r[4 * b]["s_fin"]) for b in range(2)])[None]
    k_s = np.stack([np.asarray(r[c]["ks"]).reshape(DEC, HB, DH) for c in range(8)])[None]
    v_s = np.stack([np.asarray(r[c]["vs"]).reshape(DEC, HB, DH) for c in range(8)])[None]
    s_s = np.stack([np.asarray(r[c]["ss"]) for c in range(8)])[None]
    return (y_prompt, y_sample, k_p.astype(np.float32), v_p.astype(np.float32), s_p.astype(np.float32),
            k_s.astype(np.float32), v_s.astype(np.float32), s_s.astype(np.float32))
```

```python
import numpy as np
import os
from contextlib import ExitStack
KNOB = os.environ.get('KNOB', '')
import concourse.bass as bass
import concourse.mybir as mybir
from concourse.bass_utils import run_bass_kernel_spmd

F32 = mybir.dt.float32
BF16 = mybir.dt.bfloat16
AF = mybir.ActivationFunctionType
ALU = mybir.AluOpType

D = 1024
KC = 8
FF = 2816
FC = 22
HA = 4
HB = 8
DH = 64
NIN = 5632
ALPHA = 2.0 ** 0.25
LN_EPS = 1e-5
RMS_EPS = 1e-6
TB = 512
DEC = 16


class Buf:
    __slots__ = ("name", "w", "r", "sem", "accum")

    def __init__(self, name, accum=False):
        self.name = name
        self.w = {}
        self.r = {}
        self.sem = None
        self.accum = accum


class T:
    __slots__ = ("buf", "ap")

    def __init__(self, buf, ap):
        self.buf = buf
        self.ap = ap

    def __getitem__(self, idx):
        return T(self.buf, self.ap[idx])

    def sub(self, idx, name):
        return T(Buf(name), self.ap[idx])

    def re(self, s, **kw):
        return T(self.buf, self.ap.rearrange(s, **kw))


class DSem:
    __slots__ = ("sem", "total", "id")

    def __init__(self, sem, i):
        self.sem = sem
        self.total = 0
        self.id = i


class Prog:
    ENGS = ("pe", "act", "dve", "pool", "sp")

    def __init__(self, nc, es):
        self.nc = nc
        self.es = es
        self.eng = {"pe": nc.tensor, "act": nc.scalar, "dve": nc.vector, "pool": nc.gpsimd, "sp": nc.sync}
        self.sem = {e: es.enter_context(nc.semaphore("sem_" + e)) for e in self.ENGS}
        self.cnt = {e: 0 for e in self.ENGS}
        self.seen = {e: {} for e in self.ENGS}
        self.dsems = []
        self.nwait = 0
        self.nops = 0

    def sb(self, es, name, shape, dtype):
        self.nalloc = getattr(self, "nalloc", 0) + 1
        name = "%s_%d" % (name, self.nalloc)
        nb = int(np.prod(shape[1:])) * (2 if dtype == BF16 else 4)
        self.sb_cur = getattr(self, "sb_cur", 0) + nb
        self.sb_peak = max(getattr(self, "sb_peak", 0), self.sb_cur)
        es.callback(self._free, nb)
        t = es.enter_context(self.nc.sbuf_tensor(name, list(shape), dtype))
        return T(Buf(name), t[:] if hasattr(t, "__getitem__") else t)

    def _free(self, nb):
        self.sb_cur -= nb

    def ps(self, es, name, shape, dtype=F32):
        t = es.enter_context(self.nc.psum_tensor(name, list(shape), dtype))
        return T(Buf(name), t[:])

    def dram(self, name, shape, dtype, kind="Internal", accum=False):
        t = self.nc.dram_tensor(name, list(shape), dtype, kind=kind)
        return T(Buf(name, accum=accum), t.ap())

    def dsem(self, buf, e):
        kind = "sw" if e == "pool" else "hw"
        if buf.sem is None:
            buf.sem = {}
        if kind not in buf.sem:
            s = self.es.enter_context(self.nc.semaphore("dsem%d" % len(self.dsems)))
            buf.sem[kind] = DSem(s, len(self.dsems))
            self.dsems.append(buf.sem[kind])
        return buf.sem[kind]

    def _wait(self, e, key, val):
        if key[0] == "c":
            sem = self.sem[key[1]]
            sid = key[1]
        else:
            sem = key[1].sem
            sid = key[1].id
            val = key[1].total
        if val <= 0:
            return
        if self.seen[e].get(sid, 0) >= val:
            return
        self.seen[e][sid] = val
        self.eng[e].wait_ge(sem, val)
        self.nwait += 1

    def _deps(self, e, reads, writes, is_dma):
        for b in reads:
            for key, val in list(b.w.items()):
                self._wait(e, key, val)
        for b in writes:
            if b.accum:
                continue
            for key, val in list(b.w.items()):
                if key == ("c", "pe") and e == "pe" and not is_dma:
                    continue
                self._wait(e, key, val)
            for key, val in list(b.r.items()):
                self._wait(e, key, val)

    def op(self, e, fn, reads, writes, sig=True):
        rb = [x.buf if isinstance(x, T) else x for x in reads]
        wb = [x.buf if isinstance(x, T) else x for x in writes]
        self._deps(e, rb, wb, False)
        ins = fn()
        if sig:
            self.cnt[e] += 1
            ins.then_inc(self.sem[e], 1)
            val = self.cnt[e]
        else:
            val = self.cnt[e] + 1
        key = ("c", e)
        for b in rb:
            b.r[key] = val
        for b in wb:
            if not b.accum:
                b.w = {key: val}
                b.r = {}
            else:
                b.w[key] = val
        self.nops += 1
        return ins

    def dma(self, e, out, in_, owner=None, **kw):
        ob, ib = out.buf, in_.buf
        if owner is None:
            owner = ib if ob.accum else ob
        ds = self.dsem(owner, e)
        self._deps(e, [ib], [ob], True)
        ins = self.eng[e].dma_start(out=out.ap, in_=in_.ap, **kw)
        ds.total += 16
        ins.then_inc(ds.sem, 16)
        key = ("d", ds)
        ib.r[key] = ds.total
        if ob.accum:
            ob.w[key] = ds.total
        else:
            ob.w = {key: ds.total}
            ob.r = {}
        self.nops += 1
        return ins

    def barrier(self):
        for e in self.ENGS:
            for o in self.ENGS:
                if o != e:
                    self._wait(e, ("c", o), self.cnt[o])
            for ds in self.dsems:
                self._wait(e, ("d", ds), ds.total)

    def mm(self, out, lhsT, rhs, start=True, stop=True, sig=True):
        return self.op("pe", lambda: self.nc.tensor.matmul(out.ap, lhsT.ap, rhs.ap, start=start, stop=stop),
                       [lhsT, rhs], [out], sig=sig)

    def tr(self, out, in_, ident, sig=True):
        return self.op("pe", lambda: self.nc.tensor.transpose(out.ap, in_.ap, ident.ap), [in_, ident], [out], sig=sig)

    def act(self, out, in_, func, bias=None, scale=None, extra_reads=()):
        kw = {}
        rd = [in_] + list(extra_reads)
        if bias is not None:
            if isinstance(bias, T):
                kw["bias"] = bias.ap
                rd.append(bias)
            else:
                kw["bias"] = bias
        if scale is not None:
            if isinstance(scale, T):
                kw["scale"] = scale.ap
                rd.append(scale)
            else:
                kw["scale"] = scale
        return self.op("act", lambda: self.nc.scalar.activation(out.ap, in_.ap, func, **kw), rd, [out])

    def _e(self, e):
        return self.eng[e]

    def tt(self, e, out, in0, in1, op):
        return self.op(e, lambda: self._e(e).tensor_tensor(out.ap, in0.ap, in1.ap, op), [in0, in1], [out])

    def ts(self, e, out, in0, s1, op0, s2=None, op1=None):
        rd = [in0]
        a1 = s1
        a2 = s2
        if isinstance(s1, T):
            rd.append(s1)
            a1 = s1.ap
        if isinstance(s2, T):
            rd.append(s2)
            a2 = s2.ap
        if op1 is None:
            return self.op(e, lambda: self._e(e).tensor_scalar(out.ap, in0.ap, a1, a2, op0), rd, [out])
        return self.op(e, lambda: self._e(e).tensor_scalar(out.ap, in0.ap, a1, a2, op0, op1), rd, [out])

    def stt(self, out, in0, scalar, in1, op0, op1):
        rd = [in0, in1]
        a = scalar
        if isinstance(scalar, T):
            rd.append(scalar)
            a = scalar.ap
        return self.op("dve", lambda: self.nc.vector.scalar_tensor_tensor(out.ap, in0.ap, a, in1.ap, op0, op1), rd, [out])

    def cp(self, e, out, in_):
        if e == "act":
            return self.op(e, lambda: self.nc.scalar.copy(out.ap, in_.ap), [in_], [out])
        return self.op(e, lambda: self._e(e).tensor_copy(out.ap, in_.ap), [in_], [out])

    def memset(self, e, out, val):
        return self.op(e, lambda: self._e(e).memset(out.ap, val), [], [out])


W_SHAPES = {
    "ffn1_wg": (D, FF), "ffn1_wu": (D, FF), "ffn1_wd": (FF, D), "ln1_g": (1, D), "ln1_b": (1, D),
    "w_in": (D, NIN), "lb_logits": (2, 512), "hgrn_norm_g": (1, 512),
    "w_branch_a": (512, D), "w_branch_b": (512, D), "w_out": (D, D), "ln2_g": (1, D), "ln2_b": (1, D),
    "ffn2_wg": (D, FF), "ffn2_wu": (D, FF), "ffn2_wd": (FF, D), "ln3_g": (1, D), "ln3_b": (1, D),
}
ST_BLOCKS = {"qa": (0, 4), "fa": (512, 4), "ga": (1536, 4), "qb": (2048, 4), "gta": (3584, 8), "gtb": (4608, 8)}
MV_BLOCKS = {"ia": 1024, "kb": 2560, "vb": 3072}


def build(SEQ, PAST, stop_after=None):
    NG = SEQ // TB
    NSG = SEQ // 2048
    NOWN = SEQ // 4
    NKB = SEQ // 128
    NPB = PAST // 128
    SKS = PAST + 128
    nc = bass.Bass("TRN2", target_bir_lowering=False)
    es = ExitStack()
    P = Prog(nc, es)
    IN = lambda n, s: P.dram(n, s, F32, "ExternalInput")
    OUT = lambda n, s: P.dram(n, s, F32, "ExternalOutput", accum=True)
    xb = IN("xb", [SEQ, D])
    xs = IN("xs", [DEC, D])
    ck = IN("ck", [PAST, 512])
    cv = IN("cv", [PAST, 512])
    st0 = IN("st0", [HA, 128, 128])
    Wd = {n: IN(n, list(s)) for n, s in W_SHAPES.items()}
    c_ident = IN("c_ident", [128, 128])
    c_mask64 = IN("c_mask64", [128, 512])
    c_mask2 = IN("c_mask2", [128, 512])
    c_uincl = IN("c_uincl", [128, 128])
    c_sbmask = IN("c_sbmask", [16, 128, 512])
    c_smask = IN("c_smask", [128, DEC])
    c_flag = IN("c_flag", [128, 4])
    y_own = OUT("y_own", [NOWN, D])
    k_all = OUT("k_all", [SEQ, 512])
    v_all = OUT("v_all", [SEQ, 512])
    s_fin = OUT("s_fin", [HA, 128, 128])
    ys = OUT("ys", [DEC, D])
    ks = OUT("ks", [DEC, 512])
    vs = OUT("vs", [DEC, 512])
    ss = OUT("ss", [HA, 128, 128])
    dbg = OUT("dbg", [SEQ, D]) if stop_after else None

    SCR = lambda n, s, dt=BF16: P.dram(n, s, dt, "Internal", accum=True)
    ws = {}
    for l in ("ffn1", "ffn2"):
        ws[l + "_wg"] = SCR(l + "_wg_s", [FC, 128, KC * 128])
        ws[l + "_wu"] = SCR(l + "_wu_s", [FC, 128, KC * 128])
        ws[l + "_wd"] = SCR(l + "_wd_s", [FC, 128, D])
    for n, (c0, noc) in ST_BLOCKS.items():
        ws[n] = SCR("win_" + n, [noc, 128, KC * 128])
    for n, c0 in MV_BLOCKS.items():
        ws[n] = SCR("win_" + n, [128, KC, 512])
    ws["wa"] = SCR("wa_s", [8, 128, 4 * 128])
    ws["wb"] = SCR("wb_s", [8, 64, 8 * 128])
    ws["wo"] = SCR("wo_s", [2, 128, KC, 512])
    KT_scr = SCR("KT_scr", [HB, DH, SEQ])
    V_scr = SCR("V_scr", [HB, 128, NKB, DH])
    QT_scr = SCR("QT_scr", [HB, DH, NOWN])
    GA_scr = SCR("GA_scr", [4, 128, NOWN])
    GT_scr = SCR("GT_scr", [16, 128, NOWN])
    OT_scr = SCR("OT_scr", [HA, 128, NOWN], F32)
    X1_scr = SCR("X1_scr", [NOWN, D], F32)
    HB_scr = SCR("HB_scr", [HB, DH, NOWN])
    KTs_scr = SCR("KTs_scr", [HB, DH, SKS])
    Vs_scr = SCR("Vs_scr", [HB, 128, NPB + 1, DH])
    QTs_scr = SCR("QTs_scr", [HB, DH, 128])
    GAs_scr = SCR("GAs_scr", [4, 128, 128])
    GTs_scr = SCR("GTs_scr", [16, 128, 128])
    OTs_scr = SCR("OTs_scr", [HA, 128, 128], F32)
    X1s_scr = SCR("X1s_scr", [128, D], F32)
    HBs_scr = SCR("HBs_scr", [HB, DH, 128])

    def prep_weights():
        q = "pool"
        kpc = lambda t: t.re("p (k c) -> p k c", c=128)
        for l in ("ffn1", "ffn2"):
            for f in range(FC):
                for nm in ("_wg", "_wu"):
                    P.dma(q, kpc(ws[l + nm][f]), Wd[l + nm][:, f * 128:(f + 1) * 128].re("(k p) c -> p k c", p=128))
                P.dma(q, ws[l + "_wd"][f], Wd[l + "_wd"][f * 128:(f + 1) * 128, :])
        for n, (c0, noc) in ST_BLOCKS.items():
            for oc in range(noc):
                P.dma(q, kpc(ws[n][oc]), Wd["w_in"][:, c0 + oc * 128:c0 + (oc + 1) * 128].re("(k p) c -> p k c", p=128))
        for n, c0 in MV_BLOCKS.items():
            P.dma(q, ws[n], Wd["w_in"][:, c0:c0 + 512].re("(k p) c -> p k c", p=128))
        for oc in range(8):
            P.dma(q, kpc(ws["wa"][oc]), Wd["w_branch_a"][:, oc * 128:(oc + 1) * 128].re("(k p) c -> p k c", p=128))
            P.dma(q, kpc(ws["wb"][oc]), Wd["w_branch_b"][:, oc * 128:(oc + 1) * 128].re("(h p) c -> p h c", p=64))
        for hf in range(2):
            P.dma(q, ws["wo"][hf], Wd["w_out"][:, hf * 512:(hf + 1) * 512].re("(k p) c -> p k c", p=128))
        for h in range(HB):
            for b0 in range(0, NPB, 8):
                b1 = min(NPB, b0 + 8)
                P.dma(q, Vs_scr[h, :, b0:b1, :],
                      cv[b0 * 128:b1 * 128, h * DH:(h + 1) * DH].re("(b p) d -> p b d", p=128))

    prep_weights()

    psum = es.enter_context(nc.psum_tensor("psum", [128, 8 * 512], F32))
    PS = [T(Buf("ps%d" % i), psum[:, i * 512:(i + 1) * 512]) for i in range(8)]
    ident = P.sb(es, "ident", [128, 128], F32)
    P.dma("sp", ident, c_ident)
    flag = P.sb(es, "flag", [128, 4], F32)
    P.dma("sp", flag, c_flag)
    NRING = 6
    ring = [P.sb(es, "ring%d" % i, [128, 1024], BF16) for i in range(NRING)]
    ring_i = [0]

    def ring_load(src, k=None, npart=128):
        slot = ring[ring_i[0] % NRING]
        ring_i[0] += 1
        n = src.ap.shape[-1]
        P.dma("sp", slot[0:npart, 0:n], src)
        v = slot[0:npart, 0:n]
        return v if k is None else v.re("p (k c) -> p k c", k=k)

    bank_i = [0]

    def bank(lo=0, hi=4):
        b = PS[lo + bank_i[0] % (hi - lo)]
        bank_i[0] += 1
        return b

    ev_i = [0]

    def evac(out, in_):
        e = "act" if ev_i[0] % 2 == 0 else "dve"
        ev_i[0] += 1
        P.cp(e, out, in_)

    def transpose_to_fm(src_tm, dstT, ntc, nk=KC):
        for k in range(nk):
            b = bank()
            for c in range(ntc):
                P.tr(b[:, c * 128:(c + 1) * 128], src_tm[c][:, k * 128:(k + 1) * 128], ident, sig=(c == ntc - 1))
            evac(dstT[:, k, 0:ntc * 128], b[:, 0:ntc * 128])

    def layer_norm(o, lng, lnb, small, eps):
        st, mv, ve, rstd, mhalf = small
        P.op("dve", lambda: nc.vector.bn_stats(st.ap[:, 0:6], o.ap[:, 0:512]), [o], [st])
        P.op("dve", lambda: nc.vector.bn_stats(st.ap[:, 6:12], o.ap[:, 512:1024]), [o], [st])
        P.op("dve", lambda: nc.vector.bn_aggr(mv.ap, st.ap), [st], [mv])
        P.ts("pool", ve, mv[:, 1:2], eps, ALU.add)
        P.tt("pool", rstd, ve, mhalf, ALU.pow)
        P.ts("dve", o, o, mv[:, 0:1], ALU.subtract, rstd, ALU.mult)
        P.tt("dve", o, o, lng, ALU.mult)
        P.tt("dve", o, o, lnb, ALU.add)

    def ffn_ln(pref, x_tm, xT, hT, sg, out_tm, lng, lnb, small, ntc):
        NT = ntc * 128
        wg, wu, wd = ws[pref + "_wg"], ws[pref + "_wu"], ws[pref + "_wd"]
        for f in range(FC):
            tg = ring_load(wg[f], KC)
            tu = ring_load(wu[f], KC)
            bg = bank()
            bu = bank()
            for k in range(KC):
                P.mm(bg[:, 0:NT], tg[:, k, :], xT[:, k, 0:NT], start=(k == 0), stop=(k == KC - 1), sig=(k == KC - 1))
            for k in range(KC):
                P.mm(bu[:, 0:NT], tu[:, k, :], xT[:, k, 0:NT], start=(k == 0), stop=(k == KC - 1), sig=(k == KC - 1))
            s = sg[f % 2]
            P.act(s[:, 0:NT], bg[:, 0:NT], AF.Silu)
            P.tt("dve", hT[f][:, 0:NT], s[:, 0:NT], bu[:, 0:NT], ALU.mult)
        for p0 in range(0, ntc, 2):
            cs = list(range(p0, min(p0 + 2, ntc)))
            ab = 4 if p0 == 0 else 0
            for f in range(FC):
                td = ring_load(wd[f])
                for c in cs:
                    for hf in range(2):
                        P.mm(PS[ab + (c % 2) * 2 + hf], hT[f][:, c * 128:(c + 1) * 128], td[:, hf * 512:(hf + 1) * 512],
                             start=(f == 0), stop=(f == FC - 1))
            for c in cs:
                o = out_tm[c]
                for hf in range(2):
                    P.stt(o[:, hf * 512:(hf + 1) * 512], PS[ab + (c % 2) * 2 + hf], 0.5 / ALPHA,
                          x_tm[c][:, hf * 512:(hf + 1) * 512], ALU.mult, ALU.add)
            for c in cs:
                layer_norm(out_tm[c], lng, lnb, small, LN_EPS / (ALPHA * ALPHA))

    def bcast_load(es_, name, src, n):
        t = P.sb(es_, name, [128, n], F32)
        P.dma("sp", t, T(src.buf, src.ap[0:1, :].partition_broadcast(128)))
        return t

    def mk_small(es_):
        st = P.sb(es_, "bnst", [128, 12], F32)
        mv = P.sb(es_, "bnmv", [128, 2], F32)
        ve = P.sb(es_, "bnve", [128, 1], F32)
        rstd = P.sb(es_, "bnrstd", [128, 1], F32)
        mhalf = P.sb(es_, "mhalf", [128, 1], F32)
        P.memset("pool", mhalf, -0.5)
        return (st, mv, ve, rstd, mhalf)

    with ExitStack() as ea:
        x_tm = [P.sb(ea, "x_tm%d" % c, [128, D], F32) for c in range(4)]
        x1_tm = [P.sb(ea, "x1_tm%d" % c, [128, D], F32) for c in range(4)]
        xT_all = P.sb(ea, "xT", [128, KC, TB], BF16)
        x1T_all = P.sb(ea, "x1T", [128, KC, TB], BF16)
        hT_all = P.sb(ea, "hT", [128, FC, TB], BF16)
        hT = [hT_all.sub((slice(None), f, slice(None)), "hT%d" % f) for f in range(FC)]
        sg = [P.sb(ea, "sg%d" % i, [128, TB], F32) for i in range(2)]
        ln1g = bcast_load(ea, "ln1g", Wd["ln1_g"], D)
        ln1b = bcast_load(ea, "ln1b", Wd["ln1_b"], D)
        small = mk_small(ea)
        wia = P.sb(ea, "wia", [128, KC, 512], BF16)
        wkb = P.sb(ea, "wkb", [128, KC, 512], BF16)
        wvb = P.sb(ea, "wvb", [128, KC, 512], BF16)
        P.dma("sp", wia, ws["ia"])
        P.dma("sp", wkb, ws["kb"])
        P.dma("sp", wvb, ws["vb"])
        mask64 = P.sb(ea, "mask64", [128, 512], F32)
        mask2 = P.sb(ea, "mask2", [128, 512], F32)
        P.dma("sp", mask64, c_mask64)
        P.dma("sp", mask2, c_mask2)
        lbl = P.sb(ea, "lbl", [128, 2, 4], F32)
        P.dma("sp", lbl, Wd["lb_logits"].re("r (h p) -> p r h", p=128), allow_slow_non_contiguous=True)
        oml = P.sb(ea, "oml", [128, 4], F32)
        noml = P.sb(ea, "noml", [128, 4], F32)
        lbd = P.sb(ea, "lbd", [128, 4], F32)
        P.tt("dve", lbd, lbl[:, 0, :], lbl[:, 1, :], ALU.subtract)
        P.act(lbd, lbd, AF.Exp)
        P.ts("dve", lbd, lbd, 1.0, ALU.add)
        P.op("dve", lambda: nc.vector.reciprocal(oml.ap, lbd.ap), [lbd], [oml])
        P.ts("dve", noml, oml, -1.0, ALU.mult)
        vtm = P.sb(ea, "vtm", [128, 4, 512], BF16)
        ktm = [P.sb(ea, "ktm%d" % i, [128, 512], F32) for i in range(4)]
        vbtm = [P.sb(ea, "vbtm%d" % i, [128, 512], F32) for i in range(2)]
        kTs = [P.sb(ea, "kTs%d" % i, [128, 512], F32) for i in range(2)]
        hqT = [P.sb(ea, "hqT%d" % h, [128, TB], F32) for h in range(HA)]
        he = P.sb(ea, "he", [128, TB], F32)
        hr = P.sb(ea, "hr", [128, TB], F32)
        hlf = P.sb(ea, "hlf", [128, TB], F32)
        hb = P.sb(ea, "hb", [128, TB], F32)
        hebn = P.sb(ea, "hebn", [128, TB], F32)
        qt2 = [P.sb(ea, "qt%d" % i, [128, TB], BF16) for i in range(2)]
        kt2 = [P.sb(ea, "kt%d" % i, [128, TB], BF16) for i in range(2)]
        kdT2 = [P.sb(ea, "kdT%d" % i, [128, TB], F32) for i in range(2)]
        heb2 = [P.sb(ea, "heb%d" % i, [128, TB], F32) for i in range(2)]
        kd_tm = P.sb(ea, "kd_tm", [128, 4, 128], BF16)
        AT = P.sb(ea, "AT", [128, TB], BF16)
        Sst = [[P.sb(ea, "S%d_%d" % (h, i), [128, 128], F32) for i in range(2)] for h in range(HA)]
        Ssm = [[P.sb(ea, "Ss%d_%d" % (h, i), [128, 128], F32) for i in range(2)] for h in range(HA)]
        Sb = [P.sb(ea, "Sb%d" % i, [128, 128], BF16) for i in range(2)]
        sb_i = [0]
        oTown = P.sb(ea, "oTown", [128, HA, 128], F32)
        x1own = P.sb(ea, "x1own", [128, D], F32)
        x1Town = P.sb(ea, "x1Town", [128, KC, TB], BF16)
        pj = [P.sb(ea, "pj%d" % i, [128, TB], BF16) for i in range(2)]
        pj_i = [0]
        for h in range(HA):
            P.memset("pool", Sst[h][0], 0.0)

        def kT_path(src_tm_list, dst_scr, col0, nblk):
            for oc in range(4):
                b = bank(4, 8)
                for i, src in enumerate(src_tm_list):
                    P.tr(b[:, i * 128:(i + 1) * 128], src[:, oc * 128:(oc + 1) * 128], ident, sig=(i == nblk - 1))
                kk = kTs[oc % 2]
                evac(kk[:, 0:nblk * 128], b[:, 0:nblk * 128])
                P.dma("pool", dst_scr[2 * oc:2 * oc + 2, :, col0:col0 + nblk * 128].re("h d s -> (h d) s"),
                      kk[:, 0:nblk * 128])

        def group_tail(gi, ntc, sample):
            NT = ntc * 128
            S = Ssm if sample else Sst
            ktl = []
            halves = [list(range(ntc))] if ntc < 4 else [[0, 1], [2, 3]]
            for c in [cc for hv in halves for cc in ["T%d" % halves.index(hv)] + hv]:
                if isinstance(c, str):
                    hv = halves[int(c[1:])]
                    for k in range(KC):
                        bT = bank()
                        for ii, cc in enumerate(hv):
                            P.tr(bT[:, ii * 128:(ii + 1) * 128], x1_tm[cc][:, k * 128:(k + 1) * 128], ident, sig=(ii == len(hv) - 1))
                        P.cp("act", x1T_all[:, k, hv[0] * 128:(hv[-1] + 1) * 128], bT[:, 0:len(hv) * 128])
                    continue
                csl = slice(c * 128, (c + 1) * 128)
                b = bank()
                for k in range(KC):
                    P.mm(b, x1T_all[:, k, csl], wia[:, k, :], start=(k == 0), stop=(k == KC - 1), sig=(k == KC - 1))
                evac(vtm[:, c, :], b)
                b = bank()
                for k in range(KC):
                    P.mm(b, x1T_all[:, k, csl], wkb[:, k, :], start=(k == 0), stop=(k == KC - 1), sig=(k == KC - 1))
                kk = ktm[c % 4]
                ktl.append(kk)
                evac(kk, b)
                b = bank()
                for k in range(KC):
                    P.mm(b, x1T_all[:, k, csl], wvb[:, k, :], start=(k == 0), stop=(k == KC - 1), sig=(k == KC - 1))
                vv = vbtm[c % 2]
                evac(vv, b)
                if sample:
                    P.dma("pool", ks, kk[0:DEC, :])
                    P.dma("pool", vs, vv[0:DEC, :])
                    P.dma("pool", Vs_scr[:, :, NPB, :].re("h p d -> p h d"), vv.re("p (h d) -> p h d", d=DH))
                    kT_path([kk], KTs_scr, PAST, 1)
                else:
                    r0 = gi * TB + c * 128
                    P.dma("pool", k_all[r0:r0 + 128, :], kk)
                    P.dma("pool", v_all[r0:r0 + 128, :], vv)
                    P.dma("pool", V_scr[:, :, gi * 4 + c, :].re("h p d -> p h d"), vv.re("p (h d) -> p h d", d=DH))
            if not sample:
                kT_path(ktl, KT_scr, gi * TB, ntc)
            for h in range(HA):
                tq = ring_load(ws["qa"][h], KC)
                b = bank()
                for k in range(KC):
                    P.mm(b[:, 0:NT], tq[:, k, :], x1T_all[:, k, 0:NT], start=(k == 0), stop=(k == KC - 1), sig=(k == KC - 1))
                P.act(hqT[h][:, 0:NT], b[:, 0:NT], AF.Silu)
            chunks = [(0, DEC)] if sample else [(c * 64, 64) for c in range(NT // 64)]
            N_ = slice(0, NT)

            def hg_front(h):
                hs = h % 2
                tf = ring_load(ws["fa"][h], KC)
                zf = bank()
                for k in range(KC):
                    P.mm(zf[:, 0:NT], tf[:, k, :], x1T_all[:, k, 0:NT], start=(k == 0), stop=(k == KC - 1), sig=(k == KC - 1))
                N_ = slice(0, NT)
                P.act(he[:, N_], zf[:, N_], AF.Exp)
                yield
                P.act(he[:, N_], he[:, N_], AF.Ln, bias=1.0)
                P.act(hr[:, N_], he[:, N_], AF.Exp, scale=-1.0)
                P.act(hlf[:, N_], hr[:, N_], AF.Ln, bias=1.0, scale=noml[:, h:h + 1])
                yield
                P.op("dve", lambda: nc.vector.tensor_tensor_scan(hb.ap[:, N_], mask64.ap[:, N_], hlf.ap[:, N_], 0.0,
                                                                  ALU.mult, ALU.add), [mask64, hlf], [hb])
                yield
                P.act(heb2[hs][:, N_], hb[:, N_], AF.Exp)
                P.act(hebn[:, N_], hb[:, N_], AF.Exp, scale=-1.0)
                yield
                P.tt("dve", qt2[hs][:, N_], hqT[h][:, N_], heb2[hs][:, N_], ALU.mult)
                P.stt(kt2[hs][:, N_], hr[:, N_], oml[:, h:h + 1], hebn[:, N_], ALU.mult, ALU.mult)
                if sample:
                    P.memset("pool", kdT2[hs][:, DEC:128], 0.0)
                    P.ts("dve", kdT2[hs][:, 0:DEC], kt2[hs][:, 0:DEC], heb2[hs][:, DEC - 1:DEC], ALU.mult)
                else:
                    nch = NT // 64
                    lastb = T(heb2[hs].buf, heb2[hs].ap[:, N_].rearrange("p (c t) -> p c t", t=64)[:, :, 63:64].to_broadcast([128, nch, 64]))
                    P.tt("dve", kdT2[hs][:, N_].re("p (c t) -> p c t", t=64), kt2[hs][:, N_].re("p (c t) -> p c t", t=64), lastb, ALU.mult)
                yield

            def hg_back(h):
                hs = h % 2
                b = bank()
                for c in range(ntc):
                    P.tr(b[:, c * 128:(c + 1) * 128], kdT2[hs][:, c * 128:(c + 1) * 128], ident, sig=(c == ntc - 1))
                evac(kd_tm[:, 0:ntc, :], b[:, 0:NT].re("p (c k) -> p c k", k=128))
                yield
                b = bank()
                for c in range(ntc):
                    csl = slice(c * 128, (c + 1) * 128)
                    P.mm(b[:, csl], kt2[hs][:, csl], qt2[hs][:, csl], start=True, stop=True, sig=(c == ntc - 1))
                P.tt("dve", AT[:, N_], b[:, N_], mask2[:, N_], ALU.mult)
                yield
                pb = [PS[6], PS[7]]
                for ci, (c0, cl) in enumerate(chunks):
                    prt = slice(c0 % 128, c0 % 128 + cl)
                    P.mm(pb[ci % 2][:, (ci // 2) * 128:(ci // 2 + 1) * 128], kd_tm[prt, c0 // 128, :],
                         vtm[prt, c0 // 128, h * 128:(h + 1) * 128], start=True, stop=True)
                ob = PS[4 + h % 2]
                for c in range(ntc):
                    csl = slice(c * 128, (c + 1) * 128)
                    P.mm(ob[:, csl], vtm[:, c, h * 128:(h + 1) * 128], AT[:, csl], start=(c == 0), stop=False)
                cur = 0 if not sample else 0
                for ci, (c0, cl) in enumerate(chunks):
                    s_in = S[h][cur_state[(sample, h)]]
                    s_out = S[h][1 - cur_state[(sample, h)]]
                    sbf = Sb[sb_i[0] % 2]
                    sb_i[0] += 1
                    P.cp("act", sbf, s_in)
                    cw = 64 if not sample else 64
                    P.mm(ob[:, c0:c0 + cw], sbf, qt2[hs][:, c0:c0 + cw], start=False, stop=(ci == len(chunks) - 1))
                    P.stt(s_out, s_in, heb2[hs][:, c0 + cl - 1:c0 + cl], pb[ci % 2][:, (ci // 2) * 128:(ci // 2 + 1) * 128],
                          ALU.mult, ALU.add)
                    cur_state[(sample, h)] = 1 - cur_state[(sample, h)]
                    yield
                if sample:
                    P.cp("dve", oTown[:, h, :], ob[:, 0:128])
                else:
                    P.ts("dve", oTown[:, h, :], ob[:, 0:128], flag[:, 0:1], ALU.mult)
                    for r in range(1, 4):
                        P.stt(oTown[:, h, :], ob[:, r * 128:(r + 1) * 128], flag[:, r:r + 1], oTown[:, h, :], ALU.mult, ALU.add)
                yield

            def run2(ga, gb):
                la, lb = ga is not None, gb is not None
                while la or lb:
                    if lb:
                        try:
                            next(gb)
                        except StopIteration:
                            lb = False
                    if la:
                        try:
                            next(ga)
                        except StopIteration:
                            la = False

            run2(hg_front(0), None)
            for h in range(HA):
                run2(hg_front(h + 1) if h + 1 < HA else None, hg_back(h))
            if sample:
                P.dma("pool", OTs_scr.re("h v t -> v h t"), oTown)
                P.dma("pool", X1s_scr, x1_tm[0])
                P.cp("pool", x1Town[:, :, 0:128], x1T_all[:, :, 0:128])
                own_proj(0, 128, True)
            elif 'nosel' in KNOB:
                pass
            else:
                P.dma("pool", OT_scr[:, :, gi * 128:(gi + 1) * 128].re("h v t -> v h t"), oTown)
                P.ts("dve", x1own, x1_tm[0], flag[:, 0:1], ALU.mult)
                for r in range(1, 4):
                    P.stt(x1own, x1_tm[r], flag[:, r:r + 1], x1own, ALU.mult, ALU.add)
                P.dma("pool", X1_scr[gi * 128:(gi + 1) * 128, :], x1own)
                dst = x1Town[:, :, (gi % 4) * 128:(gi % 4 + 1) * 128]
                P.ts("dve", dst, x1T_all[:, :, 0:128], flag[:, 0:1], ALU.mult)
                for r in range(1, 4):
                    P.stt(dst, x1T_all[:, :, r * 128:(r + 1) * 128], flag[:, r:r + 1], dst, ALU.mult, ALU.add)
                if gi % 4 == 3:
                    own_proj((gi // 4) * 512, 512, False)

        cur_state = {(s_, h): 0 for s_ in (False, True) for h in range(HA)}

        def own_proj(t0, N, sample):
            QT, GA, GT = (QTs_scr, GAs_scr, GTs_scr) if sample else (QT_scr, GA_scr, GT_scr)
            for (nm, noc) in (("qb", 4), ("ga", 4), ("gta", 8), ("gtb", 8)):
                for oc in range(noc):
                    tw = ring_load(ws[nm][oc], KC)
                    b = bank()
                    for k in range(KC):
                        P.mm(b[:, 0:N], tw[:, k, :], x1Town[:, k, 0:N], start=(k == 0), stop=(k == KC - 1), sig=(k == KC - 1))
                    o = pj[pj_i[0] % 2]
                    pj_i[0] += 1
                    if nm == "qb":
                        P.act(o[:, 0:N], b[:, 0:N], AF.Copy, scale=0.125)
                        P.dma("pool", QT[2 * oc:2 * oc + 2, :, t0:t0 + N].re("h d s -> (h d) s"), o[:, 0:N])
                    elif nm == "ga":
                        P.act(o[:, 0:N], b[:, 0:N], AF.Silu)
                        P.dma("pool", GA[oc, :, t0:t0 + N], o[:, 0:N])
                    else:
                        P.act(o[:, 0:N], b[:, 0:N], AF.Tanh, scale=0.5)
                        P.dma("pool", GT[oc + (8 if nm == "gtb" else 0), :, t0:t0 + N], o[:, 0:N])

        P.memset("pool", x_tm[0], 0.0)
        P.dma("sp", x_tm[0][0:DEC, :], xs)
        for h in range(HA):
            P.dma("sp", Ssm[h][0], st0[h])
        transpose_to_fm(x_tm, xT_all, 1)
        ffn_ln("ffn1", x_tm, xT_all, hT, sg, x1_tm, ln1g, ln1b, small, 1)
        def early_exit():
            P.barrier()
            ea.close()
            P.barrier()
            es.close()
            return nc, P
        if stop_after == "s_ffn":
            P.dma("pool", dbg[0:128, :], x1_tm[0])
            return early_exit()
        group_tail(0, 1, True)
        for h in range(HA):
            P.dma("pool", ss[h], Ssm[h][cur_state[(True, h)]])
        if stop_after == "s_tail":
            return early_exit()
        for b0 in range(0, NPB, 4):
            tiles = []
            nb = min(4, NPB - b0)
            for i in range(nb):
                P.dma("sp", x1_tm[i][:, 0:512], ck[(b0 + i) * 128:(b0 + i + 1) * 128, :])
                tiles.append(x1_tm[i][:, 0:512])
            kT_path(tiles, KTs_scr, b0 * 128, nb)

        if stop_after == "past":
            return early_exit()
        for gi in range(NG):
            if stop_after == "A1" and gi == 1:
                return early_exit()
            for c in range(4):
                P.dma("sp", x_tm[c], xb[gi * TB + c * 128: gi * TB + (c + 1) * 128, :])
            transpose_to_fm(x_tm, xT_all, 4)
            ffn_ln("ffn1", x_tm, xT_all, hT, sg, x1_tm, ln1g, ln1b, small, 4)
            if stop_after == "ffn1":
                for c in range(4):
                    P.dma("pool", dbg[gi * TB + c * 128: gi * TB + (c + 1) * 128, :], x1_tm[c])
                continue
            group_tail(gi, 4, False)
        for h in range(HA):
            P.dma("pool", s_fin[h], Sst[h][cur_state[(False, h)]])
        P.barrier()
    if stop_after in ("ffn1", "A"):
        P.barrier()
        es.close()
        return nc, P

    with ExitStack() as eb_:
        uincl = P.sb(eb_, "uincl", [128, 128], BF16)
        P.dma("pool", uincl, c_uincl)
        onesb = P.sb(eb_, "onesb", [128, 128], BF16)
        P.memset("pool", onesb, 1.0)
        sbm = P.sb(eb_, "sbm", [128, 16, 512], BF16)
        for m0 in range(0, 16, 4):
            P.dma("pool", sbm[:, m0:m0 + 4, :], c_sbmask[m0:m0 + 4].re("m p t -> p m t"))
        smk = P.sb(eb_, "smk", [128, DEC], BF16)
        P.dma("pool", smk, c_smask)
        KTh = [P.sb(eb_, "KTh%d" % i, [DH, max(SEQ, SKS)], BF16) for i in range(2)]
        Vh = [P.sb(eb_, "Vh%d" % i, [128, max(NKB, NPB + 1), DH], BF16) for i in range(2)]
        QTh = [P.sb(eb_, "QTh%d" % i, [DH, max(NOWN, 128)], BF16) for i in range(2)]
        E = [P.sb(eb_, "sbE%d" % i, [128, 128], F32) for i in range(4)]
        SPt = [P.sb(eb_, "sbSP%d" % i, [128, 128], BF16) for i in range(4)]
        SPm = [P.sb(eb_, "sbSPm%d" % i, [128, 128], BF16) for i in range(4)]
        Em = [P.sb(eb_, "sbEm%d" % i, [128, 128], F32) for i in range(4)]
        E2 = [P.sb(eb_, "sbE2%d" % i, [128, 128], F32) for i in range(2)]
        Wt = [P.sb(eb_, "sbW%d" % i, [128, 128], BF16) for i in range(2)]
        R32 = [P.sb(eb_, "sbR32_%d" % i, [128, 128], F32) for i in range(2)]
        Rb = [P.sb(eb_, "sbRb%d" % i, [128, 128], BF16) for i in range(4)]
        hbo = [P.sb(eb_, "hbo%d" % i, [DH, 512], BF16) for i in range(2)]
        for i in range(2):
            P.memset("pool", hbo[i], 0.0)
        ti = [0]
        oi = [0]

        def sb_attend(KT, V, QT, nq, kblocks, dst):
            Q_ = slice(0, nq)
            n = len(kblocks)
            outb = PS[6 + oi[0] % 2]
            ho = hbo[oi[0] % 2]
            oi[0] += 1
            base = ti[0]
            ti[0] += n
            spm_of, em_of = {}, {}

            def stA(i):
                kb = kblocks[i][0]
                P.mm(PS[(base + i) % 3][:, Q_], KT[:, kb * 128:(kb + 1) * 128], QT[:, Q_])

            def stB(i):
                mk = kblocks[i][1]
                j = base + i
                e = E[j % 4]
                P.act(e[:, Q_], PS[j % 3][:, Q_], AF.Exp)
                sp = SPt[j % 4]
                P.act(sp[:, Q_], e[:, Q_], AF.Ln, bias=1.0)
                if mk is not None:
                    spm = SPm[j % 4]
                    P.tt("dve", spm[:, Q_], sp[:, Q_], mk, ALU.mult)
                    em = Em[j % 4]
                    P.tt("dve", em[:, Q_], e[:, Q_], mk, ALU.mult)
                else:
                    spm, em = sp, e
                spm_of[i], em_of[i] = spm, em

            def stC(i):
                c = PS[3 + (base + i) % 3]
                ops = [(uincl, spm_of[i])]
                if i >= 1:
                    ops.append((onesb, spm_of[i - 1]))
                if i >= 2:
                    ops.append((onesb, Rb[(base + i - 2) % 4]))
                for oi_, (l, r_) in enumerate(ops):
                    P.mm(c[:, Q_], l, r_[:, Q_], start=(oi_ == 0), stop=(oi_ == len(ops) - 1), sig=(oi_ == len(ops) - 1))

            def stR(i):
                if i + 2 > n - 1:
                    return
                if i == 0:
                    P.cp("pool", R32[(base + i) % 2][:, Q_], spm_of[0][:, Q_])
                else:
                    P.tt("pool", R32[(base + i) % 2][:, Q_], R32[(base + i - 1) % 2][:, Q_], spm_of[i][:, Q_], ALU.add)

            def stRc(i):
                if i + 2 > n - 1:
                    return
                P.cp("dve", Rb[(base + i) % 4][:, Q_], R32[(base + i) % 2][:, Q_])

            def stD(i):
                j = base + i
                P.act(E2[j % 2][:, Q_], PS[3 + j % 3][:, Q_], AF.Exp, scale=-1.0)

            def stE(i):
                j = base + i
                P.tt("dve", Wt[j % 2][:, Q_], em_of[i][:, Q_], E2[j % 2][:, Q_], ALU.mult)

            def stF(i):
                kb = kblocks[i][0]
                P.mm(outb[0:DH, Q_], V[:, kb, :], Wt[(base + i) % 2][:, Q_], start=(i == 0), stop=(i == n - 1))

            for t in range(-3, n + 2):
                for fn, off in ((stA, 3), (stB, 2), (stRc, -1), (stC, 1), (stR, 0), (stD, 0), (stE, -1), (stF, -1)):
                    i = t + off
                    if 0 <= i < n:
                        fn(i)
            P.cp("act", ho[:, Q_], outb[0:DH, Q_])
            P.dma("pool", dst, ho[:, 0:max(nq, 128)])

        ZP = [T(Buf("zp%d" % m), psum[:, 2 * m * 512:(2 * m + 2) * 512]) for m in range(3)]
        E2p = [P.sb(eb_, "pE%d" % i, [128, 1024], F32) for i in range(2)]
        SP2 = [P.sb(eb_, "pSP%d" % i, [128, 1024], BF16) for i in range(3)]
        S2 = [P.sb(eb_, "pS2%d" % i, [128, 512], BF16) for i in range(4)]
        W2 = [P.sb(eb_, "pW%d" % i, [128, 1024], BF16) for i in range(3)]
        Rb2 = [P.sb(eb_, "pRb%d" % i, [128, 512], BF16) for i in range(3)]
        R2 = [P.sb(eb_, "pR%d" % i, [128, 512], F32) for i in range(2)]
        qn = [P.sb(eb_, "qn%d" % i, [DH, 512], BF16) for i in range(2)]
        pi_ = [0]

        def sb_pipeline(jobs):
            H = (slice(0, 512), slice(512, 1024))
            look = []
            for jn, jb in enumerate(jobs):
                jb["np"] = len(jb["kblocks"]) // 2
                jb["outb"] = PS[6 + jn % 2]
                jb["ho"] = hbo[jn % 2]
                jb["qneg"] = qn[jn % 2]
                for i in range(jb["np"]):
                    look.append((jb, i))
            NP = len(look)

            def kbs(jb, i):
                return jb["kblocks"][2 * i], jb["kblocks"][2 * i + 1]

            def stA(G):
                jb, i = look[G]
                if i == 0:
                    if jb.get("pre"):
                        jb["pre"]()
                    P.ts("dve", jb["qneg"], jb["QT"], -1.0, ALU.mult)
                z = ZP[G % 3]
                for hh, (kb, mk) in enumerate(kbs(jb, i)):
                    P.mm(z[:, H[hh]], jb["KT"][:, kb * 128:(kb + 1) * 128], jb["QT"])

            def stB(G):
                jb, i = look[G]
                z, e, sp = ZP[G % 3], E2p[G % 2], SP2[G % 3]
                P.act(e, z, AF.Exp)
                P.act(sp, e, AF.Ln, bias=1.0)
                for hh, (kb, mk) in enumerate(kbs(jb, i)):
                    if mk is not None:
                        P.tt("dve", sp[:, H[hh]], sp[:, H[hh]], mk, ALU.mult)
                if i + 1 <= jb["np"] - 1:
                    P.tt("dve", S2[G % 4], sp[:, H[0]], sp[:, H[1]], ALU.add)

            def stRc(G):
                jb, i = look[G]
                if i + 2 > jb["np"] - 1:
                    return
                P.cp("dve", Rb2[G % 3], R2[G % 2])

            def stC(G):
                jb, i = look[G]
                z, sp = ZP[G % 3], SP2[G % 3]
                for hh, (kb, mk) in enumerate(kbs(jb, i)):
                    ops = [(jb["KT"][:, kb * 128:(kb + 1) * 128], jb["qneg"]), (uincl, sp[:, H[hh]])]
                    if hh == 1:
                        ops.append((onesb, sp[:, H[0]]))
                    if i >= 1:
                        ops.append((onesb, S2[(G - 1) % 4]))
                    if i >= 2:
                        ops.append((onesb, Rb2[(G - 2) % 3]))
                    for oi_, (l, r_) in enumerate(ops):
                        P.mm(z[:, H[hh]], l, r_, start=(oi_ == 0), stop=(oi_ == len(ops) - 1), sig=(oi_ == len(ops) - 1))

            def stR(G):
                jb, i = look[G]
                if i + 2 > jb["np"] - 1:
                    return
                if i == 0:
                    P.cp("pool", R2[G % 2], S2[G % 4])
                else:
                    P.tt("pool", R2[G % 2], R2[(G - 1) % 2], S2[G % 4], ALU.add)

            def stD(G):
                jb, i = look[G]
                w = W2[G % 3]
                P.act(w, ZP[G % 3], AF.Exp, scale=-1.0)
                for hh, (kb, mk) in enumerate(kbs(jb, i)):
                    if mk is not None:
                        P.tt("dve", w[:, H[hh]], w[:, H[hh]], mk, ALU.mult)

            def stF(G):
                jb, i = look[G]
                w = W2[G % 3]
                for hh, (kb, mk) in enumerate(kbs(jb, i)):
                    P.mm(jb["outb"][0:DH, :], jb["V"][:, kb, :], w[:, H[hh]], start=(i == 0 and hh == 0),
                         stop=(i == jb["np"] - 1 and hh == 1))
                if i == jb["np"] - 1:
                    P.cp("act", jb["ho"], jb["outb"][0:DH, :])
                    P.dma("pool", jb["dst"], jb["ho"])

            for t in range(-2, NP + 1):
                for fn, off in ((stA, 2), (stB, 1), (stRc, -1), (stC, 1), (stR, 0), (stD, 0), (stF, -1)):
                    G = t + off
                    if 0 <= G < NP:
                        fn(G)

        hi = [0]
        for h in range(HB):
            kt_, v_, q_ = KTh[hi[0] % 2], Vh[hi[0] % 2], QTh[hi[0] % 2]
            hi[0] += 1
            P.dma("sp", kt_[:, 0:SKS], KTs_scr[h])
            P.dma("sp", v_[:, 0:NPB + 1, :], Vs_scr[h])
            P.dma("sp", q_[:, 0:128], QTs_scr[h])
            kbl = [(NPB, smk)] + [(kb, None) for kb in range(NPB - 1, -1, -1)]
            sb_attend(kt_, v_, q_, DEC, kbl, HBs_scr[h, :, 0:128])
        P.barrier()
        if stop_after != "Bs":
            jobs = []
            for h in range(HB):
                kt_, v_, q_ = KTh[hi[0] % 2], Vh[hi[0] % 2], QTh[hi[0] % 2]
                hi[0] += 1

                def pre(h=h, kt_=kt_, v_=v_, q_=q_):
                    P.dma("sp", kt_[:, 0:SEQ], KT_scr[h])
                    P.dma("sp", v_[:, 0:NKB, :], V_scr[h])
                    P.dma("sp", q_[:, 0:NOWN], QT_scr[h])
                for M in range(NSG):
                    kbl = [(kb, (sbm[:, kb - 16 * M, :] if kb >= 16 * M else None)) for kb in range(16 * M + 15, -1, -1)]
                    jobs.append(dict(pre=(pre if M == 0 else None), KT=kt_, V=v_, QT=q_[:, M * 512:(M + 1) * 512],
                                     kblocks=kbl, dst=HB_scr[h, :, M * 512:(M + 1) * 512]))
            sb_pipeline(jobs)
        P.barrier()
    if stop_after in ("B", "Bs"):
        P.barrier()
        es.close()
        return nc, P

    with ExitStack() as ec:
        x1o = [P.sb(ec, "x1o%d" % c, [128, D], F32) for c in range(4)]
        x2_tm = [P.sb(ec, "x2_tm%d" % c, [128, D], F32) for c in range(4)]
        y_tm = [P.sb(ec, "y_tm%d" % c, [128, D], F32) for c in range(4)]
        xT_all = P.sb(ec, "xT2", [128, KC, TB], BF16)
        hT_all = P.sb(ec, "hT2", [128, FC, TB], BF16)
        hT = [hT_all.sub((slice(None), f, slice(None)), "hT2_%d" % f) for f in range(FC)]
        sg = [P.sb(ec, "sg2_%d" % i, [128, TB], F32) for i in range(2)]
        ln2g = bcast_load(ec, "ln2g", Wd["ln2_g"], D)
        ln2b = bcast_load(ec, "ln2b", Wd["ln2_b"], D)
        ln3g = bcast_load(ec, "ln3g", Wd["ln3_g"], D)
        ln3b = bcast_load(ec, "ln3b", Wd["ln3_b"], D)
        small = mk_small(ec)
        mhalf = small[4]
        wo = [P.sb(ec, "wo%d" % i, [128, KC, 512], BF16) for i in range(2)]
        for i in range(2):
            P.dma("sp", wo[i], ws["wo"][i])
        gn = P.sb(ec, "gn", [128, HA], F32)
        P.dma("sp", gn, Wd["hgrn_norm_g"].re("o (h p) -> p (o h)", p=128), allow_slow_non_contiguous=True)
        onesf = P.sb(ec, "onesf", [128, 128], F32)
        P.memset("pool", onesf, 1.0 / 128.0)
        oT4 = P.sb(ec, "oT4", [128, HA, TB], F32)
        ga4 = P.sb(ec, "ga4", [128, HA, TB], BF16)
        gt16 = P.sb(ec, "gt16", [128, 16, TB], BF16)
        hb8 = P.sb(ec, "hb8", [DH, HB, TB], BF16)
        sq = P.sb(ec, "sq", [128, TB], F32)
        t1 = P.sb(ec, "t1", [128, TB], F32)
        rs = P.sb(ec, "rs", [128, TB], F32)
        haT = P.sb(ec, "haT", [128, HA, TB], BF16)
        m1 = P.sb(ec, "m1", [128, TB], F32)
        m2 = P.sb(ec, "m2", [128, TB], F32)
        mergedT = P.sb(ec, "mergedT", [128, KC, TB], BF16)

        def own_tail(t0, ntc, sample):
            N = ntc * 128
            N_ = slice(0, N)
            OTs, GAs, GTs, HBs, X1s = (OTs_scr, GAs_scr, GTs_scr, HBs_scr, X1s_scr) if sample else \
                (OT_scr, GA_scr, GT_scr, HB_scr, X1_scr)
            P.dma("sp", oT4[:, :, N_], OTs[:, :, t0:t0 + N].re("h v t -> v h t"))
            P.dma("sp", ga4[:, :, N_], GAs[:, :, t0:t0 + N].re("h p t -> p h t"))
            P.dma("sp", gt16[:, :, N_], GTs[:, :, t0:t0 + N].re("h p t -> p h t"))
            P.dma("sp", hb8[:, :, N_], HBs[:, :, t0:t0 + N].re("h d t -> d h t"))
            for c in range(ntc):
                P.dma("sp", x1o[c], X1s[t0 + c * 128:t0 + (c + 1) * 128, :])
            for h in range(HA):
                P.act(sq[:, N_], oT4[:, h, N_], AF.Square)
                b = bank()
                P.mm(b[:, N_], onesf, sq[:, N_])
                P.ts("dve", t1[:, N_], b[:, N_], RMS_EPS, ALU.add)
                P.act(rs[:, N_], t1[:, N_], AF.Ln)
                P.act(rs[:, N_], rs[:, N_], AF.Exp, scale=-0.5)
                P.tt("dve", t1[:, N_], oT4[:, h, N_], rs[:, N_], ALU.mult)
                P.stt(haT[:, h, N_], t1[:, N_], gn[:, h:h + 1], ga4[:, h, N_], ALU.mult, ALU.mult)
            for oc in range(KC):
                wa = ring_load(ws["wa"][oc], 4)
                pa = bank()
                for kc in range(4):
                    P.mm(pa[:, N_], wa[:, kc, :], haT[:, kc, N_], start=(kc == 0), stop=(kc == 3), sig=(kc == 3))
                wb = ring_load(ws["wb"][oc], 8, npart=DH)
                pb_ = bank()
                for h in range(HB):
                    P.mm(pb_[:, N_], wb[:, h, :], hb8[:, h, N_], start=(h == 0), stop=(h == HB - 1), sig=(h == HB - 1))
                P.stt(m1[:, N_], gt16[:, oc, N_], 1.0, pa[:, N_], ALU.add, ALU.mult)
                P.stt(m2[:, N_], gt16[:, 8 + oc, N_], 1.0, pb_[:, N_], ALU.add, ALU.mult)
                P.tt("dve", mergedT[:, oc, N_], m1[:, N_], m2[:, N_], ALU.add)
            for c in range(ntc):
                o = x2_tm[c]
                for hf in range(2):
                    acc = PS[4 + (c % 2) * 2 + hf]
                    for k in range(KC):
                        P.mm(acc, mergedT[:, k, c * 128:(c + 1) * 128], wo[hf][:, k, :], start=(k == 0), stop=(k == KC - 1),
                             sig=(k == KC - 1))
                    P.stt(o[:, hf * 512:(hf + 1) * 512], acc, 0.5 / ALPHA, x1o[c][:, hf * 512:(hf + 1) * 512], ALU.mult, ALU.add)
                layer_norm(o, ln2g, ln2b, small, LN_EPS / (ALPHA * ALPHA))
            transpose_to_fm(x2_tm, xT_all, ntc)
            ffn_ln("ffn2", x2_tm, xT_all, hT, sg, y_tm, ln3g, ln3b, small, ntc)
            for c in range(ntc):
                if sample:
                    P.dma("pool", ys, y_tm[0][0:DEC, :])
                else:
                    P.dma("pool", y_own[t0 + c * 128:t0 + (c + 1) * 128, :], y_tm[c])

        own_tail(0, 1, True)
        if stop_after != "Cs":
            for M in range(NSG):
                own_tail(M * 512, 4, False)
        P.barrier()
    P.barrier()
    es.close()
    return nc, P


def make_consts(g):
    c = {}
    c["c_ident"] = np.eye(128, dtype=np.float32)
    t = np.arange(512)
    c["c_mask64"] = np.broadcast_to((t % 64 != 0).astype(np.float32), (128, 512)).copy()
    s = np.arange(128)[:, None]
    tt = np.arange(128)[None, :]
    m2 = ((s // 64 == tt // 64) & (s <= tt)).astype(np.float32)
    c["c_mask2"] = np.tile(m2, (1, 4))
    c["c_uincl"] = (np.arange(128)[:, None] >= np.arange(128)[None, :]).astype(np.float32)
    sb = np.zeros((16, 128, 512), np.float32)
    for mrel in range(16):
        for j in range(4):
            kpos = 128 * mrel + np.arange(128)[:, None]
            qpos = 128 * (4 * j + g) + np.arange(128)[None, :]
            sb[mrel, :, j * 128:(j + 1) * 128] = (kpos < qpos)
    c["c_sbmask"] = sb
    c["c_smask"] = ((np.arange(128)[:, None] < np.arange(DEC)[None, :])).astype(np.float32)
    fl = np.zeros((128, 4), np.float32)
    fl[:, g] = 1.0
    c["c_flag"] = fl
    return c


def make_in_maps(inp, SEQ, PAST):
    maps = []
    f = lambda a: np.ascontiguousarray(np.asarray(a, dtype=np.float32))
    wts = {k: f(inp[k]).reshape(W_SHAPES[k]) for k in W_SHAPES}
    for c in range(8):
        b, g = c // 4, c % 4
        m = dict(wts)
        m["xb"] = f(inp["x_prompt"][b])
        m["xs"] = f(inp["x_sample"][c])
        m["ck"] = f(inp["cache_sb_k"][0, c]).reshape(PAST, 512)
        m["cv"] = f(inp["cache_sb_v"][0, c]).reshape(PAST, 512)
        m["st0"] = f(inp["state_hgrn"][0, c])
        m.update(make_consts(g))
        maps.append(m)
    return maps


_CACHE = {}


def kernel(**inputs):
    SEQ = inputs["x_prompt"].shape[1]
    PAST = inputs["cache_sb_k"].shape[2]
    key = (SEQ, PAST)
    if key not in _CACHE:
        _CACHE[key] = build(SEQ, PAST)[0]
    nc = _CACHE[key]
    maps = make_in_maps(inputs, SEQ, PAST)
    res = run_bass_kernel_spmd(nc, maps, core_ids=list(range(8)))
    return assemble(res.results, SEQ)


def assemble(r, SEQ):
    NB128 = SEQ // 128
    y_prompt = np.zeros((2, SEQ, D), np.float32)
    for c in range(8):
        b, g = c // 4, c % 4
        yo = np.asarray(r[c]["y_own"]).reshape(NB128 // 4, 128, D)
        y_prompt[b].reshape(NB128 // 4, 4, 128, D)[:, g] = yo
    y_sample = np.stack([np.asarray(r[c]["ys"]) for c in range(8)])
    k_p = np.stack([np.asarray(r[4 * b]["k_all"]).reshape(SEQ, HB, DH) for b in range(2)])[None]
    v_p = np.stack([np.asarray(r[4 * b]["v_all"]).reshape(SEQ, HB, DH) for b in range(2)])[None]
    s_p = np.stack([np.asarray(r[4 * b]["s_fin"]) for b in range(2)])[None]
    k_s = np.stack([np.asarray(r[c]["ks"]).reshape(DEC, HB, DH) for c in range(8)])[None]
    v_s = np.stack([np.asarray(r[c]["vs"]).reshape(DEC, HB, DH) for c in range(8)])[None]
    s_s = np.stack([np.asarray(r[c]["ss"]) for c in range(8)])[None]
    return (y_prompt, y_sample, k_p.astype(np.float32), v_p.astype(np.float32), s_p.astype(np.float32),
            k_s.astype(np.float32), v_s.astype(np.float32), s_s.astype(np.float32))
```

```python
import numpy as np
import os
from contextlib import ExitStack
KNOB = os.environ.get('KNOB', '')
import concourse.bass as bass
import concourse.mybir as mybir
from concourse.bass_utils import run_bass_kernel_spmd

F32 = mybir.dt.float32
BF16 = mybir.dt.bfloat16
AF = mybir.ActivationFunctionType
ALU = mybir.AluOpType

D = 1024
KC = 8
FF = 2816
FC = 22
HA = 4
HB = 8
DH = 64
NIN = 5632
ALPHA = 2.0 ** 0.25
LN_EPS = 1e-5
RMS_EPS = 1e-6
TB = 512
DEC = 16


class Buf:
    __slots__ = ("name", "w", "r", "sem", "accum")

    def __init__(self, name, accum=False):
        self.name = name
        self.w = {}
        self.r = {}
        self.sem = None
        self.accum = accum


class T:
    __slots__ = ("buf", "ap")

    def __init__(self, buf, ap):
        self.buf = buf
        self.ap = ap

    def __getitem__(self, idx):
        return T(self.buf, self.ap[idx])

    def sub(self, idx, name):
        return T(Buf(name), self.ap[idx])

    def re(self, s, **kw):
        return T(self.buf, self.ap.rearrange(s, **kw))


class DSem:
    __slots__ = ("sem", "total", "id")

    def __init__(self, sem, i):
        self.sem = sem
        self.total = 0
        self.id = i


class Prog:
    ENGS = ("pe", "act", "dve", "pool", "sp")

    def __init__(self, nc, es):
        self.nc = nc
        self.es = es
        self.eng = {"pe": nc.tensor, "act": nc.scalar, "dve": nc.vector, "pool": nc.gpsimd, "sp": nc.sync}
        self.sem = {e: es.enter_context(nc.semaphore("sem_" + e)) for e in self.ENGS}
        self.cnt = {e: 0 for e in self.ENGS}
        self.seen = {e: {} for e in self.ENGS}
        self.dsems = []
        self.nwait = 0
        self.nops = 0

    def sb(self, es, name, shape, dtype):
        self.nalloc = getattr(self, "nalloc", 0) + 1
        name = "%s_%d" % (name, self.nalloc)
        nb = int(np.prod(shape[1:])) * (2 if dtype == BF16 else 4)
        self.sb_cur = getattr(self, "sb_cur", 0) + nb
        self.sb_peak = max(getattr(self, "sb_peak", 0), self.sb_cur)
        es.callback(self._free, nb)
        t = es.enter_context(self.nc.sbuf_tensor(name, list(shape), dtype))
        return T(Buf(name), t[:] if hasattr(t, "__getitem__") else t)

    def _free(self, nb):
        self.sb_cur -= nb

    def ps(self, es, name, shape, dtype=F32):
        t = es.enter_context(self.nc.psum_tensor(name, list(shape), dtype))
        return T(Buf(name), t[:])

    def dram(self, name, shape, dtype, kind="Internal", accum=False):
        t = self.nc.dram_tensor(name, list(shape), dtype, kind=kind)
        return T(Buf(name, accum=accum), t.ap())

    def dsem(self, buf, e):
        kind = "sw" if e == "pool" else "hw"
        if buf.sem is None:
            buf.sem = {}
        if kind not in buf.sem:
            s = self.es.enter_context(self.nc.semaphore("dsem%d" % len(self.dsems)))
            buf.sem[kind] = DSem(s, len(self.dsems))
            self.dsems.append(buf.sem[kind])
        return buf.sem[kind]

    def _wait(self, e, key, val):
        if key[0] == "c":
            sem = self.sem[key[1]]
            sid = key[1]
        else:
            sem = key[1].sem
            sid = key[1].id
            val = key[1].total
        if val <= 0:
            return
        if self.seen[e].get(sid, 0) >= val:
            return
        self.seen[e][sid] = val
        self.eng[e].wait_ge(sem, val)
        self.nwait += 1

    def _deps(self, e, reads, writes, is_dma):
        for b in reads:
            for key, val in list(b.w.items()):
                self._wait(e, key, val)
        for b in writes:
            if b.accum:
                continue
            for key, val in list(b.w.items()):
                if key == ("c", "pe") and e == "pe" and not is_dma:
                    continue
                self._wait(e, key, val)
            for key, val in list(b.r.items()):
                self._wait(e, key, val)

    def op(self, e, fn, reads, writes, sig=True):
        rb = [x.buf if isinstance(x, T) else x for x in reads]
        wb = [x.buf if isinstance(x, T) else x for x in writes]
        self._deps(e, rb, wb, False)
        ins = fn()
        if sig:
            self.cnt[e] += 1
            ins.then_inc(self.sem[e], 1)
            val = self.cnt[e]
        else:
            val = self.cnt[e] + 1
        key = ("c", e)
        for b in rb:
            b.r[key] = val
        for b in wb:
            if not b.accum:
                b.w = {key: val}
                b.r = {}
            else:
                b.w[key] = val
        self.nops += 1
        return ins

    def dma(self, e, out, in_, owner=None, **kw):
        ob, ib = out.buf, in_.buf
        if owner is None:
            owner = ib if ob.accum else ob
        ds = self.dsem(owner, e)
        self._deps(e, [ib], [ob], True)
        ins = self.eng[e].dma_start(out=out.ap, in_=in_.ap, **kw)
        ds.total += 16
        ins.then_inc(ds.sem, 16)
        key = ("d", ds)
        ib.r[key] = ds.total
        if ob.accum:
            ob.w[key] = ds.total
        else:
            ob.w = {key: ds.total}
            ob.r = {}
        self.nops += 1
        return ins

    def barrier(self):
        for e in self.ENGS:
            for o in self.ENGS:
                if o != e:
                    self._wait(e, ("c", o), self.cnt[o])
            for ds in self.dsems:
                self._wait(e, ("d", ds), ds.total)

    def mm(self, out, lhsT, rhs, start=True, stop=True, sig=True):
        return self.op("pe", lambda: self.nc.tensor.matmul(out.ap, lhsT.ap, rhs.ap, start=start, stop=stop),
                       [lhsT, rhs], [out], sig=sig)

    def tr(self, out, in_, ident, sig=True):
        return self.op("pe", lambda: self.nc.tensor.transpose(out.ap, in_.ap, ident.ap), [in_, ident], [out], sig=sig)

    def act(self, out, in_, func, bias=None, scale=None, extra_reads=()):
        kw = {}
        rd = [in_] + list(extra_reads)
        if bias is not None:
            if isinstance(bias, T):
                kw["bias"] = bias.ap
                rd.append(bias)
            else:
                kw["bias"] = bias
        if scale is not None:
            if isinstance(scale, T):
                kw["scale"] = scale.ap
                rd.append(scale)
            else:
                kw["scale"] = scale
        return self.op("act", lambda: self.nc.scalar.activation(out.ap, in_.ap, func, **kw), rd, [out])

    def _e(self, e):
        return self.eng[e]

    def tt(self, e, out, in0, in1, op):
        return self.op(e, lambda: self._e(e).tensor_tensor(out.ap, in0.ap, in1.ap, op), [in0, in1], [out])

    def ts(self, e, out, in0, s1, op0, s2=None, op1=None):
        rd = [in0]
        a1 = s1
        a2 = s2
        if isinstance(s1, T):
            rd.append(s1)
            a1 = s1.ap
        if isinstance(s2, T):
            rd.append(s2)
            a2 = s2.ap
        if op1 is None:
            return self.op(e, lambda: self._e(e).tensor_scalar(out.ap, in0.ap, a1, a2, op0), rd, [out])
        return self.op(e, lambda: self._e(e).tensor_scalar(out.ap, in0.ap, a1, a2, op0, op1), rd, [out])

    def stt(self, out, in0, scalar, in1, op0, op1):
        rd = [in0, in1]
        a = scalar
        if isinstance(scalar, T):
            rd.append(scalar)
            a = scalar.ap
        return self.op("dve", lambda: self.nc.vector.scalar_tensor_tensor(out.ap, in0.ap, a, in1.ap, op0, op1), rd, [out])

    def cp(self, e, out, in_):
        if e == "act":
            return self.op(e, lambda: self.nc.scalar.copy(out.ap, in_.ap), [in_], [out])
        return self.op(e, lambda: self._e(e).tensor_copy(out.ap, in_.ap), [in_], [out])

    def memset(self, e, out, val):
        return self.op(e, lambda: self._e(e).memset(out.ap, val), [], [out])


W_SHAPES = {
    "ffn1_wg": (D, FF), "ffn1_wu": (D, FF), "ffn1_wd": (FF, D), "ln1_g": (1, D), "ln1_b": (1, D),
    "w_in": (D, NIN), "lb_logits": (2, 512), "hgrn_norm_g": (1, 512),
    "w_branch_a": (512, D), "w_branch_b": (512, D), "w_out": (D, D), "ln2_g": (1, D), "ln2_b": (1, D),
    "ffn2_wg": (D, FF), "ffn2_wu": (D, FF), "ffn2_wd": (FF, D), "ln3_g": (1, D), "ln3_b": (1, D),
}
ST_BLOCKS = {"qa": (0, 4), "fa": (512, 4), "ga": (1536, 4), "qb": (2048, 4), "gta": (3584, 8), "gtb": (4608, 8)}
MV_BLOCKS = {"ia": 1024, "kb": 2560, "vb": 3072}


def build(SEQ, PAST, stop_after=None):
    NG = SEQ // TB
    NSG = SEQ // 2048
    NOWN = SEQ // 4
    NKB = SEQ // 128
    NPB = PAST // 128
    SKS = PAST + 128
    nc = bass.Bass("TRN2", target_bir_lowering=False)
    es = ExitStack()
    P = Prog(nc, es)
    IN = lambda n, s: P.dram(n, s, F32, "ExternalInput")
    OUT = lambda n, s: P.dram(n, s, F32, "ExternalOutput", accum=True)
    xb = IN("xb", [SEQ, D])
    xs = IN("xs", [DEC, D])
    ck = IN("ck", [PAST, 512])
    cv = IN("cv", [PAST, 512])
    st0 = IN("st0", [HA, 128, 128])
    Wd = {n: IN(n, list(s)) for n, s in W_SHAPES.items()}
    c_ident = IN("c_ident", [128, 128])
    c_mask64 = IN("c_mask64", [128, 512])
    c_mask2 = IN("c_mask2", [128, 512])
    c_uincl = IN("c_uincl", [128, 128])
    c_sbmask = IN("c_sbmask", [16, 128, 512])
    c_smask = IN("c_smask", [128, DEC])
    c_flag = IN("c_flag", [128, 4])
    y_own = OUT("y_own", [NOWN, D])
    k_all = OUT("k_all", [SEQ, 512])
    v_all = OUT("v_all", [SEQ, 512])
    s_fin = OUT("s_fin", [HA, 128, 128])
    ys = OUT("ys", [DEC, D])
    ks = OUT("ks", [DEC, 512])
    vs = OUT("vs", [DEC, 512])
    ss = OUT("ss", [HA, 128, 128])
    dbg = OUT("dbg", [SEQ, D]) if stop_after else None

    SCR = lambda n, s, dt=BF16: P.dram(n, s, dt, "Internal", accum=True)
    ws = {}
    for l in ("ffn1", "ffn2"):
        ws[l + "_wg"] = SCR(l + "_wg_s", [FC, 128, KC * 128])
        ws[l + "_wu"] = SCR(l + "_wu_s", [FC, 128, KC * 128])
        ws[l + "_wd"] = SCR(l + "_wd_s", [FC, 128, D])
    for n, (c0, noc) in ST_BLOCKS.items():
        ws[n] = SCR("win_" + n, [noc, 128, KC * 128])
    for n, c0 in MV_BLOCKS.items():
        ws[n] = SCR("win_" + n, [128, KC, 512])
    ws["wa"] = SCR("wa_s", [8, 128, 4 * 128])
    ws["wb"] = SCR("wb_s", [8, 64, 8 * 128])
    ws["wo"] = SCR("wo_s", [2, 128, KC, 512])
    KT_scr = SCR("KT_scr", [HB, DH, SEQ])
    V_scr = SCR("V_scr", [HB, 128, NKB, DH])
    QT_scr = SCR("QT_scr", [HB, DH, NOWN])
    GA_scr = SCR("GA_scr", [4, 128, NOWN])
    GT_scr = SCR("GT_scr", [16, 128, NOWN])
    OT_scr = SCR("OT_scr", [HA, 128, NOWN], F32)
    X1_scr = SCR("X1_scr", [NOWN, D], F32)
    HB_scr = SCR("HB_scr", [HB, DH, NOWN])
    KTs_scr = SCR("KTs_scr", [HB, DH, SKS])
    Vs_scr = SCR("Vs_scr", [HB, 128, NPB + 1, DH])
    QTs_scr = SCR("QTs_scr", [HB, DH, 128])
    GAs_scr = SCR("GAs_scr", [4, 128, 128])
    GTs_scr = SCR("GTs_scr", [16, 128, 128])
    OTs_scr = SCR("OTs_scr", [HA, 128, 128], F32)
    X1s_scr = SCR("X1s_scr", [128, D], F32)
    HBs_scr = SCR("HBs_scr", [HB, DH, 128])

    def prep_weights():
        q = "pool"
        kpc = lambda t: t.re("p (k c) -> p k c", c=128)
        for l in ("ffn1", "ffn2"):
            for f in range(FC):
                for nm in ("_wg", "_wu"):
                    P.dma(q, kpc(ws[l + nm][f]), Wd[l + nm][:, f * 128:(f + 1) * 128].re("(k p) c -> p k c", p=128))
                P.dma(q, ws[l + "_wd"][f], Wd[l + "_wd"][f * 128:(f + 1) * 128, :])
        for n, (c0, noc) in ST_BLOCKS.items():
            for oc in range(noc):
                P.dma(q, kpc(ws[n][oc]), Wd["w_in"][:, c0 + oc * 128:c0 + (oc + 1) * 128].re("(k p) c -> p k c", p=128))
        for n, c0 in MV_BLOCKS.items():
            P.dma(q, ws[n], Wd["w_in"][:, c0:c0 + 512].re("(k p) c -> p k c", p=128))
        for oc in range(8):
            P.dma(q, kpc(ws["wa"][oc]), Wd["w_branch_a"][:, oc * 128:(oc + 1) * 128].re("(k p) c -> p k c", p=128))
            P.dma(q, kpc(ws["wb"][oc]), Wd["w_branch_b"][:, oc * 128:(oc + 1) * 128].re("(h p) c -> p h c", p=64))
        for hf in range(2):
            P.dma(q, ws["wo"][hf], Wd["w_out"][:, hf * 512:(hf + 1) * 512].re("(k p) c -> p k c", p=128))
        for h in range(HB):
            for b0 in range(0, NPB, 8):
                b1 = min(NPB, b0 + 8)
                P.dma(q, Vs_scr[h, :, b0:b1, :],
                      cv[b0 * 128:b1 * 128, h * DH:(h + 1) * DH].re("(b p) d -> p b d", p=128))

    prep_weights()

    psum = es.enter_context(nc.psum_tensor("psum", [128, 8 * 512], F32))
    PS = [T(Buf("ps%d" % i), psum[:, i * 512:(i + 1) * 512]) for i in range(8)]
    ident = P.sb(es, "ident", [128, 128], F32)
    P.dma("sp", ident, c_ident)
    flag = P.sb(es, "flag", [128, 4], F32)
    P.dma("sp", flag, c_flag)
    NRING = 6
    ring = [P.sb(es, "ring%d" % i, [128, 1024], BF16) for i in range(NRING)]
    ring_i = [0]

    def ring_load(src, k=None, npart=128):
        slot = ring[ring_i[0] % NRING]
        ring_i[0] += 1
        n = src.ap.shape[-1]
        P.dma("sp", slot[0:npart, 0:n], src)
        v = slot[0:npart, 0:n]
        return v if k is None else v.re("p (k c) -> p k c", k=k)

    bank_i = [0]

    def bank(lo=0, hi=4):
        b = PS[lo + bank_i[0] % (hi - lo)]
        bank_i[0] += 1
        return b

    ev_i = [0]

    def evac(out, in_):
        e = "act" if ev_i[0] % 2 == 0 else "dve"
        ev_i[0] += 1
        P.cp(e, out, in_)

    def transpose_to_fm(src_tm, dstT, ntc, nk=KC):
        for k in range(nk):
            b = bank()
            for c in range(ntc):
                P.tr(b[:, c * 128:(c + 1) * 128], src_tm[c][:, k * 128:(k + 1) * 128], ident, sig=(c == ntc - 1))
            evac(dstT[:, k, 0:ntc * 128], b[:, 0:ntc * 128])

    def layer_norm(o, lng, lnb, small, eps):
        st, mv, ve, rstd, mhalf = small
        P.op("dve", lambda: nc.vector.bn_stats(st.ap[:, 0:6], o.ap[:, 0:512]), [o], [st])
        P.op("dve", lambda: nc.vector.bn_stats(st.ap[:, 6:12], o.ap[:, 512:1024]), [o], [st])
        P.op("dve", lambda: nc.vector.bn_aggr(mv.ap, st.ap), [st], [mv])
        P.ts("pool", ve, mv[:, 1:2], eps, ALU.add)
        P.tt("pool", rstd, ve, mhalf, ALU.pow)
        P.ts("dve", o, o, mv[:, 0:1], ALU.subtract, rstd, ALU.mult)
        P.tt("dve", o, o, lng, ALU.mult)
        P.tt("dve", o, o, lnb, ALU.add)

    def ffn_ln(pref, x_tm, xT, hT, sg, out_tm, lng, lnb, small, ntc):
        NT = ntc * 128
        wg, wu, wd = ws[pref + "_wg"], ws[pref + "_wu"], ws[pref + "_wd"]
        for f in range(FC):
            tg = ring_load(wg[f], KC)
            tu = ring_load(wu[f], KC)
            bg = bank()
            bu = bank()
            for k in range(KC):
                P.mm(bg[:, 0:NT], tg[:, k, :], xT[:, k, 0:NT], start=(k == 0), stop=(k == KC - 1), sig=(k == KC - 1))
            for k in range(KC):
                P.mm(bu[:, 0:NT], tu[:, k, :], xT[:, k, 0:NT], start=(k == 0), stop=(k == KC - 1), sig=(k == KC - 1))
            s = sg[f % 2]
            P.act(s[:, 0:NT], bg[:, 0:NT], AF.Silu)
            P.tt("dve", hT[f][:, 0:NT], s[:, 0:NT], bu[:, 0:NT], ALU.mult)
        for p0 in range(0, ntc, 2):
            cs = list(range(p0, min(p0 + 2, ntc)))
            ab = 4 if p0 == 0 else 0
            for f in range(FC):
                td = ring_load(wd[f])
                for c in cs:
                    for hf in range(2):
                        P.mm(PS[ab + (c % 2) * 2 + hf], hT[f][:, c * 128:(c + 1) * 128], td[:, hf * 512:(hf + 1) * 512],
                             start=(f == 0), stop=(f == FC - 1))
            for c in cs:
                o = out_tm[c]
                for hf in range(2):
                    P.stt(o[:, hf * 512:(hf + 1) * 512], PS[ab + (c % 2) * 2 + hf], 0.5 / ALPHA,
                          x_tm[c][:, hf * 512:(hf + 1) * 512], ALU.mult, ALU.add)
            for c in cs:
                layer_norm(out_tm[c], lng, lnb, small, LN_EPS / (ALPHA * ALPHA))

    def bcast_load(es_, name, src, n):
        t = P.sb(es_, name, [128, n], F32)
        P.dma("sp", t, T(src.buf, src.ap[0:1, :].partition_broadcast(128)))
        return t

    def mk_small(es_):
        st = P.sb(es_, "bnst", [128, 12], F32)
        mv = P.sb(es_, "bnmv", [128, 2], F32)
        ve = P.sb(es_, "bnve", [128, 1], F32)
        rstd = P.sb(es_, "bnrstd", [128, 1], F32)
        mhalf = P.sb(es_, "mhalf", [128, 1], F32)
        P.memset("pool", mhalf, -0.5)
        return (st, mv, ve, rstd, mhalf)

    with ExitStack() as ea:
        x_tm = [P.sb(ea, "x_tm%d" % c, [128, D], F32) for c in range(4)]
        x1_tm = [P.sb(ea, "x1_tm%d" % c, [128, D], F32) for c in range(4)]
        xT_all = P.sb(ea, "xT", [128, KC, TB], BF16)
        x1T_all = P.sb(ea, "x1T", [128, KC, TB], BF16)
        hT_all = P.sb(ea, "hT", [128, FC, TB], BF16)
        hT = [hT_all.sub((slice(None), f, slice(None)), "hT%d" % f) for f in range(FC)]
        sg = [P.sb(ea, "sg%d" % i, [128, TB], F32) for i in range(2)]
        ln1g = bcast_load(ea, "ln1g", Wd["ln1_g"], D)
        ln1b = bcast_load(ea, "ln1b", Wd["ln1_b"], D)
        small = mk_small(ea)
        wia = P.sb(ea, "wia", [128, KC, 512], BF16)
        wkb = P.sb(ea, "wkb", [128, KC, 512], BF16)
        wvb = P.sb(ea, "wvb", [128, KC, 512], BF16)
        P.dma("sp", wia, ws["ia"])
        P.dma("sp", wkb, ws["kb"])
        P.dma("sp", wvb, ws["vb"])
        mask64 = P.sb(ea, "mask64", [128, 512], F32)
        mask2 = P.sb(ea, "mask2", [128, 512], F32)
        P.dma("sp", mask64, c_mask64)
        P.dma("sp", mask2, c_mask2)
        lbl = P.sb(ea, "lbl", [128, 2, 4], F32)
        P.dma("sp", lbl, Wd["lb_logits"].re("r (h p) -> p r h", p=128), allow_slow_non_contiguous=True)
        oml = P.sb(ea, "oml", [128, 4], F32)
        noml = P.sb(ea, "noml", [128, 4], F32)
        lbd = P.sb(ea, "lbd", [128, 4], F32)
        P.tt("dve", lbd, lbl[:, 0, :], lbl[:, 1, :], ALU.subtract)
        P.act(lbd, lbd, AF.Exp)
        P.ts("dve", lbd, lbd, 1.0, ALU.add)
        P.op("dve", lambda: nc.vector.reciprocal(oml.ap, lbd.ap), [lbd], [oml])
        P.ts("dve", noml, oml, -1.0, ALU.mult)
        vtm = P.sb(ea, "vtm", [128, 4, 512], BF16)
        ktm = [P.sb(ea, "ktm%d" % i, [128, 512], F32) for i in range(4)]
        vbtm = [P.sb(ea, "vbtm%d" % i, [128, 512], F32) for i in range(2)]
        kTs = [P.sb(ea, "kTs%d" % i, [128, 512], F32) for i in range(2)]
        hqT = [P.sb(ea, "hqT%d" % h, [128, TB], F32) for h in range(HA)]
        he = P.sb(ea, "he", [128, TB], F32)
        hr = P.sb(ea, "hr", [128, TB], F32)
        hlf = P.sb(ea, "hlf", [128, TB], F32)
        hb = P.sb(ea, "hb", [128, TB], F32)
        hebn = P.sb(ea, "hebn", [128, TB], F32)
        qt2 = [P.sb(ea, "qt%d" % i, [128, TB], BF16) for i in range(2)]
        kt2 = [P.sb(ea, "kt%d" % i, [128, TB], BF16) for i in range(2)]
        kdT2 = [P.sb(ea, "kdT%d" % i, [128, TB], F32) for i in range(2)]
        heb2 = [P.sb(ea, "heb%d" % i, [128, TB], F32) for i in range(2)]
        kd_tm = P.sb(ea, "kd_tm", [128, 4, 128], BF16)
        AT = P.sb(ea, "AT", [128, TB], BF16)
        Sst = [[P.sb(ea, "S%d_%d" % (h, i), [128, 128], F32) for i in range(2)] for h in range(HA)]
        Ssm = [[P.sb(ea, "Ss%d_%d" % (h, i), [128, 128], F32) for i in range(2)] for h in range(HA)]
        Sb = [P.sb(ea, "Sb%d" % i, [128, 128], BF16) for i in range(2)]
        sb_i = [0]
        oTown = P.sb(ea, "oTown", [128, HA, 128], F32)
        x1own = P.sb(ea, "x1own", [128, D], F32)
        x1Town = P.sb(ea, "x1Town", [128, KC, TB], BF16)
        pj = [P.sb(ea, "pj%d" % i, [128, TB], BF16) for i in range(2)]
        pj_i = [0]
        for h in range(HA):
            P.memset("pool", Sst[h][0], 0.0)

        def kT_path(src_tm_list, dst_scr, col0, nblk):
            for oc in range(4):
                b = bank(4, 8)
                for i, src in enumerate(src_tm_list):
                    P.tr(b[:, i * 128:(i + 1) * 128], src[:, oc * 128:(oc + 1) * 128], ident, sig=(i == nblk - 1))
                kk = kTs[oc % 2]
                evac(kk[:, 0:nblk * 128], b[:, 0:nblk * 128])
                P.dma("pool", dst_scr[2 * oc:2 * oc + 2, :, col0:col0 + nblk * 128].re("h d s -> (h d) s"),
                      kk[:, 0:nblk * 128])

        def group_tail(gi, ntc, sample):
            NT = ntc * 128
            S = Ssm if sample else Sst
            ktl = []
            halves = [list(range(ntc))] if ntc < 4 else [[0, 1], [2, 3]]
            for c in [cc for hv in halves for cc in ["T%d" % halves.index(hv)] + hv]:
                if isinstance(c, str):
                    hv = halves[int(c[1:])]
                    for k in range(KC):
                        bT = bank()
                        for ii, cc in enumerate(hv):
                            P.tr(bT[:, ii * 128:(ii + 1) * 128], x1_tm[cc][:, k * 128:(k + 1) * 128], ident, sig=(ii == len(hv) - 1))
                        evac(x1T_all[:, k, hv[0] * 128:(hv[-1] + 1) * 128], bT[:, 0:len(hv) * 128])
                    continue
                csl = slice(c * 128, (c + 1) * 128)
                b = bank()
                for k in range(KC):
                    P.mm(b, x1T_all[:, k, csl], wia[:, k, :], start=(k == 0), stop=(k == KC - 1), sig=(k == KC - 1))
                evac(vtm[:, c, :], b)
                b = bank()
                for k in range(KC):
                    P.mm(b, x1T_all[:, k, csl], wkb[:, k, :], start=(k == 0), stop=(k == KC - 1), sig=(k == KC - 1))
                kk = ktm[c % 4]
                ktl.append(kk)
                evac(kk, b)
                b = bank()
                for k in range(KC):
                    P.mm(b, x1T_all[:, k, csl], wvb[:, k, :], start=(k == 0), stop=(k == KC - 1), sig=(k == KC - 1))
                vv = vbtm[c % 2]
                evac(vv, b)
                if sample:
                    P.dma("pool", ks, kk[0:DEC, :])
                    P.dma("pool", vs, vv[0:DEC, :])
                    P.dma("pool", Vs_scr[:, :, NPB, :].re("h p d -> p h d"), vv.re("p (h d) -> p h d", d=DH))
                    kT_path([kk], KTs_scr, PAST, 1)
                else:
                    r0 = gi * TB + c * 128
                    P.dma("pool", k_all[r0:r0 + 128, :], kk)
                    P.dma("pool", v_all[r0:r0 + 128, :], vv)
                    P.dma("pool", V_scr[:, :, gi * 4 + c, :].re("h p d -> p h d"), vv.re("p (h d) -> p h d", d=DH))
            if not sample:
                kT_path(ktl, KT_scr, gi * TB, ntc)
            for h in range(HA):
                tq = ring_load(ws["qa"][h], KC)
                b = bank()
                for k in range(KC):
                    P.mm(b[:, 0:NT], tq[:, k, :], x1T_all[:, k, 0:NT], start=(k == 0), stop=(k == KC - 1), sig=(k == KC - 1))
                P.act(hqT[h][:, 0:NT], b[:, 0:NT], AF.Silu)
            chunks = [(0, DEC)] if sample else [(c * 64, 64) for c in range(NT // 64)]
            N_ = slice(0, NT)

            def hg_front(h):
                hs = h % 2
                tf = ring_load(ws["fa"][h], KC)
                zf = bank()
                for k in range(KC):
                    P.mm(zf[:, 0:NT], tf[:, k, :], x1T_all[:, k, 0:NT], start=(k == 0), stop=(k == KC - 1), sig=(k == KC - 1))
                N_ = slice(0, NT)
                P.act(he[:, N_], zf[:, N_], AF.Exp)
                yield
                P.act(he[:, N_], he[:, N_], AF.Ln, bias=1.0)
                P.act(hr[:, N_], he[:, N_], AF.Exp, scale=-1.0)
                P.act(hlf[:, N_], hr[:, N_], AF.Ln, bias=1.0, scale=noml[:, h:h + 1])
                yield
                P.op("dve", lambda: nc.vector.tensor_tensor_scan(hb.ap[:, N_], mask64.ap[:, N_], hlf.ap[:, N_], 0.0,
                                                                  ALU.mult, ALU.add), [mask64, hlf], [hb])
                yield
                P.act(heb2[hs][:, N_], hb[:, N_], AF.Exp)
                P.act(hebn[:, N_], hb[:, N_], AF.Exp, scale=-1.0)
                yield
                P.tt("dve", qt2[hs][:, N_], hqT[h][:, N_], heb2[hs][:, N_], ALU.mult)
                P.stt(kt2[hs][:, N_], hr[:, N_], oml[:, h:h + 1], hebn[:, N_], ALU.mult, ALU.mult)
                if sample:
                    P.memset("pool", kdT2[hs][:, DEC:128], 0.0)
                    P.ts("dve", kdT2[hs][:, 0:DEC], kt2[hs][:, 0:DEC], heb2[hs][:, DEC - 1:DEC], ALU.mult)
                else:
                    nch = NT // 64
                    lastb = T(heb2[hs].buf, heb2[hs].ap[:, N_].rearrange("p (c t) -> p c t", t=64)[:, :, 63:64].to_broadcast([128, nch, 64]))
                    P.tt("dve", kdT2[hs][:, N_].re("p (c t) -> p c t", t=64), kt2[hs][:, N_].re("p (c t) -> p c t", t=64), lastb, ALU.mult)
                yield

            def hg_back(h):
                hs = h % 2
                b = bank()
                for c in range(ntc):
                    P.tr(b[:, c * 128:(c + 1) * 128], kdT2[hs][:, c * 128:(c + 1) * 128], ident, sig=(c == ntc - 1))
                evac(kd_tm[:, 0:ntc, :], b[:, 0:NT].re("p (c k) -> p c k", k=128))
                yield
                b = bank()
                for c in range(ntc):
                    csl = slice(c * 128, (c + 1) * 128)
                    P.mm(b[:, csl], kt2[hs][:, csl], qt2[hs][:, csl], start=True, stop=True, sig=(c == ntc - 1))
                P.tt("dve", AT[:, N_], b[:, N_], mask2[:, N_], ALU.mult)
                yield
                pb = [PS[6], PS[7]]
                for ci, (c0, cl) in enumerate(chunks):
                    prt = slice(c0 % 128, c0 % 128 + cl)
                    P.mm(pb[ci % 2][:, (ci // 2) * 128:(ci // 2 + 1) * 128], kd_tm[prt, c0 // 128, :],
                         vtm[prt, c0 // 128, h * 128:(h + 1) * 128], start=True, stop=True)
                ob = PS[4 + h % 2]
                for c in range(ntc):
                    csl = slice(c * 128, (c + 1) * 128)
                    P.mm(ob[:, csl], vtm[:, c, h * 128:(h + 1) * 128], AT[:, csl], start=(c == 0), stop=False)
                cur = 0 if not sample else 0
                for ci, (c0, cl) in enumerate(chunks):
                    s_in = S[h][cur_state[(sample, h)]]
                    s_out = S[h][1 - cur_state[(sample, h)]]
                    sbf = Sb[sb_i[0] % 2]
                    sb_i[0] += 1
                    P.cp("act", sbf, s_in)
                    cw = 64 if not sample else 64
                    P.mm(ob[:, c0:c0 + cw], sbf, qt2[hs][:, c0:c0 + cw], start=False, stop=(ci == len(chunks) - 1))
                    P.stt(s_out, s_in, heb2[hs][:, c0 + cl - 1:c0 + cl], pb[ci % 2][:, (ci // 2) * 128:(ci // 2 + 1) * 128],
                          ALU.mult, ALU.add)
                    cur_state[(sample, h)] = 1 - cur_state[(sample, h)]
                    yield
                if sample:
                    P.cp("dve", oTown[:, h, :], ob[:, 0:128])
                else:
                    P.ts("dve", oTown[:, h, :], ob[:, 0:128], flag[:, 0:1], ALU.mult)
                    for r in range(1, 4):
                        P.stt(oTown[:, h, :], ob[:, r * 128:(r + 1) * 128], flag[:, r:r + 1], oTown[:, h, :], ALU.mult, ALU.add)
                yield

            def run2(ga, gb):
                la, lb = ga is not None, gb is not None
                while la or lb:
                    if lb:
                        try:
                            next(gb)
                        except StopIteration:
                            lb = False
                    if la:
                        try:
                            next(ga)
                        except StopIteration:
                            la = False

            run2(hg_front(0), None)
            for h in range(HA):
                run2(hg_front(h + 1) if h + 1 < HA else None, hg_back(h))
            if sample:
                P.dma("pool", OTs_scr.re("h v t -> v h t"), oTown)
                P.dma("pool", X1s_scr, x1_tm[0])
                P.cp("pool", x1Town[:, :, 0:128], x1T_all[:, :, 0:128])
                own_proj(0, 128, True)
            elif 'nosel' in KNOB:
                pass
            else:
                P.dma("pool", OT_scr[:, :, gi * 128:(gi + 1) * 128].re("h v t -> v h t"), oTown)
                P.ts("dve", x1own, x1_tm[0], flag[:, 0:1], ALU.mult)
                for r in range(1, 4):
                    P.stt(x1own, x1_tm[r], flag[:, r:r + 1], x1own, ALU.mult, ALU.add)
                P.dma("pool", X1_scr[gi * 128:(gi + 1) * 128, :], x1own)
                dst = x1Town[:, :, (gi % 4) * 128:(gi % 4 + 1) * 128]
                P.ts("dve", dst, x1T_all[:, :, 0:128], flag[:, 0:1], ALU.mult)
                for r in range(1, 4):
                    P.stt(dst, x1T_all[:, :, r * 128:(r + 1) * 128], flag[:, r:r + 1], dst, ALU.mult, ALU.add)
                if gi % 4 == 3:
                    own_proj((gi // 4) * 512, 512, False)

        cur_state = {(s_, h): 0 for s_ in (False, True) for h in range(HA)}

        def own_proj(t0, N, sample):
            QT, GA, GT = (QTs_scr, GAs_scr, GTs_scr) if sample else (QT_scr, GA_scr, GT_scr)
            for (nm, noc) in (("qb", 4), ("ga", 4), ("gta", 8), ("gtb", 8)):
                for oc in range(noc):
                    tw = ring_load(ws[nm][oc], KC)
                    b = bank()
                    for k in range(KC):
                        P.mm(b[:, 0:N], tw[:, k, :], x1Town[:, k, 0:N], start=(k == 0), stop=(k == KC - 1), sig=(k == KC - 1))
                    o = pj[pj_i[0] % 2]
                    pj_i[0] += 1
                    if nm == "qb":
                        P.act(o[:, 0:N], b[:, 0:N], AF.Copy, scale=0.125)
                        P.dma("pool", QT[2 * oc:2 * oc + 2, :, t0:t0 + N].re("h d s -> (h d) s"), o[:, 0:N])
                    elif nm == "ga":
                        P.act(o[:, 0:N], b[:, 0:N], AF.Silu)
                        P.dma("pool", GA[oc, :, t0:t0 + N], o[:, 0:N])
                    else:
                        P.act(o[:, 0:N], b[:, 0:N], AF.Tanh, scale=0.5)
                        P.dma("pool", GT[oc + (8 if nm == "gtb" else 0), :, t0:t0 + N], o[:, 0:N])

        P.memset("pool", x_tm[0], 0.0)
        P.dma("sp", x_tm[0][0:DEC, :], xs)
        for h in range(HA):
            P.dma("sp", Ssm[h][0], st0[h])
        transpose_to_fm(x_tm, xT_all, 1)
        ffn_ln("ffn1", x_tm, xT_all, hT, sg, x1_tm, ln1g, ln1b, small, 1)
        def early_exit():
            P.barrier()
            ea.close()
            P.barrier()
            es.close()
            return nc, P
        if stop_after == "s_ffn":
            P.dma("pool", dbg[0:128, :], x1_tm[0])
            return early_exit()
        group_tail(0, 1, True)
        for h in range(HA):
            P.dma("pool", ss[h], Ssm[h][cur_state[(True, h)]])
        if stop_after == "s_tail":
            return early_exit()
        for b0 in range(0, NPB, 4):
            tiles = []
            nb = min(4, NPB - b0)
            for i in range(nb):
                P.dma("sp", x1_tm[i][:, 0:512], ck[(b0 + i) * 128:(b0 + i + 1) * 128, :])
                tiles.append(x1_tm[i][:, 0:512])
            kT_path(tiles, KTs_scr, b0 * 128, nb)

        if stop_after == "past":
            return early_exit()
        for gi in range(NG):
            if stop_after == "A1" and gi == 1:
                return early_exit()
            for c in range(4):
                P.dma("sp", x_tm[c], xb[gi * TB + c * 128: gi * TB + (c + 1) * 128, :])
            transpose_to_fm(x_tm, xT_all, 4)
            ffn_ln("ffn1", x_tm, xT_all, hT, sg, x1_tm, ln1g, ln1b, small, 4)
            if stop_after == "ffn1":
                for c in range(4):
                    P.dma("pool", dbg[gi * TB + c * 128: gi * TB + (c + 1) * 128, :], x1_tm[c])
                continue
            group_tail(gi, 4, False)
        for h in range(HA):
            P.dma("pool", s_fin[h], Sst[h][cur_state[(False, h)]])
        P.barrier()
    if stop_after in ("ffn1", "A"):
        P.barrier()
        es.close()
        return nc, P

    with ExitStack() as eb_:
        uincl = P.sb(eb_, "uincl", [128, 128], BF16)
        P.dma("pool", uincl, c_uincl)
        onesb = P.sb(eb_, "onesb", [128, 128], BF16)
        P.memset("pool", onesb, 1.0)
        sbm = P.sb(eb_, "sbm", [128, 16, 512], BF16)
        for m0 in range(0, 16, 4):
            P.dma("pool", sbm[:, m0:m0 + 4, :], c_sbmask[m0:m0 + 4].re("m p t -> p m t"))
        smk = P.sb(eb_, "smk", [128, DEC], BF16)
        P.dma("pool", smk, c_smask)
        KTh = [P.sb(eb_, "KTh%d" % i, [DH, max(SEQ, SKS)], BF16) for i in range(2)]
        Vh = [P.sb(eb_, "Vh%d" % i, [128, max(NKB, NPB + 1), DH], BF16) for i in range(2)]
        QTh = [P.sb(eb_, "QTh%d" % i, [DH, max(NOWN, 128)], BF16) for i in range(2)]
        E = [P.sb(eb_, "sbE%d" % i, [128, 128], F32) for i in range(4)]
        SPt = [P.sb(eb_, "sbSP%d" % i, [128, 128], BF16) for i in range(4)]
        SPm = [P.sb(eb_, "sbSPm%d" % i, [128, 128], BF16) for i in range(4)]
        Em = [P.sb(eb_, "sbEm%d" % i, [128, 128], F32) for i in range(4)]
        E2 = [P.sb(eb_, "sbE2%d" % i, [128, 128], F32) for i in range(2)]
        Wt = [P.sb(eb_, "sbW%d" % i, [128, 128], BF16) for i in range(2)]
        R32 = [P.sb(eb_, "sbR32_%d" % i, [128, 128], F32) for i in range(2)]
        Rb = [P.sb(eb_, "sbRb%d" % i, [128, 128], BF16) for i in range(4)]
        hbo = [P.sb(eb_, "hbo%d" % i, [DH, 512], BF16) for i in range(2)]
        for i in range(2):
            P.memset("pool", hbo[i], 0.0)
        ti = [0]
        oi = [0]

        def sb_attend(KT, V, QT, nq, kblocks, dst):
            Q_ = slice(0, nq)
            n = len(kblocks)
            outb = PS[6 + oi[0] % 2]
            ho = hbo[oi[0] % 2]
            oi[0] += 1
            base = ti[0]
            ti[0] += n
            spm_of, em_of = {}, {}

            def stA(i):
                kb = kblocks[i][0]
                P.mm(PS[(base + i) % 3][:, Q_], KT[:, kb * 128:(kb + 1) * 128], QT[:, Q_])

            def stB(i):
                mk = kblocks[i][1]
                j = base + i
                e = E[j % 4]
                P.act(e[:, Q_], PS[j % 3][:, Q_], AF.Exp)
                sp = SPt[j % 4]
                P.act(sp[:, Q_], e[:, Q_], AF.Ln, bias=1.0)
                if mk is not None:
                    spm = SPm[j % 4]
                    P.tt("dve", spm[:, Q_], sp[:, Q_], mk, ALU.mult)
                    em = Em[j % 4]
                    P.tt("dve", em[:, Q_], e[:, Q_], mk, ALU.mult)
                else:
                    spm, em = sp, e
                spm_of[i], em_of[i] = spm, em

            def stC(i):
                c = PS[3 + (base + i) % 3]
                ops = [(uincl, spm_of[i])]
                if i >= 1:
                    ops.append((onesb, spm_of[i - 1]))
                if i >= 2:
                    ops.append((onesb, Rb[(base + i - 2) % 4]))
                for oi_, (l, r_) in enumerate(ops):
                    P.mm(c[:, Q_], l, r_[:, Q_], start=(oi_ == 0), stop=(oi_ == len(ops) - 1), sig=(oi_ == len(ops) - 1))

            def stR(i):
                if i + 2 > n - 1:
                    return
                if i == 0:
                    P.cp("pool", R32[(base + i) % 2][:, Q_], spm_of[0][:, Q_])
                else:
                    P.tt("pool", R32[(base + i) % 2][:, Q_], R32[(base + i - 1) % 2][:, Q_], spm_of[i][:, Q_], ALU.add)

            def stRc(i):
                if i + 2 > n - 1:
                    return
                P.cp("dve", Rb[(base + i) % 4][:, Q_], R32[(base + i) % 2][:, Q_])

            def stD(i):
                j = base + i
                P.act(E2[j % 2][:, Q_], PS[3 + j % 3][:, Q_], AF.Exp, scale=-1.0)

            def stE(i):
                j = base + i
                P.tt("dve", Wt[j % 2][:, Q_], em_of[i][:, Q_], E2[j % 2][:, Q_], ALU.mult)

            def stF(i):
                kb = kblocks[i][0]
                P.mm(outb[0:DH, Q_], V[:, kb, :], Wt[(base + i) % 2][:, Q_], start=(i == 0), stop=(i == n - 1))

            for t in range(-3, n + 2):
                for fn, off in ((stA, 3), (stB, 2), (stRc, -1), (stC, 1), (stR, 0), (stD, 0), (stE, -1), (stF, -1)):
                    i = t + off
                    if 0 <= i < n:
                        fn(i)
            P.cp("act", ho[:, Q_], outb[0:DH, Q_])
            P.dma("pool", dst, ho[:, 0:max(nq, 128)])

        ZP = [T(Buf("zp%d" % m), psum[:, 2 * m * 512:(2 * m + 2) * 512]) for m in range(3)]
        E2p = [P.sb(eb_, "pE%d" % i, [128, 1024], F32) for i in range(2)]
        SP2 = [P.sb(eb_, "pSP%d" % i, [128, 1024], BF16) for i in range(3)]
        S2 = [P.sb(eb_, "pS2%d" % i, [128, 512], BF16) for i in range(4)]
        W2 = [P.sb(eb_, "pW%d" % i, [128, 1024], BF16) for i in range(3)]
        Rb2 = [P.sb(eb_, "pRb%d" % i, [128, 512], BF16) for i in range(3)]
        R2 = [P.sb(eb_, "pR%d" % i, [128, 512], F32) for i in range(2)]
        qn = [P.sb(eb_, "qn%d" % i, [DH, 512], BF16) for i in range(2)]
        pi_ = [0]

        def sb_pipeline(jobs):
            H = (slice(0, 512), slice(512, 1024))
            look = []
            for jn, jb in enumerate(jobs):
                jb["np"] = len(jb["kblocks"]) // 2
                jb["outb"] = PS[6 + jn % 2]
                jb["ho"] = hbo[jn % 2]
                jb["qneg"] = qn[jn % 2]
                for i in range(jb["np"]):
                    look.append((jb, i))
            NP = len(look)

            def kbs(jb, i):
                return jb["kblocks"][2 * i], jb["kblocks"][2 * i + 1]

            def c0_of(jb, i):
                if i < 0:
                    return 512
                (kb, mk), _ = kbs(jb, i)
                if mk is None:
                    return 0
                rel = jb["rel0"] - 2 * i
                return 128 * max(0, (rel - 3 + 3) // 4) if rel > 3 else 0

            def both(t, c0):
                return t.re("p (h c) -> p h c", h=2)[:, :, c0:512]

            def stA(G):
                jb, i = look[G]
                if i == 0:
                    if jb.get("pre"):
                        jb["pre"]()
                    P.ts("dve", jb["qneg"], jb["QT"], -1.0, ALU.mult)
                z = ZP[G % 3]
                c0 = c0_of(jb, i)
                for hh, (kb, mk) in enumerate(kbs(jb, i)):
                    P.mm(z[:, H[hh]][:, c0:512], jb["KT"][:, kb * 128:(kb + 1) * 128], jb["QT"][:, c0:512])

            def stB(G):
                jb, i = look[G]
                z, e, sp = ZP[G % 3], E2p[G % 2], SP2[G % 3]
                c0 = c0_of(jb, i)
                P.act(both(e, c0), both(z, c0), AF.Exp)
                P.act(both(sp, c0), both(e, c0), AF.Ln, bias=1.0)
                for hh, (kb, mk) in enumerate(kbs(jb, i)):
                    if mk is not None:
                        P.tt("dve", sp[:, H[hh]][:, c0:512], sp[:, H[hh]][:, c0:512], mk[:, c0:512], ALU.mult)
                if i + 1 <= jb["np"] - 1:
                    P.tt("dve", S2[G % 4][:, c0:512], sp[:, H[0]][:, c0:512], sp[:, H[1]][:, c0:512], ALU.add)

            def stRc(G):
                jb, i = look[G]
                if i + 2 > jb["np"] - 1:
                    return
                c0 = c0_of(jb, i)
                P.cp("dve", Rb2[G % 3][:, c0:512], R2[G % 2][:, c0:512])

            def stC(G):
                jb, i = look[G]
                z, sp = ZP[G % 3], SP2[G % 3]
                c0 = c0_of(jb, i)
                for hh, (kb, mk) in enumerate(kbs(jb, i)):
                    ops = [(jb["KT"][:, kb * 128:(kb + 1) * 128], jb["qneg"], c0), (uincl, sp[:, H[hh]], c0)]
                    if hh == 1:
                        ops.append((onesb, sp[:, H[0]], c0))
                    if i >= 1:
                        ops.append((onesb, S2[(G - 1) % 4], c0_of(jb, i - 1)))
                    if i >= 2:
                        ops.append((onesb, Rb2[(G - 2) % 3], c0_of(jb, i - 2)))
                    for oi_, (l, r_, cc) in enumerate(ops):
                        P.mm(z[:, H[hh]][:, cc:512], l, r_[:, cc:512], start=(oi_ == 0), stop=(oi_ == len(ops) - 1),
                             sig=(oi_ == len(ops) - 1))

            def stR(G):
                jb, i = look[G]
                if i + 2 > jb["np"] - 1:
                    return
                c0, cp_ = c0_of(jb, i), c0_of(jb, i - 1)
                if cp_ > c0:
                    P.cp("pool", R2[G % 2][:, c0:cp_], S2[G % 4][:, c0:cp_])
                if cp_ < 512:
                    P.tt("pool", R2[G % 2][:, cp_:512], R2[(G - 1) % 2][:, cp_:512], S2[G % 4][:, cp_:512], ALU.add)

            def stD(G):
                jb, i = look[G]
                w = W2[G % 3]
                c0 = c0_of(jb, i)
                P.act(both(w, c0), both(ZP[G % 3], c0), AF.Exp, scale=-1.0)
                for hh, (kb, mk) in enumerate(kbs(jb, i)):
                    if mk is not None:
                        P.tt("dve", w[:, H[hh]][:, c0:512], w[:, H[hh]][:, c0:512], mk[:, c0:512], ALU.mult)

            def stF(G):
                jb, i = look[G]
                w = W2[G % 3]
                c0 = c0_of(jb, i)
                cp_ = c0_of(jb, i - 1)
                for hh, (kb, mk) in enumerate(kbs(jb, i)):
                    rngs = [(c0, 512)] if (i == 0 or hh == 1 or cp_ <= c0) else [(c0, cp_), (cp_, 512)]
                    for ri, (ca, cb) in enumerate(rngs):
                        P.mm(jb["outb"][0:DH, ca:cb], jb["V"][:, kb, :], w[:, H[hh]][:, ca:cb], start=(i == 0 and hh == 0),
                             stop=(i == jb["np"] - 1 and hh == 1 and ri == len(rngs) - 1))
                if i == jb["np"] - 1:
                    P.cp("act", jb["ho"], jb["outb"][0:DH, :])
                    P.dma("pool", jb["dst"], jb["ho"])

            for t in range(-2, NP + 1):
                for fn, off in ((stA, 2), (stB, 1), (stRc, -1), (stC, 1), (stR, 0), (stD, 0), (stF, -1)):
                    G = t + off
                    if 0 <= G < NP:
                        fn(G)

        hi = [0]
        for h in range(HB):
            kt_, v_, q_ = KTh[hi[0] % 2], Vh[hi[0] % 2], QTh[hi[0] % 2]
            hi[0] += 1
            P.dma("sp", kt_[:, 0:SKS], KTs_scr[h])
            P.dma("sp", v_[:, 0:NPB + 1, :], Vs_scr[h])
            P.dma("sp", q_[:, 0:128], QTs_scr[h])
            kbl = [(NPB, smk)] + [(kb, None) for kb in range(NPB - 1, -1, -1)]
            sb_attend(kt_, v_, q_, DEC, kbl, HBs_scr[h, :, 0:128])
        P.barrier()
        if stop_after != "Bs":
            jobs = []
            for h in range(HB):
                kt_, v_, q_ = KTh[hi[0] % 2], Vh[hi[0] % 2], QTh[hi[0] % 2]
                hi[0] += 1

                def pre(h=h, kt_=kt_, v_=v_, q_=q_):
                    P.dma("sp", kt_[:, 0:SEQ], KT_scr[h])
                    P.dma("sp", v_[:, 0:NKB, :], V_scr[h])
                    P.dma("sp", q_[:, 0:NOWN], QT_scr[h])
                for M in range(NSG):
                    kbl = [(kb, (sbm[:, kb - 16 * M, :] if kb >= 16 * M else None)) for kb in range(16 * M + 15, -1, -1)]
                    jobs.append(dict(pre=(pre if M == 0 else None), KT=kt_, V=v_, QT=q_[:, M * 512:(M + 1) * 512],
                                     kblocks=kbl, dst=HB_scr[h, :, M * 512:(M + 1) * 512], rel0=15))
            sb_pipeline(jobs)
        P.barrier()
    if stop_after in ("B", "Bs"):
        P.barrier()
        es.close()
        return nc, P

    with ExitStack() as ec:
        x1o = [P.sb(ec, "x1o%d" % c, [128, D], F32) for c in range(4)]
        x2_tm = [P.sb(ec, "x2_tm%d" % c, [128, D], F32) for c in range(4)]
        y_tm = [P.sb(ec, "y_tm%d" % c, [128, D], F32) for c in range(4)]
        xT_all = P.sb(ec, "xT2", [128, KC, TB], BF16)
        hT_all = P.sb(ec, "hT2", [128, FC, TB], BF16)
        hT = [hT_all.sub((slice(None), f, slice(None)), "hT2_%d" % f) for f in range(FC)]
        sg = [P.sb(ec, "sg2_%d" % i, [128, TB], F32) for i in range(2)]
        ln2g = bcast_load(ec, "ln2g", Wd["ln2_g"], D)
        ln2b = bcast_load(ec, "ln2b", Wd["ln2_b"], D)
        ln3g = bcast_load(ec, "ln3g", Wd["ln3_g"], D)
        ln3b = bcast_load(ec, "ln3b", Wd["ln3_b"], D)
        small = mk_small(ec)
        mhalf = small[4]
        wo = [P.sb(ec, "wo%d" % i, [128, KC, 512], BF16) for i in range(2)]
        for i in range(2):
            P.dma("sp", wo[i], ws["wo"][i])
        gn = P.sb(ec, "gn", [128, HA], F32)
        P.dma("sp", gn, Wd["hgrn_norm_g"].re("o (h p) -> p (o h)", p=128), allow_slow_non_contiguous=True)
        onesf = P.sb(ec, "onesf", [128, 128], F32)
        P.memset("pool", onesf, 1.0 / 128.0)
        oT4 = P.sb(ec, "oT4", [128, HA, TB], F32)
        ga4 = P.sb(ec, "ga4", [128, HA, TB], BF16)
        gt16 = P.sb(ec, "gt16", [128, 16, TB], BF16)
        hb8 = P.sb(ec, "hb8", [DH, HB, TB], BF16)
        sq = P.sb(ec, "sq", [128, TB], F32)
        t1 = P.sb(ec, "t1", [128, TB], F32)
        rs = P.sb(ec, "rs", [128, TB], F32)
        haT = P.sb(ec, "haT", [128, HA, TB], BF16)
        m1 = P.sb(ec, "m1", [128, TB], F32)
        m2 = P.sb(ec, "m2", [128, TB], F32)
        mergedT = P.sb(ec, "mergedT", [128, KC, TB], BF16)

        def own_tail(t0, ntc, sample):
            N = ntc * 128
            N_ = slice(0, N)
            OTs, GAs, GTs, HBs, X1s = (OTs_scr, GAs_scr, GTs_scr, HBs_scr, X1s_scr) if sample else \
                (OT_scr, GA_scr, GT_scr, HB_scr, X1_scr)
            P.dma("sp", oT4[:, :, N_], OTs[:, :, t0:t0 + N].re("h v t -> v h t"))
            P.dma("sp", ga4[:, :, N_], GAs[:, :, t0:t0 + N].re("h p t -> p h t"))
            P.dma("sp", gt16[:, :, N_], GTs[:, :, t0:t0 + N].re("h p t -> p h t"))
            P.dma("sp", hb8[:, :, N_], HBs[:, :, t0:t0 + N].re("h d t -> d h t"))
            for c in range(ntc):
                P.dma("sp", x1o[c], X1s[t0 + c * 128:t0 + (c + 1) * 128, :])
            for h in range(HA):
                P.act(sq[:, N_], oT4[:, h, N_], AF.Square)
                b = bank()
                P.mm(b[:, N_], onesf, sq[:, N_])
                P.ts("dve", t1[:, N_], b[:, N_], RMS_EPS, ALU.add)
                P.act(rs[:, N_], t1[:, N_], AF.Ln)
                P.act(rs[:, N_], rs[:, N_], AF.Exp, scale=-0.5)
                P.tt("dve", t1[:, N_], oT4[:, h, N_], rs[:, N_], ALU.mult)
                P.stt(haT[:, h, N_], t1[:, N_], gn[:, h:h + 1], ga4[:, h, N_], ALU.mult, ALU.mult)
            for oc in range(KC):
                wa = ring_load(ws["wa"][oc], 4)
                pa = bank()
                for kc in range(4):
                    P.mm(pa[:, N_], wa[:, kc, :], haT[:, kc, N_], start=(kc == 0), stop=(kc == 3), sig=(kc == 3))
                wb = ring_load(ws["wb"][oc], 8, npart=DH)
                pb_ = bank()
                for h in range(HB):
                    P.mm(pb_[:, N_], wb[:, h, :], hb8[:, h, N_], start=(h == 0), stop=(h == HB - 1), sig=(h == HB - 1))
                P.stt(m1[:, N_], gt16[:, oc, N_], 1.0, pa[:, N_], ALU.add, ALU.mult)
                P.stt(m2[:, N_], gt16[:, 8 + oc, N_], 1.0, pb_[:, N_], ALU.add, ALU.mult)
                P.tt("dve", mergedT[:, oc, N_], m1[:, N_], m2[:, N_], ALU.add)
            for c in range(ntc):
                o = x2_tm[c]
                for hf in range(2):
                    acc = PS[4 + (c % 2) * 2 + hf]
                    for k in range(KC):
                        P.mm(acc, mergedT[:, k, c * 128:(c + 1) * 128], wo[hf][:, k, :], start=(k == 0), stop=(k == KC - 1),
                             sig=(k == KC - 1))
                    P.stt(o[:, hf * 512:(hf + 1) * 512], acc, 0.5 / ALPHA, x1o[c][:, hf * 512:(hf + 1) * 512], ALU.mult, ALU.add)
                layer_norm(o, ln2g, ln2b, small, LN_EPS / (ALPHA * ALPHA))
            transpose_to_fm(x2_tm, xT_all, ntc)
            ffn_ln("ffn2", x2_tm, xT_all, hT, sg, y_tm, ln3g, ln3b, small, ntc)
            for c in range(ntc):
                if sample:
                    P.dma("pool", ys, y_tm[0][0:DEC, :])
                else:
                    P.dma("pool", y_own[t0 + c * 128:t0 + (c + 1) * 128, :], y_tm[c])

        own_tail(0, 1, True)
        if stop_after != "Cs":
            for M in range(NSG):
                own_tail(M * 512, 4, False)
        P.barrier()
    P.barrier()
    es.close()
    return nc, P


def make_consts(g):
    c = {}
    c["c_ident"] = np.eye(128, dtype=np.float32)
    t = np.arange(512)
    c["c_mask64"] = np.broadcast_to((t % 64 != 0).astype(np.float32), (128, 512)).copy()
    s = np.arange(128)[:, None]
    tt = np.arange(128)[None, :]
    m2 = ((s // 64 == tt // 64) & (s <= tt)).astype(np.float32)
    c["c_mask2"] = np.tile(m2, (1, 4))
    c["c_uincl"] = (np.arange(128)[:, None] >= np.arange(128)[None, :]).astype(np.float32)
    sb = np.zeros((16, 128, 512), np.float32)
    for mrel in range(16):
        for j in range(4):
            kpos = 128 * mrel + np.arange(128)[:, None]
            qpos = 128 * (4 * j + g) + np.arange(128)[None, :]
            sb[mrel, :, j * 128:(j + 1) * 128] = (kpos < qpos)
    c["c_sbmask"] = sb
    c["c_smask"] = ((np.arange(128)[:, None] < np.arange(DEC)[None, :])).astype(np.float32)
    fl = np.zeros((128, 4), np.float32)
    fl[:, g] = 1.0
    c["c_flag"] = fl
    return c


def make_in_maps(inp, SEQ, PAST):
    maps = []
    f = lambda a: np.ascontiguousarray(np.asarray(a, dtype=np.float32))
    wts = {k: f(inp[k]).reshape(W_SHAPES[k]) for k in W_SHAPES}
    for c in range(8):
        b, g = c // 4, c % 4
        m = dict(wts)
        m["xb"] = f(inp["x_prompt"][b])
        m["xs"] = f(inp["x_sample"][c])
        m["ck"] = f(inp["cache_sb_k"][0, c]).reshape(PAST, 512)
        m["cv"] = f(inp["cache_sb_v"][0, c]).reshape(PAST, 512)
        m["st0"] = f(inp["state_hgrn"][0, c])
        m.update(make_consts(g))
        maps.append(m)
    return maps


_CACHE = {}


def kernel(**inputs):
    SEQ = inputs["x_prompt"].shape[1]
    PAST = inputs["cache_sb_k"].shape[2]
    key = (SEQ, PAST)
    if key not in _CACHE:
        _CACHE[key] = build(SEQ, PAST)[0]
    nc = _CACHE[key]
    maps = make_in_maps(inputs, SEQ, PAST)
    res = run_bass_kernel_spmd(nc, maps, core_ids=list(range(8)))
    return assemble(res.results, SEQ)


def assemble(r, SEQ):
    NB128 = SEQ // 128
    y_prompt = np.zeros((2, SEQ, D), np.float32)
    for c in range(8):
        b, g = c // 4, c % 4
        yo = np.asarray(r[c]["y_own"]).reshape(NB128 // 4, 128, D)
        y_prompt[b].reshape(NB128 // 4, 4, 128, D)[:, g] = yo
    y_sample = np.stack([np.asarray(r[c]["ys"]) for c in range(8)])
    k_p = np.stack([np.asarray(r[4 * b]["k_all"]).reshape(SEQ, HB, DH) for b in range(2)])[None]
    v_p = np.stack([np.asarray(r[4 * b]["v_all"]).reshape(SEQ, HB, DH) for b in range(2)])[None]
    s_p = np.stack([np.asarray(r[4 * b]["s_fin"]) for b in range(2)])[None]
    k_s = np.stack([np.asarray(r[c]["ks"]).reshape(DEC, HB, DH) for c in range(8)])[None]
    v_s = np.stack([np.asarray(r[c]["vs"]).reshape(DEC, HB, DH) for c in range(8)])[None]
    s_s = np.stack([np.asarray(r[c]["ss"]) for c in range(8)])[None]
    return (y_prompt, y_sample, k_p.astype(np.float32), v_p.astype(np.float32), s_p.astype(np.float32),
            k_s.astype(np.float32), v_s.astype(np.float32), s_s.astype(np.float32))
```
